# Optimizing a Trainium2 kernel written in Bass

```python
import math
import jax, jax.numpy as jnp
from jax import lax
import numpy as np

D_MODEL = 2048
BATCH = 4
SEQ = 2048
DEPTH = 1

HEAD_DIM = 128
N_Q_HEADS = 8
N_KV_HEADS = 2
Q_PER_KV = N_Q_HEADS // N_KV_HEADS
ATTN_WIDTH = N_Q_HEADS * HEAD_DIM
KV_WIDTH = N_KV_HEADS * HEAD_DIM
FOURIER_GROUPS = 8
FOURIER_GROUP_DIM = 128
FOURIER_WIDTH = FOURIER_GROUPS * FOURIER_GROUP_DIM
MIX_WIDTH = ATTN_WIDTH + FOURIER_WIDTH
IN_WIDTH = ATTN_WIDTH + 2 * KV_WIDTH + FOURIER_WIDTH
Q_BLOCK = 128
GRID_W = 64
ROPE_THETA = 10000.0
ROPE_PAIRS = HEAD_DIM // 4
D_FF = 5632
CONV_WIDTH = 3
EPS = 1e-6

kernel_name = "hybrid_attn_fourier_convffn_encoder"


def rmsnorm(x, g):
    xf = x.astype(jnp.float32)
    xf = xf * lax.rsqrt(jnp.mean(xf * xf, axis=-1, keepdims=True) + EPS)
    return xf.astype(x.dtype) * g


def rope_tables(pos, dtype):
    inv_freq = ROPE_THETA ** (-jnp.arange(ROPE_PAIRS, dtype=jnp.float32) / ROPE_PAIRS)
    ang = pos.astype(jnp.float32)[:, None] * inv_freq[None, :]
    ang = jnp.concatenate([ang, ang], axis=-1)[:, None, :]
    return jnp.cos(ang).astype(dtype), jnp.sin(ang).astype(dtype)


def rotate(v, cos, sin):
    h = v.shape[-1] // 2
    rot = jnp.concatenate([-v[..., h:], v[..., :h]], axis=-1)
    return v * cos + rot * sin


def axial_rope(x, cos_r, sin_r, cos_c, sin_c):
    half = x.shape[-1] // 2
    return jnp.concatenate([rotate(x[..., :half], cos_r, sin_r),
                            rotate(x[..., half:], cos_c, sin_c)], axis=-1)


def attention_group(q, k, v):
    B, S = q.shape[0], q.shape[1]
    n_blk = S // Q_BLOCK
    scale = 1.0 / math.sqrt(HEAD_DIM)
    qb = q.reshape(B, n_blk, Q_BLOCK, N_KV_HEADS, Q_PER_KV, HEAD_DIM)
    qb = jnp.transpose(qb, (1, 0, 2, 3, 4, 5))

    def attend_block(qblk):
        s = jnp.einsum('bqkgd,bskd->bkgqs', qblk, k).astype(jnp.float32) * scale
        p = jax.nn.softmax(s, axis=-1).astype(v.dtype)
        return jnp.einsum('bkgqs,bskd->bqkgd', p, v)

    o = lax.map(attend_block, qb)
    o = jnp.transpose(o, (1, 0, 2, 3, 4, 5))
    return o.reshape(B, S, ATTN_WIDTH)


def fourier_group(f, w_fmix):
    B, S = f.shape[0], f.shape[1]
    fg = f.reshape(B, S, FOURIER_GROUPS, FOURIER_GROUP_DIM).astype(jnp.float32)
    spec = jnp.fft.fft2(fg, axes=(1, 3), norm='ortho').real.astype(f.dtype)
    out = jnp.einsum('bsgc,gcd->bsgd', spec, w_fmix)
    return out.reshape(B, S, FOURIER_WIDTH)


def depthwise_conv_centred(h, w, b):
    hp = jnp.pad(h, ((0, 0), (1, 1), (0, 0)))
    S = h.shape[1]
    return hp[:, 0:S] * w[0] + hp[:, 1:S + 1] * w[1] + hp[:, 2:S + 2] * w[2] + b


def setup_inputs(seed: int = 0) -> dict:
    key = jax.random.key(seed)
    ks = jax.random.split(key, 16)
    f32 = jnp.float32

    def nrm(k, shape, fan_in):
        return jax.random.normal(k, shape, f32) * (fan_in ** -0.5)

    def gain(k, shape):
        return 1.0 + 0.02 * jax.random.normal(k, shape, f32)

    return {
        "x": jax.random.normal(ks[0], (BATCH, SEQ, D_MODEL), f32),
        "norm1_g": gain(ks[1], (DEPTH, D_MODEL)),
        "w_in": nrm(ks[2], (DEPTH, D_MODEL, IN_WIDTH), D_MODEL),
        "q_norm_g": gain(ks[3], (DEPTH, HEAD_DIM)),
        "k_norm_g": gain(ks[4], (DEPTH, HEAD_DIM)),
        "w_fmix": nrm(ks[5], (DEPTH, FOURIER_GROUPS, FOURIER_GROUP_DIM, FOURIER_GROUP_DIM), FOURIER_GROUP_DIM),
        "attn_out_g": gain(ks[6], (DEPTH, ATTN_WIDTH)),
        "fourier_out_g": gain(ks[7], (DEPTH, FOURIER_WIDTH)),
        "w_out": nrm(ks[8], (DEPTH, MIX_WIDTH, D_MODEL), MIX_WIDTH),
        "norm2_g": gain(ks[9], (DEPTH, D_MODEL)),
        "w_up": nrm(ks[10], (DEPTH, D_MODEL, 2 * D_FF), D_MODEL),
        "conv_w": nrm(ks[11], (DEPTH, CONV_WIDTH, 2 * D_FF), CONV_WIDTH),
        "conv_b": 0.01 * jax.random.normal(ks[12], (DEPTH, 2 * D_FF), f32),
        "w_down": nrm(ks[13], (DEPTH, D_FF, D_MODEL), D_FF),
        "final_g": gain(ks[14], (D_MODEL,)),
    }


def reference(x, norm1_g, w_in, q_norm_g, k_norm_g, w_fmix, attn_out_g, fourier_out_g,
              w_out, norm2_g, w_up, conv_w, conv_b, w_down, final_g):
    B, S = x.shape[0], x.shape[1]
    ROWS = S // GRID_W
    row = jnp.repeat(jnp.arange(ROWS, dtype=jnp.int32), GRID_W)
    col = jnp.tile(jnp.arange(GRID_W, dtype=jnp.int32), ROWS)
    cos_r, sin_r = rope_tables(row, x.dtype)
    cos_c, sin_c = rope_tables(col, x.dtype)

    h = x
    for l in range(DEPTH):
        u = rmsnorm(h, norm1_g[l])
        proj = jnp.einsum('bsd,de->bse', u, w_in[l])
        q = proj[..., :ATTN_WIDTH].reshape(B, S, N_Q_HEADS, HEAD_DIM)
        k = proj[..., ATTN_WIDTH:ATTN_WIDTH + KV_WIDTH].reshape(B, S, N_KV_HEADS, HEAD_DIM)
        v = proj[..., ATTN_WIDTH + KV_WIDTH:ATTN_WIDTH + 2 * KV_WIDTH].reshape(B, S, N_KV_HEADS, HEAD_DIM)
        f = proj[..., ATTN_WIDTH + 2 * KV_WIDTH:]

        q = axial_rope(rmsnorm(q, q_norm_g[l]), cos_r, sin_r, cos_c, sin_c)
        k = axial_rope(rmsnorm(k, k_norm_g[l]), cos_r, sin_r, cos_c, sin_c)
        a_out = rmsnorm(attention_group(q, k, v), attn_out_g[l])
        f_out = rmsnorm(fourier_group(f, w_fmix[l]), fourier_out_g[l])
        mix = jnp.concatenate([a_out, f_out], axis=-1)
        h = h + jnp.einsum('bse,ed->bsd', mix, w_out[l])

        u2 = rmsnorm(h, norm2_g[l])
        up = jnp.einsum('bsd,df->bsf', u2, w_up[l])
        up = depthwise_conv_centred(up, conv_w[l], conv_b[l])
        gate, val = up[..., :D_FF], up[..., D_FF:]
        h = h + jnp.einsum('bsf,fd->bsd', jax.nn.silu(gate) * val, w_down[l])

    return rmsnorm(h, final_g)
```

```python
from contextlib import ExitStack

import numpy as np
import ml_dtypes

import concourse.bass as bass
import concourse.mybir as mybir
from concourse.bass_utils import run_bass_kernel_spmd

F32 = mybir.dt.float32
BF16 = mybir.dt.bfloat16
AF = mybir.ActivationFunctionType
ALU = mybir.AluOpType
AX = mybir.AxisListType

D = 2048
S = 2048
T = 1024
NQ = T + 1
HD = 128
NH = 8
NKV = 2
DFF = 5632
NCH = DFF // 128
EPS = 1e-6
FFB = [(0, 12), (12, 12), (24, 12), (36, 8)]
SEM_GEN = 3000
TAP_UTE = False

ENGS = ["pe", "act", "dve", "pool", "sp"]


class Op:
    __slots__ = ("eng", "fn", "stream", "sval", "signal", "sigval", "gidx", "deps")


class Sched:
    def __init__(self):
        self.ops = []
        self.per = {e: [] for e in ENGS}
        self.lastw = {}
        self.readers = {}
        self.touch = {}
        self.inherit = {}
        self.stream_cnt = {}

    @staticmethod
    def _k(o):
        return ("d", o.stream) if o.stream else ("e", o.eng)

    def op(self, eng, fn, reads=(), writes=(), stream=None):
        o = Op()
        o.eng = eng
        o.fn = fn
        o.stream = stream
        o.signal = False
        o.sigval = 0
        o.sval = 0
        o.gidx = len(self.ops)
        if stream:
            self.stream_cnt[stream] = self.stream_cnt.get(stream, 0) + 1
            o.sval = 16 * self.stream_cnt[stream]
        deps = {}

        def add(d):
            k = self._k(d)
            if k not in deps or deps[k].gidx < d.gidx:
                deps[k] = d

        reads = list(reads)
        writes = list(writes)
        for key in reads + writes:
            if key not in self.lastw and key not in self.readers:
                for d in self.inherit.get(key[0], {}).values():
                    add(d)
        for key in reads:
            w = self.lastw.get(key)
            if w is not None:
                add(w)
        for key in writes:
            w = self.lastw.get(key)
            if w is not None:
                add(w)
            for r in self.readers.get(key, {}).values():
                add(r)
        for key in reads:
            self.readers.setdefault(key, {})[self._k(o)] = o
        for key in writes:
            self.lastw[key] = o
            self.readers[key] = {}
        for key in reads + writes:
            self.touch.setdefault(key[0], {})[self._k(o)] = o
        o.deps = []
        for d in deps.values():
            if (not d.stream) and d.eng == "pe" and eng == "pe" and not stream:
                continue
            o.deps.append(d)
            if not d.stream:
                d.signal = True
        self.ops.append(o)
        self.per[eng].append(o)
        return o

    def alias(self, new_name, old_names):
        m = self.inherit.setdefault(new_name, {})
        for n in old_names:
            for k, d in self.touch.get(n, {}).items():
                if k not in m or m[k].gidx < d.gidx:
                    m[k] = d
            for k, d in self.inherit.get(n, {}).items():
                if k not in m or m[k].gidx < d.gidx:
                    m[k] = d


class Arena:
    def __init__(self, tensor, sched, nbytes):
        self.t = tensor
        self.s = sched
        self.nbytes = nbytes
        self.live = []

    def alloc(self, name, off, nbytes, dtype, shape_str=None, **dims):
        assert off % 4 == 0 and off + nbytes <= self.nbytes, (name, off, nbytes)
        b0, b1 = off, off + nbytes
        old = sorted(set(n for (n, a0, a1) in self.live if a0 < b1 and b0 < a1))
        self.live.append((name, b0, b1))
        if old:
            self.s.alias(name, old)
        w0 = off // 4
        w1 = (off + nbytes + 3) // 4
        ap = self.t[:, w0:w1]
        if dtype != F32:
            ap = ap.bitcast(dtype)
        if shape_str:
            ap = ap.rearrange(shape_str, **dims)
        return ap


def build_program(debug=False):
    nc = bass.Bass("TRN2", target_bir_lowering=False)
    S_ = Sched()

    def dram(name, shape, dt=F32, kind="ExternalInput"):
        return nc.dram_tensor(name, list(shape), dt, kind=kind).ap()

    x_d = dram("x_loc", [S, D])
    rope_d = dram("rope", [S, 256])
    dft_d = dram("dft", [5, 128, 2 * 16 * 256], BF16)
    ccs_d = dram("ccs", [128, 256], BF16)
    ident_d = dram("ident", [128, 128], BF16)
    g1_d = dram("g1", [D])
    g2_d = dram("g2", [D])
    gF_d = dram("gF", [D])
    ga_d = dram("ga", [1024])
    gf_d = dram("gf", [1024])
    gq_d = dram("gq", [128])
    gk_d = dram("gk", [128])
    cw_d = dram("cw", [128, 88 * 3])
    cb_d = dram("cb", [128, 88])
    wf_d = dram("wf", [128, 1024])
    win_d = dram("win", [128, 5 * 8192])
    wout_d = dram("wout", [128, 4 * 8192])
    wup_d = dram("wup", [128, 22 * 8192])
    wdn_d = dram("wdn", [128, 44 * 2048])
    y_d = dram("y", [T, D], F32, kind="ExternalOutput")

    ARENA_BYTES = 209920
    es = ExitStack()
    with es:
        arena_t = es.enter_context(nc.sbuf_tensor("arena", [128, ARENA_BYTES // 4], F32))
        psum_t = es.enter_context(nc.psum_tensor("psum", [128, 4096], F32))
        A = Arena(arena_t, S_, ARENA_BYTES)

        R_SLOT = 0
        R_F = 49152
        R_Q = 81920
        R_MIX = 114944
        R_A = 147840
        R_C = 207232

        slots = [A.alloc("slot", R_SLOT + i * 16384, 16384, BF16) for i in range(3)]
        ident = A.alloc("ident", R_C, 256, BF16)
        cw = A.alloc("cw", R_C + 256, 1056, F32)
        cb = A.alloc("cb", R_C + 1312, 352, F32)
        stat = A.alloc("stat", R_C + 1664, 512, F32)
        epsc = A.alloc("epsc", R_C + 2176, 4, F32)

        def ps(bank, n=512, off=0):
            return psum_t[:, bank * 512 + off: bank * 512 + off + n]

        def ps_bf(bank, nbanks=1):
            return psum_t[:, bank * 512:(bank + nbanks) * 512].bitcast(BF16)

        def mm(out, lhsT, rhs, start, stop, reads, writes):
            return S_.op("pe", lambda e: e.matmul(out, lhsT, rhs, start=start, stop=stop), reads, writes)

        def tr(out, in_, idn, reads, writes):
            return S_.op("pe", lambda e: e.transpose(out, in_, idn), reads, writes)

        def act(out, in_, func, reads, writes, bias=None, scale=None, accum=None):
            kw = {}
            if bias is not None:
                kw["bias"] = bias
            if scale is not None:
                kw["scale"] = scale
            if accum is not None:
                kw["accum_out"] = accum
            return S_.op("act", lambda e: e.activation(out, in_, func, **kw), reads, writes)

        def dve(fn, reads, writes):
            return S_.op("dve", fn, reads, writes)

        def tt(out, in0, in1, op, reads, writes):
            return dve(lambda e: e.tensor_tensor(out, in0, in1, op), reads, writes)

        def stt(out, in0, scalar, in1, op0, op1, reads, writes):
            return dve(lambda e: e.scalar_tensor_tensor(out, in0, scalar, in1, op0, op1), reads, writes)

        def tsc(out, in0, s1, s2, op0, op1, reads, writes):
            if op1 is None:
                return dve(lambda e: e.tensor_scalar(out, in0, s1, s2, op0), reads, writes)
            return dve(lambda e: e.tensor_scalar(out, in0, s1, s2, op0, op1), reads, writes)

        def red(out, in_, reads, writes):
            return dve(lambda e: e.tensor_reduce(out, in_, AX.X, ALU.add), reads, writes)

        def recip(out, in_, reads, writes):
            return dve(lambda e: e.reciprocal(out, in_), reads, writes)

        def copy_any(which, out, in_, reads, writes):
            if which % 2 == 0:
                return act(out, in_, AF.Copy, reads, writes)
            return dve(lambda e: e.tensor_copy(out, in_), reads, writes)

        def dma(eng, out, in_, reads, writes, stream, **kw):
            return S_.op(eng, lambda e: e.dma_start(out=out, in_=in_, **kw), reads, writes, stream=stream)

        stat_ctr = [0]

        def stat_slot(n=1):
            i = stat_ctr[0]
            if i % 128 + n > 128:
                i += 128 - i % 128
            stat_ctr[0] = i + n
            c = i % 128
            return stat[:, c:c + n], ("stat", c // 16, (c + n - 1) // 16)

        def stat_keys(k):
            return [("stat", j) for j in range(k[1], k[2] + 1)]

        def rstd_from_ss(ss, sskeys, rows, ncols, inv_n):
            r, rk = stat_slot(ncols)
            rkeys = stat_keys(rk)
            tsc(r[:rows], ss[:rows], inv_n, EPS, ALU.mult, ALU.add, sskeys, rkeys)
            act(r[:rows], r[:rows], AF.Sqrt, rkeys, rkeys)
            recip(r[:rows], r[:rows], rkeys, rkeys)
            return r, rkeys

        slot_ctr = [0]

        def load_slot(src_ap, ncols):
            i = slot_ctr[0] % 3
            slot_ctr[0] += 1
            dma("pool", slots[i][:, 0:ncols], src_ap, [], [("slot", i)], f"slot{i}", max_dma_last_dim=8192)
            return i

        def tap(name, ap, names):
            if not debug:
                return
            shape = [int(s) for s in ap.shape]
            dd = nc.dram_tensor("dbg_" + name, shape, ap.dtype, kind="ExternalOutput").ap()
            reads = [k for k in S_.lastw if k[0] in names]
            dma("sp", dd, ap, reads, [("dbg", name)], "dbg_" + name)

        dma("sp", ident, ident_d, [], [("ident", 0)], "c_ident")
        dma("sp", cw, cw_d, [], [("cw", 0)], "c_cw")
        dma("sp", cb, cb_d, [], [("cb", 0)], "c_cb")
        KI = [("ident", 0)]

        f_tm = A.alloc("f_tm", R_F, 32768, BF16, "p (s c) -> p s c", c=1024)
        qT = A.alloc("qT", R_Q, 16416, BF16, "p (h t) -> p h t", t=1026)
        kT = A.alloc("kT", R_Q + 16416, 8192, BF16, "p (h t) -> p h t", t=2048)
        Vaug = A.alloc("V", R_Q + 24608, 8320, BF16, "p (s h c) -> p s h c", h=2, c=130)
        uT = A.alloc("uT", R_A, 32768, BF16, "p (k t) -> p k t", t=1024)
        o = R_A + 32768
        sq = A.alloc("sq", o, 2048, F32); o += 2048
        qn = A.alloc("qn", o, 2048, F32); o += 2048
        t1 = A.alloc("t1", o, 2048, F32); o += 2048
        t2 = A.alloc("t2", o, 2048, F32); o += 2048
        qr = A.alloc("qr", o, 1024, BF16); o += 1024
        ropet = [A.alloc("ropet", o + i * 1024, 1024, F32) for i in range(2)]; o += 2048
        gq_bc = A.alloc("gq_bc", o, 512, F32); o += 512
        gk_bc = A.alloc("gk_bc", o, 512, F32); o += 512
        xs = [A.alloc("xs", R_MIX + i * 8192, 8192, F32) for i in range(2)]
        g1_bc = A.alloc("g1_bc", R_MIX + 16384, 8192, F32)
        u_tm = A.alloc("u_tm", R_MIX + 24576, 4096, BF16)

        dma("sp", g1_bc, g1_d.partition_broadcast(128), [], [("g1_bc", 0)], "c_g1")
        dma("sp", gq_bc, gq_d.partition_broadcast(128), [], [("gq_bc", 0)], "c_gq")
        dma("sp", gk_bc, gk_d.partition_broadcast(128), [], [("gk_bc", 0)], "c_gk")
        dve(lambda e: e.memset(Vaug[:, :, :, 128:129], 1.0), [], [("V", 99)])
        dve(lambda e: e.memset(epsc, EPS), [], [("epsc", 0)])

        cpy = [0]
        rope_ctr = [0]
        psacc_ctr = [0]
        pstr_ctr = [0]

        def norm_tile(src, srckeys, rows, gbc, gkey, junk, junkkeys, dst, dstkeys):
            ss, sk = stat_slot(1)
            sskeys = stat_keys(sk)
            act(junk[:rows], src[:rows], AF.Square, srckeys, junkkeys + sskeys, accum=ss[:rows])
            r, rkeys = rstd_from_ss(ss, sskeys, rows, 1, 1.0 / D)
            stt(dst[:rows], src[:rows], r[:rows, 0:1], gbc[:rows], ALU.mult, ALU.mult,
                srckeys + rkeys + [gkey], dstkeys + junkkeys)

        def transposes16(src, srckeys, rows, dstT, dstkeys_fn, col0, bankpair):
            pb = ps_bf(bankpair, 2)
            pk = [("ps", bankpair), ("ps", bankpair + 1)]
            for j in range(16):
                tr(pb[:, j * 128: j * 128 + rows], src[:rows, j * 128:(j + 1) * 128], ident[:rows, :rows],
                   srckeys + KI, pk)
            cpy[0] += 1
            copy_any(cpy[0], dstT[:, :, col0:col0 + rows],
                     pb.rearrange("p (k t) -> p k t", t=128)[:, :, 0:rows], pk, dstkeys_fn)

        def qk_post(psrc, pkeys, rows, nheads, gbc, gbkey, rtile, rkey, dstT, dst_h0, col0, dstkeys):
            n = nheads * 128
            act(sq[:rows, :n], psrc[:rows, :n], AF.Square, pkeys, [("sq", 0)])
            ss, sk = stat_slot(nheads)
            sskeys = stat_keys(sk)
            red(ss[:rows], sq[:rows, :n].rearrange("p (h d) -> p h d", d=128), [("sq", 0)], sskeys)
            r, rkeys = rstd_from_ss(ss, sskeys, rows, nheads, 1.0 / HD)
            for h in range(nheads):
                stt(qn[:rows, h * 128:(h + 1) * 128], psrc[:rows, h * 128:(h + 1) * 128],
                    r[:rows, h:h + 1], gbc[:rows], ALU.mult, ALU.mult,
                    pkeys + rkeys + [gbkey], [("qn", 0)])
            qn3 = qn[:rows, :n].rearrange("p (h d) -> p h d", d=128)
            cosb = rtile[:rows, 0:128].unsqueeze(1).to_broadcast([rows, nheads, 128])
            tt(t1[:rows, :n].rearrange("p (h d) -> p h d", d=128), qn3, cosb, ALU.mult,
               [("qn", 0), rkey], [("t1", 0)])
            qn5 = qn[:rows, :n].rearrange("p (h a b c) -> p h a b c", a=2, b=2, c=32)
            t25 = t2[:rows, :n].rearrange("p (h a b c) -> p h a b c", a=2, b=2, c=32)
            sin5 = rtile[:rows, 128:256].rearrange("p (a b c) -> p a b c", a=2, b=2, c=32)
            for blk in range(2):
                sb = sin5[:, :, blk, :].unsqueeze(1).to_broadcast([rows, nheads, 2, 32])
                tt(t25[:, :, :, blk, :], qn5[:, :, :, 1 - blk, :], sb, ALU.mult,
                   [("qn", 0), rkey], [("t2", blk)])
            tt(qr[:rows, :n], t1[:rows, :n], t2[:rows, :n], ALU.add,
               [("t1", 0), ("t2", 0), ("t2", 1)], [("qr", 0)])
            bank = pstr_ctr[0] % 2
            pstr_ctr[0] += 1
            pb = ps_bf(bank, 1)
            for h in range(nheads):
                tr(pb[:, h * 128:h * 128 + rows], qr[:rows, h * 128:(h + 1) * 128], ident[:rows, :rows],
                   [("qr", 0)] + KI, [("ps", bank)])
            cpy[0] += 1
            copy_any(cpy[0], dstT[:, dst_h0:dst_h0 + nheads, col0:col0 + rows],
                     pb[:, 0:n].rearrange("p (h t) -> p h t", t=128)[:, :, 0:rows], [("ps", bank)], dstkeys)

        def load_rope(gt):
            i = rope_ctr[0] % 2
            rope_ctr[0] += 1
            dma("sp", ropet[i], rope_d[gt * 128:(gt + 1) * 128, :], [], [("ropet", i)], f"ropet{i}")
            return ropet[i], ("ropet", i)

        for pa in range(2):
            if pa == 1 and TAP_UTE:
                tap("uTE", uT, ["uT"])
            for t in range(8):
                gt = pa * 8 + t
                b = gt % 2
                dma("sp", xs[b], x_d[gt * 128:(gt + 1) * 128, :], [], [("xs", b)], f"xs{b}")
                norm_tile(xs[b], [("xs", b)], 128, g1_bc, ("g1_bc", 0), u_tm, [("u_tm", 0)], u_tm, [("u_tm", 0)])
                transposes16(u_tm, [("u_tm", 0)], 128, uT, [("uT", t)], t * 128, 2 * (t % 2))
            blocks = [0, 1, 2, 3, 4] if pa == 0 else [2, 3, 4, 0, 1]
            for blk in blocks:
                si = load_slot(win_d[:, blk * 8192:(blk + 1) * 8192], 8192)
                w = slots[si].rearrange("p (k c) -> p k c", c=512)
                halo_only = (pa == 1 and blk < 2)
                tiles = [0] if halo_only else list(range(8))
                for t in tiles:
                    gt = pa * 8 + t
                    rows = 1 if halo_only else 128
                    bank = 4 + psacc_ctr[0] % 4
                    psacc_ctr[0] += 1
                    pk = [("ps", bank)]
                    for k in range(16):
                        mm(ps(bank)[:rows, :], uT[:, k, t * 128:t * 128 + rows], w[:, k, :], k == 0, k == 15,
                           [("uT", t), ("slot", si)], pk)
                    if blk < 2:
                        rt, rk = load_rope(gt)
                        col0 = 1024 if halo_only else t * 128
                        qk_post(ps(bank), pk, rows, 4, gq_bc, ("gq_bc", 0), rt, rk, qT, 4 * blk, col0,
                                [("qT", blk, gt)])
                    elif blk == 2:
                        rt, rk = load_rope(gt)
                        qk_post(ps(bank), pk, 128, 2, gk_bc, ("gk_bc", 0), rt, rk, kT, 0, gt * 128, [("kT", gt)])
                        cpy[0] += 1
                        copy_any(cpy[0], Vaug[:, gt, :, 0:128],
                                 ps(bank)[:, 256:512].rearrange("p (h d) -> p h d", d=128), pk, [("V", gt)])
                    else:
                        fb = blk - 3
                        cpy[0] += 1
                        copy_any(cpy[0], f_tm[:, gt, fb * 512:(fb + 1) * 512], ps(bank), pk, [("f_tm", gt, fb)])

        tap("qT", qT, ["qT"])
        tap("kT", kT, ["kT"])
        tap("V", Vaug, ["V"])
        tap("f", f_tm, ["f_tm"])
        mixT = A.alloc("mixT", R_MIX, 32832, BF16, "p (k t) -> p k t", t=1026)
        o = R_A
        PT = [A.alloc("PT", o + i * 8192, 8192, BF16, "p (s q) -> p s q", q=256) for i in range(2)]; o += 16384
        O_sb = [A.alloc("O_sb", o + i * 4160, 4160, F32, "p (h c) -> p h c", c=130) for i in range(2)]; o += 8320
        sqa = A.alloc("sqa", o, 4096, F32); o += 4096
        a_tm = A.alloc("a_tm", o, 2048, BF16); o += 2048
        ga_bc = A.alloc("ga_bc", o, 4096, F32); o += 4096
        dma("sp", ga_bc, ga_d.partition_broadcast(128), [], [("ga_bc", 0)], "c_ga")

        scale = 1.0 / float(np.sqrt(HD))
        qgroups = [(i * 256, 256) for i in range(4)] + [(1024, 1)]
        sbank = [0]
        obank = [0]
        ptc = [0]
        for (q0, n) in qgroups:
            ntile = 2 if n == 256 else 1
            rows = 128 if n == 256 else 1
            for h in range(NH):
                kv = h // 4
                pi = ptc[0] % 2
                ptc[0] += 1
                for sc in range(16):
                    bank = sbank[0] % 3
                    sbank[0] += 1
                    mm(ps(bank)[:, :n], kT[:, kv, sc * 128:(sc + 1) * 128], qT[:, h, q0:q0 + n], True, True,
                       [("kT", sc), ("qT", h // 4, q0 // 128), ("qT", h // 4, (q0 + n - 1) // 128)], [("ps", bank)])
                    act(PT[pi][:, sc, :n], ps(bank)[:, :n], AF.Exp, [("ps", bank)], [("PT", pi, sc)], scale=scale)
                for tl in range(ntile):
                    bank = 3 + obank[0] % 2
                    obank[0] += 1
                    for sc in range(16):
                        mm(ps(bank)[:rows, 0:129], PT[pi][:, sc, tl * 128:tl * 128 + rows],
                           Vaug[:, sc, kv, 0:129], sc == 0, sc == 15,
                           [("PT", pi, sc), ("V", sc), ("V", 99)], [("ps", bank)])
                    cpy[0] += 1
                    copy_any(cpy[0], O_sb[tl][:rows, h, 0:129], ps(bank)[:rows, 0:129], [("ps", bank)],
                             [("O_sb", tl, h)])
            for tl in range(ntile):
                okeys = [("O_sb", tl, h) for h in range(NH)]
                col0 = q0 + tl * 128
                rl, rlk = stat_slot(NH)
                rlkeys = stat_keys(rlk)
                recip(rl[:rows].unsqueeze(2), O_sb[tl][:rows, :, 128:129], okeys, rlkeys)
                act(sqa[:rows, :].rearrange("p (h d) -> p h d", d=128), O_sb[tl][:rows, :, 0:128], AF.Square,
                    okeys, [("sqa", 0)])
                ssh, sk = stat_slot(NH)
                sshk = stat_keys(sk)
                red(ssh[:rows], sqa[:rows, :].rearrange("p (h d) -> p h d", d=128), [("sqa", 0)], sshk)
                tt(ssh[:rows], ssh[:rows], rl[:rows], ALU.mult, sshk + rlkeys, sshk)
                tt(ssh[:rows], ssh[:rows], rl[:rows], ALU.mult, sshk + rlkeys, sshk)
                ss1, sk1 = stat_slot(1)
                ss1k = stat_keys(sk1)
                red(ss1[:rows], ssh[:rows], sshk, ss1k)
                r, rkeys = rstd_from_ss(ss1, ss1k, rows, 1, 1.0 / 1024.0)
                fac, fk = stat_slot(NH)
                fkeys = stat_keys(fk)
                tsc(fac[:rows], rl[:rows], r[:rows, 0:1], None, ALU.mult, None, rlkeys + rkeys, fkeys)
                for h in range(NH):
                    stt(a_tm[:rows, h * 128:(h + 1) * 128], O_sb[tl][:rows, h, 0:128], fac[:rows, h:h + 1],
                        ga_bc[:rows, h * 128:(h + 1) * 128], ALU.mult, ALU.mult,
                        okeys + fkeys + [("ga_bc", 0)], [("a_tm", 0)])
                pb = ps_bf(5, 1)
                for h in range(NH):
                    tr(pb[:, h * 128:h * 128 + rows], a_tm[:rows, h * 128:(h + 1) * 128], ident[:rows, :rows],
                       [("a_tm", 0)] + KI, [("ps", 5)])
                cpy[0] += 1
                copy_any(cpy[0], mixT[:, 0:8, col0:col0 + rows],
                         pb.rearrange("p (h t) -> p h t", t=128)[:, :, 0:rows], [("ps", 5)], [("mixT", 0, col0 // 128)])

        ZT = A.alloc("ZT", R_Q, 32832, BF16, "p (a g t) -> p a g t", a=2, g=8)
        o = R_A
        CS = [A.alloc("CS", o + i * 16384, 16384, BF16, "p (a s k) -> p a s k", a=2, s=16) for i in range(2)]
        o += 32768
        gf_bc = A.alloc("gf_bc", o, 4096, F32); o += 4096
        f_n = A.alloc("f_n", o, 2048, BF16); o += 2048
        junk4 = A.alloc("junk4", o, 4096, F32); o += 4096
        AB = A.alloc("AB", o, 4096, BF16, "p (g a d) -> p g a d", a=2, d=128); o += 4096
        ccs = A.alloc("ccs", o, 512, BF16, "p (a c) -> p a c", a=2); o += 512
        wfb = A.alloc("wfb", o, 2048, BF16); o += 2048
        dma("sp", gf_bc, gf_d.partition_broadcast(128), [], [("gf_bc", 0)], "c_gf")
        dma("sp", ccs, ccs_d.rearrange("p (a c) -> p a c", a=2), [], [("ccs", 0)], "c_ccs")
        dma("pool", wfb, wf_d, [], [("wfb", 0)], "c_wf")
        for g in range(8):
            for a in range(2):
                mm(ps(0)[:, a * 128:(a + 1) * 128], ccs[:, a, :], wfb[:, g * 128:(g + 1) * 128], True, True,
                   [("ccs", 0), ("wfb", 0)], [("ps", 0)])
            cpy[0] += 1
            copy_any(cpy[0], AB[:, g, :, :], ps(0)[:, 0:256].rearrange("p (a d) -> p a d", d=128), [("ps", 0)],
                     [("AB", g)])
        zb = [0]
        for gi, (k0, n) in enumerate(qgroups):
            ntile = 2 if n == 256 else 1
            rows = 128 if n == 256 else 1
            ci = gi % 2
            dma("sp", CS[ci], dft_d[gi].rearrange("p (a s k) -> p a s k", a=2, s=16), [], [("CS", ci)], f"CS{ci}")
            for g in range(8):
                bz = 2 * (zb[0] % 2)
                zb[0] += 1
                for sc in range(16):
                    for a in range(2):
                        mm(ps(bz + a)[:, :n], f_tm[:, sc, g * 128:(g + 1) * 128], CS[ci][:, a, sc, :n],
                           sc == 0, sc == 15, [("f_tm", sc, g // 4), ("CS", ci)], [("ps", bz + a)])
                for a in range(2):
                    cpy[0] += 1
                    copy_any(cpy[0], ZT[:, a, g, k0:k0 + n], ps(bz + a)[:, :n], [("ps", bz + a)],
                             [("ZT", a, g, gi)])
            for tl in range(ntile):
                col0 = k0 + tl * 128
                pk = [("ps", 4), ("ps", 5)]
                pout = psum_t[:, 4 * 512:6 * 512]
                for g in range(8):
                    for a in range(2):
                        mm(pout[:rows, g * 128:(g + 1) * 128], ZT[:, a, g, col0:col0 + rows], AB[:, g, a, :],
                           a == 0, a == 1, [("ZT", a, g, gi), ("AB", g)], [("ps", 4 + g // 4)])
                ss, sk = stat_slot(1)
                sskeys = stat_keys(sk)
                act(junk4[:rows], pout[:rows], AF.Square, pk, [("junk4", 0)] + sskeys, accum=ss[:rows])
                r, rkeys = rstd_from_ss(ss, sskeys, rows, 1, 1.0 / 1024.0)
                stt(f_n[:rows], pout[:rows], r[:rows, 0:1], gf_bc[:rows], ALU.mult, ALU.mult,
                    pk + rkeys + [("gf_bc", 0)], [("f_n", 0)])
                bank = 6 + (tl % 2)
                pb = ps_bf(bank, 1)
                for h in range(8):
                    tr(pb[:, h * 128:h * 128 + rows], f_n[:rows, h * 128:(h + 1) * 128], ident[:rows, :rows],
                       [("f_n", 0)] + KI, [("ps", bank)])
                cpy[0] += 1
                copy_any(cpy[0], mixT[:, 8:16, col0:col0 + rows],
                         pb.rearrange("p (h t) -> p h t", t=128)[:, :, 0:rows], [("ps", bank)],
                         [("mixT", 1, col0 // 128)])

        tap("mixT", mixT, ["mixT"])
        h_sb = A.alloc("h", R_F, 65536, F32, "p (t d) -> p t d", d=2048)
        o = R_A
        u2T = A.alloc("u2T", o, 32832, BF16, "p (k t) -> p k t", t=1026); o += 32832
        g2_bc = A.alloc("g2_bc", o, 8192, F32); o += 8192
        u2_tm = A.alloc("u2_tm", o, 4096, BF16); o += 4096
        h_halo = A.alloc("h", o, 8192, F32); o += 8192
        dma("sp", g2_bc, g2_d.partition_broadcast(128), [], [("g2_bc", 0)], "c_g2")
        for t in range(8):
            dma("sp", h_sb[:, t, :], x_d[t * 128:(t + 1) * 128, :], [], [("h", t, q) for q in range(4)], f"hx{t}")
        dma("sp", h_halo[0:1, :], x_d[1024:1025, :], [], [("h", 8, q) for q in range(4)], "hx8")

        def htile(t):
            return h_sb[:, t, :] if t < 8 else h_halo

        wacc = [0]
        for cbk in range(4):
            si = load_slot(wout_d[:, cbk * 8192:(cbk + 1) * 8192], 8192)
            w = slots[si].rearrange("p (k c) -> p k c", c=512)
            for t in range(9):
                rows = 128 if t < 8 else 1
                col0 = t * 128
                bank = wacc[0] % 4
                wacc[0] += 1
                for fc in range(16):
                    mm(ps(bank)[:rows, :], mixT[:, fc, col0:col0 + rows], w[:, fc, :], fc == 0, fc == 15,
                       [("mixT", fc // 8, t), ("slot", si)], [("ps", bank)])
                hv = htile(t)[:rows, cbk * 512:(cbk + 1) * 512]
                tt(hv, ps(bank)[:rows, :], hv, ALU.add, [("ps", bank), ("h", t, cbk)], [("h", t, cbk)])
        for t in range(9):
            rows = 128 if t < 8 else 1
            hk = [("h", t, q) for q in range(4)]
            norm_tile(htile(t), hk, rows, g2_bc, ("g2_bc", 0), u2_tm, [("u2_tm", 0)], u2_tm, [("u2_tm", 0)])
            transposes16(u2_tm, [("u2_tm", 0)], rows, u2T, [("u2T", t)], t * 128, 4 + 2 * (t % 2))

        tap("h1", h_sb, ["h"])
        tap("u2T", u2T, ["u2T"])
        actT = A.alloc("actT", R_MIX, 24576, BF16, "p (c t) -> p c t", t=1024)
        o = R_A + 32832
        tmpG = A.alloc("tmp", o, 4096, F32); o += 4096
        tmpV = A.alloc("tmp", o, 4096, F32); o += 4096
        sg = A.alloc("sg", o, 4096, F32); o += 4096
        u2keys = [("u2T", t) for t in range(9)]
        dacc = [0]
        for fb, (cbase, nch) in enumerate(FFB):
            for pbk in range(nch // 2):
                blk = cbase // 2 + pbk
                si = load_slot(wup_d[:, blk * 8192:(blk + 1) * 8192], 8192)
                w = slots[si].rearrange("p (k j c) -> p k j c", j=4, c=128)
                for pi in range(2):
                    cglob = blk * 2 + pi
                    lc = cglob - cbase
                    for gv in range(2):
                        j = 2 * pi + gv
                        b0 = 2 * gv
                        pkm = [("ps", b0), ("ps", b0 + 1)]
                        for k in range(16):
                            mm(ps(b0), w[:, k, j, :], u2T[:, k, 0:512], k == 0, k == 15,
                               [("slot", si)] + u2keys[0:4], [("ps", b0)])
                            mm(ps(b0 + 1), w[:, k, j, :], u2T[:, k, 512:1024], k == 0, k == 15,
                               [("slot", si)] + u2keys[4:8], [("ps", b0 + 1)])
                            mm(ps(4, 1, gv), w[:, k, j, :], u2T[:, k, 1024:1025], k == 0, k == 15,
                               [("slot", si), ("u2T", 8)], [("psH", gv)])
                        chunk = cglob if gv == 0 else NCH + cglob
                        w0 = cw[:, chunk * 3 + 0:chunk * 3 + 1]
                        w1 = cw[:, chunk * 3 + 1:chunk * 3 + 2]
                        w2 = cw[:, chunk * 3 + 2:chunk * 3 + 3]
                        bb = cb[:, chunk:chunk + 1]
                        tmp = tmpG if gv == 0 else tmpV
                        tk = [("tmp", gv)]
                        pfull = psum_t[:, b0 * 512:(b0 + 2) * 512]
                        act(tmp, pfull, AF.Identity, pkm + [("cw", 0), ("cb", 0)], tk, bias=bb, scale=w1)
                        stt(tmp[:, 1:1024], pfull[:, 0:1023], w0, tmp[:, 1:1024], ALU.mult, ALU.add,
                            pkm + tk + [("cw", 0)], tk)
                        stt(tmp[:, 0:1023], pfull[:, 1:1024], w2, tmp[:, 0:1023], ALU.mult, ALU.add,
                            pkm + tk + [("cw", 0)], tk)
                        stt(tmp[:, 1023:1024], ps(4, 1, gv), w2, tmp[:, 1023:1024], ALU.mult, ALU.add,
                            [("psH", gv)] + tk + [("cw", 0)], tk)
                    act(sg, tmpG, AF.Silu, [("tmp", 0)], [("sg", 0)])
                    tt(actT[:, lc, :], tmpV, sg, ALU.mult, [("tmp", 1), ("sg", 0)], [("actT", lc)])
            for q in range(4):
                off = (cbase * 4 + q * nch) * 512
                si = load_slot(wdn_d[:, off:off + nch * 512], nch * 512)
                w = slots[si][:, 0:nch * 512].rearrange("p (j c) -> p j c", c=512)
                for t in range(8):
                    bank = 5 + dacc[0] % 3
                    dacc[0] += 1
                    for j in range(nch):
                        mm(ps(bank), actT[:, j, t * 128:(t + 1) * 128], w[:, j, :], j == 0, j == nch - 1,
                           [("actT", j), ("slot", si)], [("ps", bank)])
                    hv = h_sb[:, t, q * 512:(q + 1) * 512]
                    tt(hv, ps(bank), hv, ALU.add, [("ps", bank), ("h", t, q)], [("h", t, q)])

        gF_bc = A.alloc("gF_bc", R_A, 8192, F32)
        junk7 = A.alloc("junk7", R_A + 8192, 4096, BF16)
        dma("sp", gF_bc, gF_d.partition_broadcast(128), [], [("gF_bc", 0)], "c_gF")
        for t in range(8):
            hk = [("h", t, q) for q in range(4)]
            ss, sk = stat_slot(1)
            sskeys = stat_keys(sk)
            act(junk7, h_sb[:, t, :], AF.Square, hk, [("junk7", 0)] + sskeys, accum=ss)
            r, rkeys = rstd_from_ss(ss, sskeys, 128, 1, 1.0 / D)
            stt(h_sb[:, t, :], h_sb[:, t, :], r[:, 0:1], gF_bc, ALU.mult, ALU.mult,
                hk + rkeys + [("gF_bc", 0)], hk)
            dma("sp", y_d[t * 128:(t + 1) * 128, :], h_sb[:, t, :], hk, [("y", t)], f"y{t % 2}")
        S_.op("sp", None, [("y", t) for t in range(8)] + [k for k in S_.lastw if k[0] == "dbg"], [])

        for e in ENGS:
            cnt = 0
            for o_ in S_.per[e]:
                if o_.signal and not o_.stream:
                    cnt += 1
                    o_.sigval = cnt
        ngen = {e: (max([o_.sigval for o_ in S_.per[e]] + [0]) // SEM_GEN) + 1 for e in ENGS}
        eng_sems = {e: [es.enter_context(nc.semaphore(f"s_{e}_{g}")) for g in range(ngen[e])] for e in ENGS}
        stream_sems = {s: es.enter_context(nc.semaphore(f"d_{s}")) for s in S_.stream_cnt}
        block = es.enter_context(nc.Block())

        def emit(engname, eng):
            seen = {}
            for o_ in S_.per[engname]:
                for d in o_.deps:
                    if d.stream:
                        k = ("d", d.stream)
                        val = d.sval
                        sem = stream_sems[d.stream]
                    else:
                        g = (d.sigval - 1) // SEM_GEN
                        k = ("e", d.eng, g)
                        val = d.sigval - g * SEM_GEN
                        sem = eng_sems[d.eng][g]
                        if seen.get(("e", d.eng, g + 1), 0) > 0:
                            continue
                    if seen.get(k, 0) >= val:
                        continue
                    seen[k] = val
                    eng.wait_ge(sem, val)
                if o_.fn is None:
                    continue
                ins = o_.fn(eng)
                if o_.stream:
                    ins.then_inc(stream_sems[o_.stream], 16)
                elif o_.signal:
                    g = (o_.sigval - 1) // SEM_GEN
                    ins.then_inc(eng_sems[o_.eng][g], 1)

        @block.tensor
        def _(e):
            emit("pe", e)

        @block.scalar
        def _(e):
            emit("act", e)

        @block.vector
        def _(e):
            emit("dve", e)

        @block.gpsimd
        def _(e):
            emit("pool", e)

        @block.sync
        def _(e):
            emit("sp", e)

    return nc


def _const_tables(half):
    perm = np.arange(S) if half == 0 else (S - 1 - np.arange(S))
    inv_freq = (np.float32(10000.0) ** (-np.arange(32, dtype=np.float32) / np.float32(32))).astype(np.float32)
    row = (perm // 64).astype(np.float32)
    col = (perm % 64).astype(np.float32)
    ar = row[:, None] * inv_freq[None, :]
    ac = col[:, None] * inv_freq[None, :]
    cr, sr = np.cos(ar).astype(np.float32), np.sin(ar).astype(np.float32)
    cc, sc_ = np.cos(ac).astype(np.float32), np.sin(ac).astype(np.float32)
    rope = np.concatenate([cr, cr, cc, cc, -sr, sr, -sc_, sc_], axis=1).astype(np.float32)
    ps_ = perm.astype(np.int64)
    pk_ = perm[:NQ].astype(np.int64)
    m = (ps_[:, None] * pk_[None, :]) % S
    ang = 2.0 * np.pi * m.astype(np.float64) / S
    ct = (np.cos(ang) / np.sqrt(S))
    st = (np.sin(ang) / np.sqrt(S))
    dft = np.zeros((5, 128, 2, 16, 256), dtype=np.float32)
    for gi in range(5):
        k0 = gi * 256
        n = 256 if gi < 4 else 1
        for a, tab in enumerate((ct, st)):
            blk = tab[:, k0:k0 + n].reshape(16, 128, n)
            dft[gi, :, a, :, :n] = blk.transpose(1, 0, 2)
    dft = dft.reshape(5, 128, 2 * 16 * 256).astype(ml_dtypes.bfloat16)
    return rope, dft


def _channel_dft():
    c = np.arange(128)
    ang = 2.0 * np.pi * ((c[:, None] * c[None, :]) % 128) / 128.0
    cc = np.cos(ang) / np.sqrt(128.0)
    ns = -np.sin(ang) / np.sqrt(128.0)
    return np.concatenate([cc, ns], axis=1).astype(ml_dtypes.bfloat16)


def _tile_rows(w, ncols_blk):
    K, N = w.shape
    nk = K // 128
    nb = N // ncols_blk
    a = w.reshape(nk, 128, nb, ncols_blk).transpose(1, 2, 0, 3)
    return np.ascontiguousarray(a).reshape(128, nb * nk * ncols_blk)


def prep_inputs(x, norm1_g, w_in, q_norm_g, k_norm_g, w_fmix, attn_out_g, fourier_out_g, w_out, norm2_g,
                w_up, conv_w, conv_b, w_down, final_g):
    f32 = np.float32
    x = np.asarray(x, f32)
    w_in0 = np.asarray(w_in, f32)[0]
    w_out0 = np.asarray(w_out, f32)[0]
    w_up0 = np.asarray(w_up, f32)[0]
    w_dn0 = np.asarray(w_down, f32)[0]
    conv_w0 = np.asarray(conv_w, f32)[0]
    conv_b0 = np.asarray(conv_b, f32)[0]
    win_t = _tile_rows(w_in0, 512)
    wout_t = _tile_rows(w_out0, 512)
    g = w_up0[:, :DFF].reshape(16, 128, NCH, 128)
    v = w_up0[:, DFF:].reshape(16, 128, NCH, 128)
    gv = np.stack([g, v], axis=3)
    gv = gv.reshape(16, 128, 22, 2, 2, 128)
    wup_t = np.ascontiguousarray(gv.transpose(1, 2, 0, 3, 4, 5)).reshape(128, 22 * 8192)
    parts = []
    wd = w_dn0.reshape(NCH, 128, 4, 512)
    for (cbase, nch) in FFB:
        for q in range(4):
            parts.append(np.ascontiguousarray(wd[cbase:cbase + nch, :, q, :].transpose(1, 0, 2)).reshape(128, nch * 512))
    wdn_t = np.concatenate(parts, axis=1)
    wf_t = np.ascontiguousarray(np.asarray(w_fmix, f32)[0].transpose(1, 0, 2)).reshape(128, 1024)
    cb_t = np.ascontiguousarray(conv_b0.reshape(88, 128).T)
    ccs = _channel_dft()
    ident = np.eye(128, dtype=f32).astype(ml_dtypes.bfloat16)
    tabs = [_const_tables(0), _const_tables(1)]
    common = {
        "ccs": ccs, "ident": ident,
        "g1": np.asarray(norm1_g, f32)[0], "g2": np.asarray(norm2_g, f32)[0], "gF": np.asarray(final_g, f32),
        "ga": np.asarray(attn_out_g, f32)[0], "gf": np.asarray(fourier_out_g, f32)[0],
        "gq": np.asarray(q_norm_g, f32)[0], "gk": np.asarray(k_norm_g, f32)[0],
        "cb": cb_t, "wf": wf_t, "win": win_t, "wout": wout_t, "wup": wup_t, "wdn": wdn_t,
    }
    in_maps = []
    for c in range(8):
        b, half = c // 2, c % 2
        xl = x[b] if half == 0 else x[b][::-1]
        cwc = conv_w0 if half == 0 else conv_w0[::-1]
        cw_t = np.ascontiguousarray(cwc.reshape(3, 88, 128).transpose(2, 1, 0)).reshape(128, 88 * 3)
        m = dict(common)
        m["x_loc"] = np.ascontiguousarray(xl)
        m["rope"] = tabs[half][0]
        m["dft"] = tabs[half][1]
        m["cw"] = cw_t
        in_maps.append(m)
    return in_maps


_NC_CACHE = {}


def kernel(**inputs):
    in_maps = prep_inputs(**inputs)
    if "nc" not in _NC_CACHE:
        _NC_CACHE["nc"] = build_program()
    nc = _NC_CACHE["nc"]
    res = run_bass_kernel_spmd(nc, in_maps, core_ids=list(range(8)))
    out = np.empty((4, S, D), dtype=np.float32)
    for c in range(8):
        b, half = c // 2, c % 2
        y = np.asarray(res.results[c]["y"], dtype=np.float32)
        if half == 0:
            out[b, :T] = y
        else:
            out[b, T:] = y[::-1]
    return out
```

```python
from contextlib import ExitStack

import numpy as np
import ml_dtypes

import concourse.bass as bass
import concourse.mybir as mybir
from concourse.bass_utils import run_bass_kernel_spmd

F32 = mybir.dt.float32
BF16 = mybir.dt.bfloat16
AF = mybir.ActivationFunctionType
ALU = mybir.AluOpType
AX = mybir.AxisListType

D = 2048
S = 2048
T = 1024
NQ = T + 1
HD = 128
NH = 8
NKV = 2
DFF = 5632
NCH = DFF // 128
EPS = 1e-6
FFB = [(0, 12), (12, 12), (24, 12), (36, 8)]
SEM_GEN = 3000
TAP_UTE = False

ENGS = ["pe", "act", "dve", "pool", "sp"]


class Op:
    __slots__ = ("eng", "fn", "stream", "sval", "signal", "sigval", "gidx", "deps", "phase")


class Sched:
    def __init__(self):
        self.ops = []
        self.per = {e: [] for e in ENGS}
        self.lastw = {}
        self.readers = {}
        self.touch = {}
        self.inherit = {}
        self.stream_cnt = {}
        self.phase = ""

    @staticmethod
    def _k(o):
        return ("d", o.stream) if o.stream else ("e", o.eng)

    def op(self, eng, fn, reads=(), writes=(), stream=None):
        o = Op()
        o.eng = eng
        o.fn = fn
        o.stream = stream
        o.signal = False
        o.sigval = 0
        o.sval = 0
        o.gidx = len(self.ops)
        o.phase = self.phase
        if stream:
            self.stream_cnt[stream] = self.stream_cnt.get(stream, 0) + 1
            o.sval = 16 * self.stream_cnt[stream]
        deps = {}

        def add(d):
            k = self._k(d)
            if k not in deps or deps[k].gidx < d.gidx:
                deps[k] = d

        reads = list(reads)
        writes = list(writes)
        for key in reads + writes:
            if key not in self.lastw and key not in self.readers:
                for d in self.inherit.get(key[0], {}).values():
                    add(d)
        for key in reads:
            w = self.lastw.get(key)
            if w is not None:
                add(w)
        for key in writes:
            w = self.lastw.get(key)
            if w is not None and not (key[0] == "junkq" and w.eng == eng and not w.stream):
                add(w)
            for r in self.readers.get(key, {}).values():
                add(r)
        for key in reads:
            self.readers.setdefault(key, {})[self._k(o)] = o
        for key in writes:
            self.lastw[key] = o
            self.readers[key] = {}
        for key in reads + writes:
            self.touch.setdefault(key[0], {})[self._k(o)] = o
        o.deps = []
        for d in deps.values():
            if (not d.stream) and d.eng == "pe" and eng == "pe" and not stream:
                continue
            o.deps.append(d)
            if not d.stream:
                d.signal = True
        self.ops.append(o)
        self.per[eng].append(o)
        return o

    def alias(self, new_name, old_names):
        m = self.inherit.setdefault(new_name, {})
        for n in old_names:
            for k, d in self.touch.get(n, {}).items():
                if k not in m or m[k].gidx < d.gidx:
                    m[k] = d
            for k, d in self.inherit.get(n, {}).items():
                if k not in m or m[k].gidx < d.gidx:
                    m[k] = d


class Arena:
    def __init__(self, tensor, sched, nbytes):
        self.t = tensor
        self.s = sched
        self.nbytes = nbytes
        self.live = []

    def alloc(self, name, off, nbytes, dtype, shape_str=None, **dims):
        assert off % 4 == 0 and off + nbytes <= self.nbytes, (name, off, nbytes)
        b0, b1 = off, off + nbytes
        old = sorted(set(n for (n, a0, a1) in self.live if a0 < b1 and b0 < a1))
        self.live.append((name, b0, b1))
        if old:
            self.s.alias(name, old)
        w0 = off // 4
        w1 = (off + nbytes + 3) // 4
        ap = self.t[:, w0:w1]
        if dtype != F32:
            ap = ap.bitcast(dtype)
        if shape_str:
            ap = ap.rearrange(shape_str, **dims)
        return ap


def build_program(debug=False):
    nc = bass.Bass("TRN2", target_bir_lowering=False)
    S_ = Sched()

    def dram(name, shape, dt=F32, kind="ExternalInput"):
        return nc.dram_tensor(name, list(shape), dt, kind=kind).ap()

    x_d = dram("x_loc", [S, D])
    rope_d = dram("rope", [S, 256])
    dft_d = dram("dft", [5, 128, 2 * 16 * 256], BF16)
    ccs_d = dram("ccs", [128, 256], BF16)
    ident_d = dram("ident", [128, 128], BF16)
    g1_d = dram("g1", [D])
    g2_d = dram("g2", [D])
    gF_d = dram("gF", [D])
    ga_d = dram("ga", [1024])
    gf_d = dram("gf", [1024])
    gq_d = dram("gq", [128])
    gk_d = dram("gk", [128])
    cw_d = dram("cw", [128, 88 * 3])
    cb_d = dram("cb", [128, 88])
    wf_d = dram("wf", [128, 1024])
    win_d = dram("win", [128, 5 * 8192])
    wout_d = dram("wout", [128, 4 * 8192])
    wup_d = dram("wup", [128, 22 * 8192])
    wdn_d = dram("wdn", [128, 44 * 2048])
    y_d = dram("y", [T, D], F32, kind="ExternalOutput")

    ARENA_BYTES = 212480
    es = ExitStack()
    with es:
        arena_t = es.enter_context(nc.sbuf_tensor("arena", [128, ARENA_BYTES // 4], F32))
        psum_t = es.enter_context(nc.psum_tensor("psum", [128, 4096], F32))
        A = Arena(arena_t, S_, ARENA_BYTES)

        R_SLOT = 0
        R_F = 49152
        R_Q = 81920
        R_MIX = 114944
        R_A = 147840
        R_C = 207744

        slots = [A.alloc("slot", R_SLOT + i * 16384, 16384, BF16) for i in range(3)]
        ident = A.alloc("ident", R_C, 256, BF16)
        cw = A.alloc("cw", R_C + 256, 1056, F32)
        cb = A.alloc("cb", R_C + 1312, 352, F32)
        stat = A.alloc("stat", R_C + 1664, 512, F32)
        epsc = A.alloc("epsc", R_C + 2176, 4, F32)
        junkq = A.alloc("junkq", R_C + 2184, 2048, mybir.dt.float8e4)
        JK = [("junkq", 0)]

        def ps(bank, n=512, off=0):
            return psum_t[:, bank * 512 + off: bank * 512 + off + n]

        def ps_bf(bank, nbanks=1):
            return psum_t[:, bank * 512:(bank + nbanks) * 512].bitcast(BF16)

        def mm(out, lhsT, rhs, start, stop, reads, writes):
            return S_.op("pe", lambda e: e.matmul(out, lhsT, rhs, start=start, stop=stop), reads, writes)

        def tr(out, in_, idn, reads, writes):
            return S_.op("pe", lambda e: e.transpose(out, in_, idn), reads, writes)

        def act(out, in_, func, reads, writes, bias=None, scale=None, accum=None, sat=None):
            kw = {}
            if sat is not None:
                kw["saturate"] = sat
            if bias is not None:
                kw["bias"] = bias
            if scale is not None:
                kw["scale"] = scale
            if accum is not None:
                kw["accum_out"] = accum
            return S_.op("act", lambda e: e.activation(out, in_, func, **kw), reads, writes)

        def dve(fn, reads, writes):
            return S_.op("dve", fn, reads, writes)

        def tt(out, in0, in1, op, reads, writes):
            return dve(lambda e: e.tensor_tensor(out, in0, in1, op), reads, writes)

        def stt(out, in0, scalar, in1, op0, op1, reads, writes):
            return dve(lambda e: e.scalar_tensor_tensor(out, in0, scalar, in1, op0, op1), reads, writes)

        def tsc(out, in0, s1, s2, op0, op1, reads, writes):
            if op1 is None:
                return dve(lambda e: e.tensor_scalar(out, in0, s1, s2, op0), reads, writes)
            return dve(lambda e: e.tensor_scalar(out, in0, s1, s2, op0, op1), reads, writes)

        def red(out, in_, reads, writes):
            return dve(lambda e: e.tensor_reduce(out, in_, AX.X, ALU.add), reads, writes)

        def recip(out, in_, reads, writes):
            return dve(lambda e: e.reciprocal(out, in_), reads, writes)

        def copy_any(which, out, in_, reads, writes):
            if which % 2 == 0:
                return act(out, in_, AF.Copy, reads, writes)
            return dve(lambda e: e.tensor_copy(out, in_), reads, writes)

        def dma(eng, out, in_, reads, writes, stream, **kw):
            return S_.op(eng, lambda e: e.dma_start(out=out, in_=in_, **kw), reads, writes, stream=stream)

        stat_ctr = [0]

        def stat_slot(n=1):
            i = stat_ctr[0]
            if i % 128 + n > 128:
                i += 128 - i % 128
            stat_ctr[0] = i + n
            c = i % 128
            return stat[:, c:c + n], ("stat", c // 16, (c + n - 1) // 16)

        def stat_keys(k):
            return [("stat", j) for j in range(k[1], k[2] + 1)]

        def rstd_from_ss(ss, sskeys, rows, ncols, inv_n):
            r, rk = stat_slot(ncols)
            rkeys = stat_keys(rk)
            act(r[:rows], ss[:rows], AF.Sqrt, sskeys + [("epsc", 0)], rkeys, bias=epsc[:rows], scale=inv_n)
            recip(r[:rows], r[:rows], rkeys, rkeys)
            return r, rkeys

        slot_ctr = [0]

        def load_slot(src_ap, ncols):
            i = slot_ctr[0] % 3
            slot_ctr[0] += 1
            dma("pool", slots[i][:, 0:ncols], src_ap, [], [("slot", i)], f"slot{i}", max_dma_last_dim=8192)
            return i

        def tap(name, ap, names):
            if not debug:
                return
            shape = [int(s) for s in ap.shape]
            dd = nc.dram_tensor("dbg_" + name, shape, ap.dtype, kind="ExternalOutput").ap()
            reads = [k for k in S_.lastw if k[0] in names]
            dma("sp", dd, ap, reads, [("dbg", name)], "dbg_" + name)

        dma("sp", ident, ident_d, [], [("ident", 0)], "c_ident")
        dma("sp", cw, cw_d, [], [("cw", 0)], "c_cw")
        dma("sp", cb, cb_d, [], [("cb", 0)], "c_cb")
        KI = [("ident", 0)]

        f_tm = A.alloc("f_tm", R_F, 32768, BF16, "p (s c) -> p s c", c=1024)
        qT = A.alloc("qT", R_Q, 16416, BF16, "p (h t) -> p h t", t=1026)
        kT = A.alloc("kT", R_Q + 16416, 8192, BF16, "p (h t) -> p h t", t=2048)
        Vaug = A.alloc("V", R_Q + 24608, 8320, BF16, "p (s h c) -> p s h c", h=2, c=130)
        uT = A.alloc("uT", R_A, 32768, BF16, "p (k t) -> p k t", t=1024)
        o = R_A + 32768
        sq = A.alloc("sq", o, 2048, F32); o += 2048
        qn = A.alloc("qn", o, 2048, F32); o += 2048
        t1 = A.alloc("t1", o, 2048, F32); o += 2048
        t2 = A.alloc("t2", o, 2048, F32); o += 2048
        qr = [A.alloc("qr", o + i * 1024, 1024, BF16) for i in range(3)]; o += 3072
        ropet = [A.alloc("ropet", o + i * 1024, 1024, F32) for i in range(2)]; o += 2048
        gq_bc = A.alloc("gq_bc", o, 512, F32); o += 512
        gk_bc = A.alloc("gk_bc", o, 512, F32); o += 512
        xs = [A.alloc("xs", R_MIX + i * 8192, 8192, F32) for i in range(2)]
        g1_bc = A.alloc("g1_bc", R_MIX + 16384, 8192, F32)
        u_tm = [A.alloc("u_tm", R_MIX + 24576 + i * 4096, 4096, BF16) for i in range(2)]

        dve(lambda e: e.memset(epsc, EPS), [], [("epsc", 0)])
        dma("sp", g1_bc, g1_d.partition_broadcast(128), [], [("g1_bc", 0)], "c_g1")
        dma("sp", gq_bc, gq_d.partition_broadcast(128), [], [("gq_bc", 0)], "c_gq")
        dma("sp", gk_bc, gk_d.partition_broadcast(128), [], [("gk_bc", 0)], "c_gk")
        dve(lambda e: e.memset(Vaug[:, :, :, 128:129], 1.0), [], [("V", 99)])

        cpy = [0]
        rope_ctr = [0]
        psacc_ctr = [0]
        pstr_ctr = [0]
        qr_ctr = [0]

        def norm_tile(src, srckeys, rows, gbc, gkey, dst, dstkeys):
            ss, sk = stat_slot(1)
            sskeys = stat_keys(sk)
            act(junkq[:rows], src[:rows], AF.Square, srckeys, JK + sskeys, accum=ss[:rows], sat=False)
            r, rkeys = rstd_from_ss(ss, sskeys, rows, 1, 1.0 / D)
            stt(dst[:rows], src[:rows], r[:rows, 0:1], gbc[:rows], ALU.mult, ALU.mult,
                srckeys + rkeys + [gkey], dstkeys)

        def transposes16(src, srckeys, rows, dstT, dstkeys_fn, col0, bankpair, which):
            pb = ps_bf(bankpair, 2)
            pk = [("ps", bankpair), ("ps", bankpair + 1)]
            for j in range(16):
                tr(pb[:, j * 128: j * 128 + rows], src[:rows, j * 128:(j + 1) * 128], ident[:rows, :rows],
                   srckeys + KI, pk)
            copy_any(which, dstT[:, :, col0:col0 + rows],
                     pb.rearrange("p (k t) -> p k t", t=128)[:, :, 0:rows], pk, dstkeys_fn)

        def qk_post(psrc, pkeys, rows, nheads, gbc, gbkey, rtile, rkey, dstT, dst_h0, col0, dstkeys):
            n = nheads * 128
            act(sq[:rows, :n], psrc[:rows, :n], AF.Square, pkeys, [("sq", 0)])
            ss, sk = stat_slot(nheads)
            sskeys = stat_keys(sk)
            red(ss[:rows], sq[:rows, :n].rearrange("p (h d) -> p h d", d=128), [("sq", 0)], sskeys)
            r, rkeys = rstd_from_ss(ss, sskeys, rows, nheads, 1.0 / HD)
            for h in range(nheads):
                stt(qn[:rows, h * 128:(h + 1) * 128], psrc[:rows, h * 128:(h + 1) * 128],
                    r[:rows, h:h + 1], gbc[:rows], ALU.mult, ALU.mult,
                    pkeys + rkeys + [gbkey], [("qn", 0)])
            qn3 = qn[:rows, :n].rearrange("p (h d) -> p h d", d=128)
            cosb = rtile[:rows, 0:128].unsqueeze(1).to_broadcast([rows, nheads, 128])
            tt(t1[:rows, :n].rearrange("p (h d) -> p h d", d=128), qn3, cosb, ALU.mult,
               [("qn", 0), rkey], [("t1", 0)])
            qn5 = qn[:rows, :n].rearrange("p (h a b c) -> p h a b c", a=2, b=2, c=32)
            t25 = t2[:rows, :n].rearrange("p (h a b c) -> p h a b c", a=2, b=2, c=32)
            sin5 = rtile[:rows, 128:256].rearrange("p (a b c) -> p a b c", a=2, b=2, c=32)
            for blk in range(2):
                sb = sin5[:, :, blk, :].unsqueeze(1).to_broadcast([rows, nheads, 2, 32])
                tt(t25[:, :, :, blk, :], qn5[:, :, :, 1 - blk, :], sb, ALU.mult,
                   [("qn", 0), rkey], [("t2", blk)])
            qi = qr_ctr[0] % 3
            qr_ctr[0] += 1
            qrb = qr[qi]
            tt(qrb[:rows, :n], t1[:rows, :n], t2[:rows, :n], ALU.add,
               [("t1", 0), ("t2", 0), ("t2", 1)], [("qr", qi)])

            def tail():
                bank = pstr_ctr[0] % 3
                pstr_ctr[0] += 1
                pb = ps_bf(bank, 1)
                for h in range(nheads):
                    tr(pb[:, h * 128:h * 128 + rows], qrb[:rows, h * 128:(h + 1) * 128], ident[:rows, :rows],
                       [("qr", qi)] + KI, [("ps", bank)])
                copy_any(0, dstT[:, dst_h0:dst_h0 + nheads, col0:col0 + rows],
                         pb[:, 0:n].rearrange("p (h t) -> p h t", t=128)[:, :, 0:rows], [("ps", bank)], dstkeys)
            return tail

        def load_rope(gt):
            i = rope_ctr[0] % 2
            rope_ctr[0] += 1
            dma("sp", ropet[i], rope_d[gt * 128:(gt + 1) * 128, :], [], [("ropet", i)], f"ropet{i}")
            return ropet[i], ("ropet", i)

        pend = []

        def flush(keep):
            while len(pend) > keep:
                pend.pop(0)()

        for pa in range(2):
            if pa == 1 and TAP_UTE:
                tap("uTE", uT, ["uT"])
            S_.phase = f"P1{'AB'[pa]}"
            for t in range(8):
                gt = pa * 8 + t
                b = gt % 2
                dma("sp", xs[b], x_d[gt * 128:(gt + 1) * 128, :], [], [("xs", b)], f"xs{b}")
                norm_tile(xs[b], [("xs", b)], 128, g1_bc, ("g1_bc", 0), u_tm[b], [("u_tm", b)])
                pend.append(lambda t=t, b=b: transposes16(u_tm[b], [("u_tm", b)], 128, uT, [("uT", t)], t * 128,
                                                          2 * (t % 2), t))
                flush(1)
            flush(0)
            S_.phase = f"P2{'AB'[pa]}"
            blocks = [0, 1, 2, 3, 4] if pa == 0 else [2, 3, 4, 0, 1]
            for blk in blocks:
                si = load_slot(win_d[:, blk * 8192:(blk + 1) * 8192], 8192)
                w = slots[si].rearrange("p (k c) -> p k c", c=512)
                halo_only = (pa == 1 and blk < 2)
                tiles = [0] if halo_only else list(range(8))
                for t in tiles:
                    gt = pa * 8 + t
                    rows = 1 if halo_only else 128
                    bank = 4 + psacc_ctr[0] % 4
                    psacc_ctr[0] += 1
                    pk = [("ps", bank)]
                    for k in range(16):
                        mm(ps(bank)[:rows, :], uT[:, k, t * 128:t * 128 + rows], w[:, k, :], k == 0, k == 15,
                           [("uT", t), ("slot", si)], pk)
                    if blk < 2:
                        rt, rk = load_rope(gt)
                        col0 = 1024 if halo_only else t * 128
                        pend.append(qk_post(ps(bank), pk, rows, 4, gq_bc, ("gq_bc", 0), rt, rk, qT, 4 * blk, col0,
                                            [("qT", blk, gt)]))
                    elif blk == 2:
                        rt, rk = load_rope(gt)
                        act(Vaug[:, gt, :, 0:128], ps(bank)[:, 256:512].rearrange("p (h d) -> p h d", d=128),
                            AF.Copy, pk, [("V", gt)])
                        pend.append(qk_post(ps(bank), pk, 128, 2, gk_bc, ("gk_bc", 0), rt, rk, kT, 0, gt * 128,
                                            [("kT", gt)]))
                    else:
                        fb = blk - 3
                        act(f_tm[:, gt, fb * 512:(fb + 1) * 512], ps(bank), AF.Copy, pk, [("f_tm", gt, fb)])
                    flush(2)
            flush(0)

        tap("qT", qT, ["qT"])
        tap("kT", kT, ["kT"])
        tap("V", Vaug, ["V"])
        tap("f", f_tm, ["f_tm"])
        S_.phase = "P3"
        mixT = A.alloc("mixT", R_MIX, 32832, BF16, "p (k t) -> p k t", t=1026)
        o = R_A
        PT = [A.alloc("PT", o + i * 16384, 16384, BF16, "p (s q) -> p s q", q=512) for i in range(2)]; o += 32768
        O_sb = [A.alloc("O_sb", o + i * 4160, 4160, F32, "p (h c) -> p h c", c=130) for i in range(4)]; o += 16640
        sqa = A.alloc("sqa", o, 2048, BF16); o += 2048
        a_tm = A.alloc("a_tm", o, 2048, BF16); o += 2048
        ga_bc = A.alloc("ga_bc", o, 4096, F32); o += 4096
        dma("sp", ga_bc, ga_d.partition_broadcast(128), [], [("ga_bc", 0)], "c_ga")

        scale = 1.0 / float(np.sqrt(HD))
        sbank = [0]
        obank = [0]
        ptc = [0]

        def attn_norm(tl, rows, col0):
            okeys = [("O_sb", tl, h) for h in range(NH)]
            rl, rlk = stat_slot(NH)
            rlkeys = stat_keys(rlk)
            recip(rl[:rows].unsqueeze(2), O_sb[tl][:rows, :, 128:129], okeys, rlkeys)
            act(sqa[:rows, :].rearrange("p (h d) -> p h d", d=128), O_sb[tl][:rows, :, 0:128], AF.Square,
                okeys, [("sqa", 0)])
            ssh, sk = stat_slot(NH)
            sshk = stat_keys(sk)
            red(ssh[:rows], sqa[:rows, :].rearrange("p (h d) -> p h d", d=128), [("sqa", 0)], sshk)
            tt(ssh[:rows], ssh[:rows], rl[:rows], ALU.mult, sshk + rlkeys, sshk)
            tt(ssh[:rows], ssh[:rows], rl[:rows], ALU.mult, sshk + rlkeys, sshk)
            ss1, sk1 = stat_slot(1)
            ss1k = stat_keys(sk1)
            red(ss1[:rows], ssh[:rows], sshk, ss1k)
            r, rkeys = rstd_from_ss(ss1, ss1k, rows, 1, 1.0 / 1024.0)
            fac, fk = stat_slot(NH)
            fkeys = stat_keys(fk)
            tsc(fac[:rows], rl[:rows], r[:rows, 0:1], None, ALU.mult, None, rlkeys + rkeys, fkeys)
            for h in range(NH):
                stt(a_tm[:rows, h * 128:(h + 1) * 128], O_sb[tl][:rows, h, 0:128], fac[:rows, h:h + 1],
                    ga_bc[:rows, h * 128:(h + 1) * 128], ALU.mult, ALU.mult,
                    okeys + fkeys + [("ga_bc", 0)], [("a_tm", 0)])
            pb = ps_bf(7, 1)
            for h in range(NH):
                tr(pb[:, h * 128:h * 128 + rows], a_tm[:rows, h * 128:(h + 1) * 128], ident[:rows, :rows],
                   [("a_tm", 0)] + KI, [("ps", 7)])
            copy_any(1, mixT[:, 0:8, col0:col0 + rows],
                     pb.rearrange("p (h t) -> p h t", t=128)[:, :, 0:rows], [("ps", 7)], [("mixT", 0, col0 // 128)])

        def pv_head(pi, h, q0, ntile, rows):
            kv = h // 4
            for tl in range(ntile):
                bank = 3 + obank[0] % 4
                obank[0] += 1
                for sc in range(16):
                    mm(ps(bank)[:rows, 0:129], PT[pi][:, sc, tl * 128:tl * 128 + rows],
                       Vaug[:, sc, kv, 0:129], sc == 0, sc == 15,
                       [("PT", pi, sc), ("V", sc), ("V", 99)], [("ps", bank)])
                copy_any(1, O_sb[tl][:rows, h, 0:129], ps(bank)[:rows, 0:129], [("ps", bank)], [("O_sb", tl, h)])

        pend = []
        pend_norm = []
        for (q0, n) in [(0, 512), (512, 512)]:
            for h in range(NH):
                kv = h // 4
                pi = ptc[0] % 2
                ptc[0] += 1
                for sc in range(16):
                    bank = sbank[0] % 3
                    sbank[0] += 1
                    mm(ps(bank)[:, :n], kT[:, kv, sc * 128:(sc + 1) * 128], qT[:, h, q0:q0 + n], True, True,
                       [("kT", sc)] + [("qT", h // 4, q0 // 128 + j) for j in range(4)], [("ps", bank)])
                    act(PT[pi][:, sc, :n], ps(bank)[:, :n], AF.Exp, [("ps", bank)], [("PT", pi, sc)], scale=scale)
                if h == 1:
                    while pend_norm:
                        pend_norm.pop(0)()
                while pend:
                    pend.pop(0)()
                pend.append(lambda pi=pi, h=h, q0=q0: pv_head(pi, h, q0, 4, 128))
            while pend:
                pend.pop(0)()
            for tl in range(4):
                pend_norm.append(lambda tl=tl, q0=q0: attn_norm(tl, 128, q0 + tl * 128))
        pi = ptc[0] % 2
        ptc[0] += 1
        for h in range(NH):
            kv = h // 4
            bank = sbank[0] % 3
            sbank[0] += 1
            for sc in range(16):
                mm(ps(bank)[:, sc:sc + 1], kT[:, kv, sc * 128:(sc + 1) * 128], qT[:, h, 1024:1025], True, True,
                   [("kT", sc), ("qT", h // 4, 8)], [("ps", bank)])
            act(PT[pi][:, h, 0:16], ps(bank)[:, 0:16], AF.Exp, [("ps", bank)], [("PT", pi, h)], scale=scale)
        while pend_norm:
            pend_norm.pop(0)()
        for h in range(NH):
            kv = h // 4
            bank = 3 + obank[0] % 4
            obank[0] += 1
            for sc in range(16):
                mm(ps(bank)[:1, 0:129], PT[pi][:, h, sc:sc + 1], Vaug[:, sc, kv, 0:129], sc == 0, sc == 15,
                   [("PT", pi, h), ("V", sc), ("V", 99)], [("ps", bank)])
            copy_any(1, O_sb[0][:1, h, 0:129], ps(bank)[:1, 0:129], [("ps", bank)], [("O_sb", 0, h)])
        attn_norm(0, 1, 1024)

        S_.phase = "P4"
        ZT = A.alloc("ZT", R_Q, 32832, BF16, "p (a g t) -> p a g t", a=2, g=8)
        o = R_A
        CS = [A.alloc("CS", o + i * 16384, 16384, BF16, "p (a s k) -> p a s k", a=2, s=16) for i in range(2)]
        o += 32768
        gf_bc = A.alloc("gf_bc", o, 4096, F32); o += 4096
        f_n = A.alloc("f_n", o, 2048, BF16); o += 2048
        junk4 = A.alloc("junk4", o, 4096, F32); o += 4096
        AB = A.alloc("AB", o, 4096, BF16, "p (g a d) -> p g a d", a=2, d=128); o += 4096
        ccs = A.alloc("ccs", o, 512, BF16, "p (a c) -> p a c", a=2); o += 512
        wfb = A.alloc("wfb", o, 2048, BF16); o += 2048
        dma("sp", gf_bc, gf_d.partition_broadcast(128), [], [("gf_bc", 0)], "c_gf")
        dma("sp", ccs, ccs_d.rearrange("p (a c) -> p a c", a=2), [], [("ccs", 0)], "c_ccs")
        dma("pool", wfb, wf_d, [], [("wfb", 0)], "c_wf")
        for g in range(8):
            for a in range(2):
                mm(ps(0)[:, a * 128:(a + 1) * 128], ccs[:, a, :], wfb[:, g * 128:(g + 1) * 128], True, True,
                   [("ccs", 0), ("wfb", 0)], [("ps", 0)])
            cpy[0] += 1
            copy_any(cpy[0], AB[:, g, :, :], ps(0)[:, 0:256].rearrange("p (a d) -> p a d", d=128), [("ps", 0)],
                     [("AB", g)])
        zb = [0]
        qgroups = [(i * 256, 256) for i in range(4)] + [(1024, 1)]
        for gi, (k0, n) in enumerate(qgroups):
            ntile = 2 if n == 256 else 1
            rows = 128 if n == 256 else 1
            ci = gi % 2
            dma("sp", CS[ci], dft_d[gi].rearrange("p (a s k) -> p a s k", a=2, s=16), [], [("CS", ci)], f"CS{ci}")
            for g in range(8):
                bz = 2 * (zb[0] % 2)
                zb[0] += 1
                for sc in range(16):
                    for a in range(2):
                        mm(ps(bz + a)[:, :n], f_tm[:, sc, g * 128:(g + 1) * 128], CS[ci][:, a, sc, :n],
                           sc == 0, sc == 15, [("f_tm", sc, g // 4), ("CS", ci)], [("ps", bz + a)])
                for a in range(2):
                    cpy[0] += 1
                    copy_any(cpy[0], ZT[:, a, g, k0:k0 + n], ps(bz + a)[:, :n], [("ps", bz + a)],
                             [("ZT", a, g, gi)])
            for tl in range(ntile):
                col0 = k0 + tl * 128
                pk = [("ps", 4), ("ps", 5)]
                pout = psum_t[:, 4 * 512:6 * 512]
                for g in range(8):
                    for a in range(2):
                        mm(pout[:rows, g * 128:(g + 1) * 128], ZT[:, a, g, col0:col0 + rows], AB[:, g, a, :],
                           a == 0, a == 1, [("ZT", a, g, gi), ("AB", g)], [("ps", 4 + g // 4)])
                ss, sk = stat_slot(1)
                sskeys = stat_keys(sk)
                act(junkq[:rows, 0:1024], pout[:rows], AF.Square, pk, JK + sskeys, accum=ss[:rows], sat=False)
                r, rkeys = rstd_from_ss(ss, sskeys, rows, 1, 1.0 / 1024.0)
                stt(f_n[:rows], pout[:rows], r[:rows, 0:1], gf_bc[:rows], ALU.mult, ALU.mult,
                    pk + rkeys + [("gf_bc", 0)], [("f_n", 0)])
                bank = 6 + (tl % 2)
                pb = ps_bf(bank, 1)
                for h in range(8):
                    tr(pb[:, h * 128:h * 128 + rows], f_n[:rows, h * 128:(h + 1) * 128], ident[:rows, :rows],
                       [("f_n", 0)] + KI, [("ps", bank)])
                cpy[0] += 1
                copy_any(cpy[0], mixT[:, 8:16, col0:col0 + rows],
                         pb.rearrange("p (h t) -> p h t", t=128)[:, :, 0:rows], [("ps", bank)],
                         [("mixT", 1, col0 // 128)])

        tap("mixT", mixT, ["mixT"])
        S_.phase = "P5"
        h_sb = A.alloc("h", R_F, 65536, F32, "p (t d) -> p t d", d=2048)
        o = R_A
        u2T = A.alloc("u2T", o, 32832, BF16, "p (k t) -> p k t", t=1026); o += 32832
        g2_bc = A.alloc("g2_bc", o, 8192, F32); o += 8192
        u2_tm = [A.alloc("u2_tm", o + i * 4096, 4096, BF16) for i in range(2)]; o += 8192
        h_halo = A.alloc("h", o, 8192, F32); o += 8192
        dma("sp", g2_bc, g2_d.partition_broadcast(128), [], [("g2_bc", 0)], "c_g2")
        for t in range(8):
            dma("sp", h_sb[:, t, :], x_d[t * 128:(t + 1) * 128, :], [], [("h", t, q) for q in range(4)], f"hx{t}")
        dma("sp", h_halo[0:1, :], x_d[1024:1025, :], [], [("h", 8, q) for q in range(4)], "hx8")

        def htile(t):
            return h_sb[:, t, :] if t < 8 else h_halo

        wacc = [0]
        pend = []
        for cbk in range(4):
            si = load_slot(wout_d[:, cbk * 8192:(cbk + 1) * 8192], 8192)
            w = slots[si].rearrange("p (k c) -> p k c", c=512)
            for t in range(9):
                rows = 128 if t < 8 else 1
                col0 = t * 128
                bank = wacc[0] % 4
                wacc[0] += 1
                for fc in range(16):
                    mm(ps(bank)[:rows, :], mixT[:, fc, col0:col0 + rows], w[:, fc, :], fc == 0, fc == 15,
                       [("mixT", fc // 8, t), ("slot", si)], [("ps", bank)])
                hv = htile(t)[:rows, cbk * 512:(cbk + 1) * 512]
                tt(hv, ps(bank)[:rows, :], hv, ALU.add, [("ps", bank), ("h", t, cbk)], [("h", t, cbk)])
                if cbk == 3:
                    hk = [("h", t, q) for q in range(4)]
                    b = t % 2
                    norm_tile(htile(t), hk, rows, g2_bc, ("g2_bc", 0), u2_tm[b], [("u2_tm", b)])
                    pend.append(lambda t=t, b=b, rows=rows: transposes16(
                        u2_tm[b], [("u2_tm", b)], rows, u2T, [("u2T", t)], t * 128, 4 + 2 * (t % 2), t))
                    while len(pend) > 1:
                        pend.pop(0)()
        while pend:
            pend.pop(0)()

        tap("h1", h_sb, ["h"])
        tap("u2T", u2T, ["u2T"])
        S_.phase = "P6"
        actT = A.alloc("actT", R_MIX, 24576, BF16, "p (c t) -> p c t", t=1024)
        o = R_A + 32832
        tmpG = A.alloc("tmp", o, 4096, F32); o += 4096
        tmpV = A.alloc("tmp", o, 4096, F32); o += 4096
        sg = A.alloc("sg", o, 4096, F32); o += 4096
        gF_bc = A.alloc("gF_bc", o, 8192, F32); o += 8192
        dma("sp", gF_bc, gF_d.partition_broadcast(128), [], [("gF_bc", 0)], "c_gF")
        u2keys = [("u2T", t) for t in range(9)]
        dacc = [0]

        def final_tile(t):
            hk = [("h", t, q) for q in range(4)]
            ss, sk = stat_slot(1)
            sskeys = stat_keys(sk)
            act(junkq, h_sb[:, t, :], AF.Square, hk, JK + sskeys, accum=ss, sat=False)
            r, rkeys = rstd_from_ss(ss, sskeys, 128, 1, 1.0 / D)
            stt(h_sb[:, t, :], h_sb[:, t, :], r[:, 0:1], gF_bc, ALU.mult, ALU.mult,
                hk + rkeys + [("gF_bc", 0)], hk)
            dma("sp", y_d[t * 128:(t + 1) * 128, :], h_sb[:, t, :], hk, [("y", t)], f"y{t % 2}")

        for fb, (cbase, nch) in enumerate(FFB):
            for pbk in range(nch // 2):
                blk = cbase // 2 + pbk
                si = load_slot(wup_d[:, blk * 8192:(blk + 1) * 8192], 8192)
                w = slots[si].rearrange("p (k j c) -> p k j c", j=4, c=128)
                for pi in range(2):
                    cglob = blk * 2 + pi
                    lc = cglob - cbase
                    for gv in range(2):
                        j = 2 * pi + gv
                        b0 = 2 * gv
                        pkm = [("ps", b0), ("ps", b0 + 1)]
                        for k in range(16):
                            mm(ps(b0), w[:, k, j, :], u2T[:, k, 0:512], k == 0, k == 15,
                               [("slot", si)] + u2keys[0:4], [("ps", b0)])
                            mm(ps(b0 + 1), w[:, k, j, :], u2T[:, k, 512:1024], k == 0, k == 15,
                               [("slot", si)] + u2keys[4:8], [("ps", b0 + 1)])
                            mm(ps(4, 1, gv), w[:, k, j, :], u2T[:, k, 1024:1025], k == 0, k == 15,
                               [("slot", si), ("u2T", 8)], [("psH", gv)])
                        chunk = cglob if gv == 0 else NCH + cglob
                        w0 = cw[:, chunk * 3 + 0:chunk * 3 + 1]
                        w1 = cw[:, chunk * 3 + 1:chunk * 3 + 2]
                        w2 = cw[:, chunk * 3 + 2:chunk * 3 + 3]
                        bb = cb[:, chunk:chunk + 1]
                        tmp = tmpG if gv == 0 else tmpV
                        tk = [("tmp", gv)]
                        pfull = psum_t[:, b0 * 512:(b0 + 2) * 512]
                        act(tmp, pfull, AF.Identity, pkm + [("cw", 0), ("cb", 0)], tk, bias=bb, scale=w1)
                        stt(tmp[:, 1:1024], pfull[:, 0:1023], w0, tmp[:, 1:1024], ALU.mult, ALU.add,
                            pkm + tk + [("cw", 0)], tk)
                        stt(tmp[:, 0:1023], pfull[:, 1:1024], w2, tmp[:, 0:1023], ALU.mult, ALU.add,
                            pkm + tk + [("cw", 0)], tk)
                        stt(tmp[:, 1023:1024], ps(4, 1, gv), w2, tmp[:, 1023:1024], ALU.mult, ALU.add,
                            [("psH", gv)] + tk + [("cw", 0)], tk)
                    act(sg, tmpG, AF.Silu, [("tmp", 0)], [("sg", 0)])
                    tt(actT[:, lc, :], tmpV, sg, ALU.mult, [("tmp", 1), ("sg", 0)], [("actT", lc)])
            last = (fb == len(FFB) - 1)
            if not last:
                for q in range(4):
                    off = (cbase * 4 + q * nch) * 512
                    si = load_slot(wdn_d[:, off:off + nch * 512], nch * 512)
                    w = slots[si][:, 0:nch * 512].rearrange("p (j c) -> p j c", c=512)
                    for t in range(8):
                        bank = 5 + dacc[0] % 3
                        dacc[0] += 1
                        for j in range(nch):
                            mm(ps(bank), actT[:, j, t * 128:(t + 1) * 128], w[:, j, :], j == 0, j == nch - 1,
                               [("actT", j), ("slot", si)], [("ps", bank)])
                        hv = h_sb[:, t, q * 512:(q + 1) * 512]
                        tt(hv, ps(bank), hv, ALU.add, [("ps", bank), ("h", t, q)], [("h", t, q)])
            else:
                S_.phase = "P7"
                ws = []
                for half in range(2):
                    off = (cbase * 4 + 2 * half * nch) * 512
                    si = load_slot(wdn_d[:, off:off + 2 * nch * 512], 2 * nch * 512)
                    ws.append((si, slots[si][:, 0:2 * nch * 512].rearrange("p (q j c) -> p q j c", q=2, c=512)))
                for t in range(8):
                    for q in range(4):
                        si, w = ws[q // 2]
                        bank = 5 + dacc[0] % 3
                        dacc[0] += 1
                        for j in range(nch):
                            mm(ps(bank), actT[:, j, t * 128:(t + 1) * 128], w[:, q % 2, j, :], j == 0, j == nch - 1,
                               [("actT", j), ("slot", si)], [("ps", bank)])
                        hv = h_sb[:, t, q * 512:(q + 1) * 512]
                        tt(hv, ps(bank), hv, ALU.add, [("ps", bank), ("h", t, q)], [("h", t, q)])
                    final_tile(t)
        S_.op("sp", None, [("y", t) for t in range(8)] + [k for k in S_.lastw if k[0] == "dbg"], [])

        for e in ENGS:
            cnt = 0
            for o_ in S_.per[e]:
                if o_.signal and not o_.stream:
                    cnt += 1
                    o_.sigval = cnt
        ngen = {e: (max([o_.sigval for o_ in S_.per[e]] + [0]) // SEM_GEN) + 1 for e in ENGS}
        eng_sems = {e: [es.enter_context(nc.semaphore(f"s_{e}_{g}")) for g in range(ngen[e])] for e in ENGS}
        stream_sems = {s: es.enter_context(nc.semaphore(f"d_{s}")) for s in S_.stream_cnt}
        block = es.enter_context(nc.Block())

        def emit(engname, eng):
            seen = {}
            for o_ in S_.per[engname]:
                for d in o_.deps:
                    if d.stream:
                        k = ("d", d.stream)
                        val = d.sval
                        sem = stream_sems[d.stream]
                    else:
                        g = (d.sigval - 1) // SEM_GEN
                        k = ("e", d.eng, g)
                        val = d.sigval - g * SEM_GEN
                        sem = eng_sems[d.eng][g]
                        if seen.get(("e", d.eng, g + 1), 0) > 0:
                            continue
                    if seen.get(k, 0) >= val:
                        continue
                    seen[k] = val
                    eng.wait_ge(sem, val)
                if o_.fn is None:
                    continue
                ins = o_.fn(eng)
                if o_.stream:
                    ins.then_inc(stream_sems[o_.stream], 16)
                elif o_.signal:
                    g = (o_.sigval - 1) // SEM_GEN
                    ins.then_inc(eng_sems[o_.eng][g], 1)

        @block.tensor
        def _(e):
            emit("pe", e)

        @block.scalar
        def _(e):
            emit("act", e)

        @block.vector
        def _(e):
            emit("dve", e)

        @block.gpsimd
        def _(e):
            emit("pool", e)

        @block.sync
        def _(e):
            emit("sp", e)

    nc._sched = S_
    return nc


def _const_tables(half):
    perm = np.arange(S) if half == 0 else (S - 1 - np.arange(S))
    inv_freq = (np.float32(10000.0) ** (-np.arange(32, dtype=np.float32) / np.float32(32))).astype(np.float32)
    row = (perm // 64).astype(np.float32)
    col = (perm % 64).astype(np.float32)
    ar = row[:, None] * inv_freq[None, :]
    ac = col[:, None] * inv_freq[None, :]
    cr, sr = np.cos(ar).astype(np.float32), np.sin(ar).astype(np.float32)
    cc, sc_ = np.cos(ac).astype(np.float32), np.sin(ac).astype(np.float32)
    rope = np.concatenate([cr, cr, cc, cc, -sr, sr, -sc_, sc_], axis=1).astype(np.float32)
    ps_ = perm.astype(np.int64)
    pk_ = perm[:NQ].astype(np.int64)
    m = (ps_[:, None] * pk_[None, :]) % S
    ang = 2.0 * np.pi * m.astype(np.float64) / S
    ct = (np.cos(ang) / np.sqrt(S))
    st = (np.sin(ang) / np.sqrt(S))
    dft = np.zeros((5, 128, 2, 16, 256), dtype=np.float32)
    for gi in range(5):
        k0 = gi * 256
        n = 256 if gi < 4 else 1
        for a, tab in enumerate((ct, st)):
            blk = tab[:, k0:k0 + n].reshape(16, 128, n)
            dft[gi, :, a, :, :n] = blk.transpose(1, 0, 2)
    dft = dft.reshape(5, 128, 2 * 16 * 256).astype(ml_dtypes.bfloat16)
    return rope, dft


def _channel_dft():
    c = np.arange(128)
    ang = 2.0 * np.pi * ((c[:, None] * c[None, :]) % 128) / 128.0
    cc = np.cos(ang) / np.sqrt(128.0)
    ns = -np.sin(ang) / np.sqrt(128.0)
    return np.concatenate([cc, ns], axis=1).astype(ml_dtypes.bfloat16)


def _tile_rows(w, ncols_blk):
    K, N = w.shape
    nk = K // 128
    nb = N // ncols_blk
    a = w.reshape(nk, 128, nb, ncols_blk).transpose(1, 2, 0, 3)
    return np.ascontiguousarray(a).reshape(128, nb * nk * ncols_blk)


def prep_inputs(x, norm1_g, w_in, q_norm_g, k_norm_g, w_fmix, attn_out_g, fourier_out_g, w_out, norm2_g,
                w_up, conv_w, conv_b, w_down, final_g):
    f32 = np.float32
    x = np.asarray(x, f32)
    w_in0 = np.asarray(w_in, f32)[0]
    w_out0 = np.asarray(w_out, f32)[0]
    w_up0 = np.asarray(w_up, f32)[0]
    w_dn0 = np.asarray(w_down, f32)[0]
    conv_w0 = np.asarray(conv_w, f32)[0]
    conv_b0 = np.asarray(conv_b, f32)[0]
    win_t = _tile_rows(w_in0, 512)
    wout_t = _tile_rows(w_out0, 512)
    g = w_up0[:, :DFF].reshape(16, 128, NCH, 128)
    v = w_up0[:, DFF:].reshape(16, 128, NCH, 128)
    gv = np.stack([g, v], axis=3)
    gv = gv.reshape(16, 128, 22, 2, 2, 128)
    wup_t = np.ascontiguousarray(gv.transpose(1, 2, 0, 3, 4, 5)).reshape(128, 22 * 8192)
    parts = []
    wd = w_dn0.reshape(NCH, 128, 4, 512)
    for (cbase, nch) in FFB:
        for q in range(4):
            parts.append(np.ascontiguousarray(wd[cbase:cbase + nch, :, q, :].transpose(1, 0, 2)).reshape(128, nch * 512))
    wdn_t = np.concatenate(parts, axis=1)
    wf_t = np.ascontiguousarray(np.asarray(w_fmix, f32)[0].transpose(1, 0, 2)).reshape(128, 1024)
    cb_t = np.ascontiguousarray(conv_b0.reshape(88, 128).T)
    ccs = _channel_dft()
    ident = np.eye(128, dtype=f32).astype(ml_dtypes.bfloat16)
    tabs = [_const_tables(0), _const_tables(1)]
    common = {
        "ccs": ccs, "ident": ident,
        "g1": np.asarray(norm1_g, f32)[0], "g2": np.asarray(norm2_g, f32)[0], "gF": np.asarray(final_g, f32),
        "ga": np.asarray(attn_out_g, f32)[0], "gf": np.asarray(fourier_out_g, f32)[0],
        "gq": np.asarray(q_norm_g, f32)[0], "gk": np.asarray(k_norm_g, f32)[0],
        "cb": cb_t, "wf": wf_t, "win": win_t, "wout": wout_t, "wup": wup_t, "wdn": wdn_t,
    }
    in_maps = []
    for c in range(8):
        b, half = c // 2, c % 2
        xl = x[b] if half == 0 else x[b][::-1]
        cwc = conv_w0 if half == 0 else conv_w0[::-1]
        cw_t = np.ascontiguousarray(cwc.reshape(3, 88, 128).transpose(2, 1, 0)).reshape(128, 88 * 3)
        m = dict(common)
        m["x_loc"] = np.ascontiguousarray(xl)
        m["rope"] = tabs[half][0]
        m["dft"] = tabs[half][1]
        m["cw"] = cw_t
        in_maps.append(m)
    return in_maps


_NC_CACHE = {}


def kernel(**inputs):
    in_maps = prep_inputs(**inputs)
    if "nc" not in _NC_CACHE:
        _NC_CACHE["nc"] = build_program()
    nc = _NC_CACHE["nc"]
    res = run_bass_kernel_spmd(nc, in_maps, core_ids=list(range(8)))
    out = np.empty((4, S, D), dtype=np.float32)
    for c in range(8):
        b, half = c // 2, c % 2
        y = np.asarray(res.results[c]["y"], dtype=np.float32)
        if half == 0:
            out[b, :T] = y
        else:
            out[b, T:] = y[::-1]
    return out
```

```python
from contextlib import ExitStack

import numpy as np
import ml_dtypes

import concourse.bass as bass
import concourse.mybir as mybir
from concourse.bass_utils import run_bass_kernel_spmd

F32 = mybir.dt.float32
BF16 = mybir.dt.bfloat16
AF = mybir.ActivationFunctionType
ALU = mybir.AluOpType
AX = mybir.AxisListType

D = 2048
S = 2048
T = 1024
NQ = T + 1
HD = 128
NH = 8
NKV = 2
DFF = 5632
NCH = DFF // 128
EPS = 1e-6
FFB = [(0, 12), (12, 12), (24, 12), (36, 8)]
SEM_GEN = 3000
TAP_UTE = False

ENGS = ["pe", "act", "dve", "pool", "sp"]


class Op:
    __slots__ = ("eng", "fn", "stream", "sval", "signal", "sigval", "gidx", "deps", "phase")


class Sched:
    def __init__(self):
        self.ops = []
        self.per = {e: [] for e in ENGS}
        self.lastw = {}
        self.readers = {}
        self.touch = {}
        self.inherit = {}
        self.stream_cnt = {}
        self.phase = ""

    @staticmethod
    def _k(o):
        return ("d", o.stream) if o.stream else ("e", o.eng)

    def op(self, eng, fn, reads=(), writes=(), stream=None):
        o = Op()
        o.eng = eng
        o.fn = fn
        o.stream = stream
        o.signal = False
        o.sigval = 0
        o.sval = 0
        o.gidx = len(self.ops)
        o.phase = self.phase
        if stream:
            self.stream_cnt[stream] = self.stream_cnt.get(stream, 0) + 1
            o.sval = 16 * self.stream_cnt[stream]
        deps = {}

        def add(d):
            k = self._k(d)
            if k not in deps or deps[k].gidx < d.gidx:
                deps[k] = d

        reads = list(reads)
        writes = list(writes)
        for key in reads + writes:
            if key not in self.lastw and key not in self.readers:
                for d in self.inherit.get(key[0], {}).values():
                    add(d)
        for key in reads:
            w = self.lastw.get(key)
            if w is not None:
                add(w)
        for key in writes:
            w = self.lastw.get(key)
            if w is not None and not (key[0] == "junkq" and w.eng == eng and not w.stream):
                add(w)
            for r in self.readers.get(key, {}).values():
                add(r)
        for key in reads:
            self.readers.setdefault(key, {})[self._k(o)] = o
        for key in writes:
            self.lastw[key] = o
            self.readers[key] = {}
        for key in reads + writes:
            self.touch.setdefault(key[0], {})[self._k(o)] = o
        o.deps = []
        for d in deps.values():
            if (not d.stream) and d.eng == "pe" and eng == "pe" and not stream:
                continue
            o.deps.append(d)
            if not d.stream:
                d.signal = True
        self.ops.append(o)
        self.per[eng].append(o)
        return o

    def alias(self, new_name, old_names):
        m = self.inherit.setdefault(new_name, {})
        for n in old_names:
            for k, d in self.touch.get(n, {}).items():
                if k not in m or m[k].gidx < d.gidx:
                    m[k] = d
            for k, d in self.inherit.get(n, {}).items():
                if k not in m or m[k].gidx < d.gidx:
                    m[k] = d


class Arena:
    def __init__(self, tensor, sched, nbytes):
        self.t = tensor
        self.s = sched
        self.nbytes = nbytes
        self.live = []

    def alloc(self, name, off, nbytes, dtype, shape_str=None, **dims):
        assert off % 4 == 0 and off + nbytes <= self.nbytes, (name, off, nbytes)
        b0, b1 = off, off + nbytes
        old = sorted(set(n for (n, a0, a1) in self.live if a0 < b1 and b0 < a1))
        self.live.append((name, b0, b1))
        if old:
            self.s.alias(name, old)
        w0 = off // 4
        w1 = (off + nbytes + 3) // 4
        ap = self.t[:, w0:w1]
        if dtype != F32:
            ap = ap.bitcast(dtype)
        if shape_str:
            ap = ap.rearrange(shape_str, **dims)
        return ap


def build_program(debug=False):
    nc = bass.Bass("TRN2", target_bir_lowering=False)
    S_ = Sched()

    def dram(name, shape, dt=F32, kind="ExternalInput"):
        return nc.dram_tensor(name, list(shape), dt, kind=kind).ap()

    x_d = dram("x_loc", [S, D])
    rope_d = dram("rope", [S, 256])
    dft_d = dram("dft", [5, 128, 2 * 16 * 256], BF16)
    ccs_d = dram("ccs", [128, 256], BF16)
    ident_d = dram("ident", [128, 128], BF16)
    g1_d = dram("g1", [D])
    g2_d = dram("g2", [D])
    gF_d = dram("gF", [D])
    ga_d = dram("ga", [1024])
    gf_d = dram("gf", [1024])
    gq_d = dram("gq", [128])
    gk_d = dram("gk", [128])
    cw_d = dram("cw", [128, 88 * 3])
    cb_d = dram("cb", [128, 88])
    wf_d = dram("wf", [128, 1024])
    win_d = dram("win", [128, 5 * 8192])
    wout_d = dram("wout", [128, 4 * 8192])
    wup_d = dram("wup", [128, 22 * 8192])
    wdn_d = dram("wdn", [128, 44 * 2048])
    y_d = dram("y", [T, D], F32, kind="ExternalOutput")

    ARENA_BYTES = 212480
    es = ExitStack()
    with es:
        arena_t = es.enter_context(nc.sbuf_tensor("arena", [128, ARENA_BYTES // 4], F32))
        psum_t = es.enter_context(nc.psum_tensor("psum", [128, 4096], F32))
        A = Arena(arena_t, S_, ARENA_BYTES)

        R_SLOT = 0
        R_F = 49152
        R_Q = 81920
        R_MIX = 114944
        R_A = 147840
        R_C = 207744

        slots = [A.alloc("slot", R_SLOT + i * 16384, 16384, BF16) for i in range(3)]
        ident = A.alloc("ident", R_C, 256, BF16)
        cw = A.alloc("cw", R_C + 256, 1056, F32)
        cb = A.alloc("cb", R_C + 1312, 352, F32)
        stat = A.alloc("stat", R_C + 1664, 512, F32)
        epsc = A.alloc("epsc", R_C + 2176, 4, F32)
        junkq = A.alloc("junkq", R_C + 2184, 2048, mybir.dt.float8e4)
        JK = [("junkq", 0)]

        def ps(bank, n=512, off=0):
            return psum_t[:, bank * 512 + off: bank * 512 + off + n]

        def ps_bf(bank, nbanks=1):
            return psum_t[:, bank * 512:(bank + nbanks) * 512].bitcast(BF16)

        def mm(out, lhsT, rhs, start, stop, reads, writes):
            return S_.op("pe", lambda e: e.matmul(out, lhsT, rhs, start=start, stop=stop), reads, writes)

        def tr(out, in_, idn, reads, writes):
            return S_.op("pe", lambda e: e.transpose(out, in_, idn), reads, writes)

        def act(out, in_, func, reads, writes, bias=None, scale=None, accum=None, sat=None):
            kw = {}
            if sat is not None:
                kw["saturate"] = sat
            if bias is not None:
                kw["bias"] = bias
            if scale is not None:
                kw["scale"] = scale
            if accum is not None:
                kw["accum_out"] = accum
            return S_.op("act", lambda e: e.activation(out, in_, func, **kw), reads, writes)

        def dve(fn, reads, writes):
            return S_.op("dve", fn, reads, writes)

        def tt(out, in0, in1, op, reads, writes):
            return dve(lambda e: e.tensor_tensor(out, in0, in1, op), reads, writes)

        def stt(out, in0, scalar, in1, op0, op1, reads, writes):
            return dve(lambda e: e.scalar_tensor_tensor(out, in0, scalar, in1, op0, op1), reads, writes)

        def tsc(out, in0, s1, s2, op0, op1, reads, writes):
            if op1 is None:
                return dve(lambda e: e.tensor_scalar(out, in0, s1, s2, op0), reads, writes)
            return dve(lambda e: e.tensor_scalar(out, in0, s1, s2, op0, op1), reads, writes)

        def red(out, in_, reads, writes):
            return dve(lambda e: e.tensor_reduce(out, in_, AX.X, ALU.add), reads, writes)

        def recip(out, in_, reads, writes):
            return dve(lambda e: e.reciprocal(out, in_), reads, writes)

        def copy_any(which, out, in_, reads, writes):
            if which % 2 == 0:
                return act(out, in_, AF.Copy, reads, writes)
            return dve(lambda e: e.tensor_copy(out, in_), reads, writes)

        def dma(eng, out, in_, reads, writes, stream, **kw):
            return S_.op(eng, lambda e: e.dma_start(out=out, in_=in_, **kw), reads, writes, stream=stream)

        stat_ctr = [0]

        def stat_slot(n=1):
            i = stat_ctr[0]
            if i % 128 + n > 128:
                i += 128 - i % 128
            stat_ctr[0] = i + n
            c = i % 128
            return stat[:, c:c + n], ("stat", c // 16, (c + n - 1) // 16)

        def stat_keys(k):
            return [("stat", j) for j in range(k[1], k[2] + 1)]

        def rstd_from_ss(ss, sskeys, rows, ncols, inv_n):
            r, rk = stat_slot(ncols)
            rkeys = stat_keys(rk)
            act(r[:rows], ss[:rows], AF.Sqrt, sskeys + [("epsc", 0)], rkeys, bias=epsc[:rows], scale=inv_n)
            recip(r[:rows], r[:rows], rkeys, rkeys)
            return r, rkeys

        slot_ctr = [0]

        def load_slot(src_ap, ncols, after=()):
            i = slot_ctr[0] % 3
            slot_ctr[0] += 1
            dma("pool", slots[i][:, 0:ncols], src_ap, list(after), [("slot", i)], f"slot{i}",
                max_dma_last_dim=8192)
            return i

        def tap(name, ap, names):
            if not debug:
                return
            shape = [int(s) for s in ap.shape]
            dd = nc.dram_tensor("dbg_" + name, shape, ap.dtype, kind="ExternalOutput").ap()
            reads = [k for k in S_.lastw if k[0] in names]
            dma("sp", dd, ap, reads, [("dbg", name)], "dbg_" + name)

        dma("sp", ident, ident_d, [], [("ident", 0)], "c_ident")
        dma("sp", cw, cw_d, [], [("cw", 0)], "c_cw")
        dma("sp", cb, cb_d, [], [("cb", 0)], "c_cb")
        KI = [("ident", 0)]

        f_tm = A.alloc("f_tm", R_F, 32768, BF16, "p (s c) -> p s c", c=1024)
        qT = A.alloc("qT", R_Q, 16416, BF16, "p (h t) -> p h t", t=1026)
        kT = A.alloc("kT", R_Q + 16416, 8192, BF16, "p (h t) -> p h t", t=2048)
        Vaug = A.alloc("V", R_Q + 24608, 8320, BF16, "p (s h c) -> p s h c", h=2, c=130)
        uT = A.alloc("uT", R_A, 32768, BF16, "p (k t) -> p k t", t=1024)
        o = R_A + 32768
        sq = A.alloc("sq", o, 2048, F32); o += 2048
        qn = A.alloc("qn", o, 2048, F32); o += 2048
        t1 = A.alloc("t1", o, 2048, F32); o += 2048
        t2 = A.alloc("t2", o, 2048, F32); o += 2048
        qr = [A.alloc("qr", o + i * 1024, 1024, BF16) for i in range(3)]; o += 3072
        ropet = [A.alloc("ropet", o + i * 1024, 1024, F32) for i in range(2)]; o += 2048
        gq_bc = A.alloc("gq_bc", o, 512, F32); o += 512
        gk_bc = A.alloc("gk_bc", o, 512, F32); o += 512
        xs = [A.alloc("xs", R_MIX + i * 8192, 8192, F32) for i in range(2)] + [A.alloc("xs", R_A + 49152, 8192, F32)]
        g1_bc = A.alloc("g1_bc", R_MIX + 16384, 8192, F32)
        u_tm = [A.alloc("u_tm", R_MIX + 24576 + i * 4096, 4096, BF16) for i in range(2)]

        dve(lambda e: e.memset(epsc, EPS), [], [("epsc", 0)])
        dma("sp", g1_bc, g1_d.partition_broadcast(128), [], [("g1_bc", 0)], "c_g1")
        dma("sp", gq_bc, gq_d.partition_broadcast(128), [], [("gq_bc", 0)], "c_gq")
        dma("sp", gk_bc, gk_d.partition_broadcast(128), [], [("gk_bc", 0)], "c_gk")
        dve(lambda e: e.memset(Vaug[:, :, :, 128:129], 1.0), [], [("V", 99)])

        cpy = [0]
        rope_ctr = [0]
        psacc_ctr = [0]
        pstr_ctr = [0]
        qr_ctr = [0]

        def norm_tile(src, srckeys, rows, gbc, gkey, dst, dstkeys):
            ss, sk = stat_slot(1)
            sskeys = stat_keys(sk)
            act(junkq[:rows], src[:rows], AF.Square, srckeys, JK + sskeys, accum=ss[:rows], sat=False)
            r, rkeys = rstd_from_ss(ss, sskeys, rows, 1, 1.0 / D)
            stt(dst[:rows], src[:rows], r[:rows, 0:1], gbc[:rows], ALU.mult, ALU.mult,
                srckeys + rkeys + [gkey], dstkeys)

        def transposes16(src, srckeys, rows, dstT, dstkeys_fn, col0, bankpair, which):
            pb = ps_bf(bankpair, 2)
            pk = [("ps", bankpair), ("ps", bankpair + 1)]
            for j in range(16):
                tr(pb[:, j * 128: j * 128 + rows], src[:rows, j * 128:(j + 1) * 128], ident[:rows, :rows],
                   srckeys + KI, pk)
            copy_any(which, dstT[:, :, col0:col0 + rows],
                     pb.rearrange("p (k t) -> p k t", t=128)[:, :, 0:rows], pk, dstkeys_fn)

        def qk_post(psrc, pkeys, rows, nheads, gbc, gbkey, rtile, rkey, dstT, dst_h0, col0, dstkeys):
            n = nheads * 128
            act(sq[:rows, :n], psrc[:rows, :n], AF.Square, pkeys, [("sq", 0)])
            ss, sk = stat_slot(nheads)
            sskeys = stat_keys(sk)
            red(ss[:rows], sq[:rows, :n].rearrange("p (h d) -> p h d", d=128), [("sq", 0)], sskeys)
            r, rkeys = rstd_from_ss(ss, sskeys, rows, nheads, 1.0 / HD)
            for h in range(nheads):
                stt(qn[:rows, h * 128:(h + 1) * 128], psrc[:rows, h * 128:(h + 1) * 128],
                    r[:rows, h:h + 1], gbc[:rows], ALU.mult, ALU.mult,
                    pkeys + rkeys + [gbkey], [("qn", 0)])
            qn3 = qn[:rows, :n].rearrange("p (h d) -> p h d", d=128)
            cosb = rtile[:rows, 0:128].unsqueeze(1).to_broadcast([rows, nheads, 128])
            tt(t1[:rows, :n].rearrange("p (h d) -> p h d", d=128), qn3, cosb, ALU.mult,
               [("qn", 0), rkey], [("t1", 0)])
            qn5 = qn[:rows, :n].rearrange("p (h a b c) -> p h a b c", a=2, b=2, c=32)
            t25 = t2[:rows, :n].rearrange("p (h a b c) -> p h a b c", a=2, b=2, c=32)
            sin5 = rtile[:rows, 128:256].rearrange("p (a b c) -> p a b c", a=2, b=2, c=32)
            for blk in range(2):
                sb = sin5[:, :, blk, :].unsqueeze(1).to_broadcast([rows, nheads, 2, 32])
                tt(t25[:, :, :, blk, :], qn5[:, :, :, 1 - blk, :], sb, ALU.mult,
                   [("qn", 0), rkey], [("t2", blk)])
            qi = qr_ctr[0] % 3
            qr_ctr[0] += 1
            qrb = qr[qi]
            tt(qrb[:rows, :n], t1[:rows, :n], t2[:rows, :n], ALU.add,
               [("t1", 0), ("t2", 0), ("t2", 1)], [("qr", qi)])

            def tail():
                bank = pstr_ctr[0] % 3
                pstr_ctr[0] += 1
                pb = ps_bf(bank, 1)
                for h in range(nheads):
                    tr(pb[:, h * 128:h * 128 + rows], qrb[:rows, h * 128:(h + 1) * 128], ident[:rows, :rows],
                       [("qr", qi)] + KI, [("ps", bank)])
                copy_any(0, dstT[:, dst_h0:dst_h0 + nheads, col0:col0 + rows],
                         pb[:, 0:n].rearrange("p (h t) -> p h t", t=128)[:, :, 0:rows], [("ps", bank)], dstkeys)
            return tail

        def load_rope(gt):
            i = rope_ctr[0] % 2
            rope_ctr[0] += 1
            dma("sp", ropet[i], rope_d[gt * 128:(gt + 1) * 128, :], [], [("ropet", i)], f"ropet{i}")
            return ropet[i], ("ropet", i)

        pend = []

        def flush(keep):
            while len(pend) > keep:
                pend.pop(0)()

        for pa in range(2):
            if pa == 1 and TAP_UTE:
                tap("uTE", uT, ["uT"])
            S_.phase = f"P1{'AB'[pa]}"
            for t in range(8):
                gt = pa * 8 + t
                b = gt % 2
                xb = gt % 3
                dma("sp", xs[xb], x_d[gt * 128:(gt + 1) * 128, :], [], [("xs", xb)], f"xs{xb}")
                norm_tile(xs[xb], [("xs", xb)], 128, g1_bc, ("g1_bc", 0), u_tm[b], [("u_tm", b)])
                pend.append(lambda t=t, b=b: transposes16(u_tm[b], [("u_tm", b)], 128, uT, [("uT", t)], t * 128,
                                                          2 * (t % 2), t))
                flush(1)
            flush(0)
            S_.phase = f"P2{'AB'[pa]}"
            blocks = [0, 1, 2, 3, 4] if pa == 0 else [2, 3, 4, 0, 1]
            for blk in blocks:
                after = []
                if pa == 0 and blk == 1:
                    after = [("uT", 3)]
                if pa == 0 and blk == 2:
                    after = [("uT", 6)]
                si = load_slot(win_d[:, blk * 8192:(blk + 1) * 8192], 8192, after)
                w = slots[si].rearrange("p (k c) -> p k c", c=512)
                halo_only = (pa == 1 and blk < 2)
                tiles = [0] if halo_only else list(range(8))
                for t in tiles:
                    gt = pa * 8 + t
                    rows = 1 if halo_only else 128
                    bank = 4 + psacc_ctr[0] % 4
                    psacc_ctr[0] += 1
                    pk = [("ps", bank)]
                    for k in range(16):
                        mm(ps(bank)[:rows, :], uT[:, k, t * 128:t * 128 + rows], w[:, k, :], k == 0, k == 15,
                           [("uT", t), ("slot", si)], pk)
                    if blk < 2:
                        rt, rk = load_rope(gt)
                        col0 = 1024 if halo_only else t * 128
                        pend.append(qk_post(ps(bank), pk, rows, 4, gq_bc, ("gq_bc", 0), rt, rk, qT, 4 * blk, col0,
                                            [("qT", blk, gt)]))
                    elif blk == 2:
                        rt, rk = load_rope(gt)
                        act(Vaug[:, gt, :, 0:128], ps(bank)[:, 256:512].rearrange("p (h d) -> p h d", d=128),
                            AF.Copy, pk, [("V", gt)])
                        pend.append(qk_post(ps(bank), pk, 128, 2, gk_bc, ("gk_bc", 0), rt, rk, kT, 0, gt * 128,
                                            [("kT", gt)]))
                    else:
                        fb = blk - 3
                        act(f_tm[:, gt, fb * 512:(fb + 1) * 512], ps(bank), AF.Copy, pk, [("f_tm", gt, fb)])
                    flush(2)
            flush(0)

        tap("qT", qT, ["qT"])
        tap("kT", kT, ["kT"])
        tap("V", Vaug, ["V"])
        tap("f", f_tm, ["f_tm"])
        S_.phase = "P3"
        mixT = A.alloc("mixT", R_MIX, 32832, BF16, "p (k t) -> p k t", t=1026)
        o = R_A
        PT = [A.alloc("PT", o + i * 16384, 16384, BF16, "p (s q) -> p s q", q=512) for i in range(2)]; o += 32768
        O_sb = [A.alloc("O_sb", o + i * 4160, 4160, F32, "p (h c) -> p h c", c=130) for i in range(4)]; o += 16640
        sqa = A.alloc("sqa", o, 2048, BF16); o += 2048
        a_tm = A.alloc("a_tm", o, 2048, BF16); o += 2048
        ga_bc = A.alloc("ga_bc", o, 4096, F32); o += 4096
        dma("sp", ga_bc, ga_d.partition_broadcast(128), [], [("ga_bc", 0)], "c_ga")

        scale = 1.0 / float(np.sqrt(HD))
        sbank = [0]
        obank = [0]
        ptc = [0]

        def attn_norm(tl, rows, col0):
            okeys = [("O_sb", tl, h) for h in range(NH)]
            rl, rlk = stat_slot(NH)
            rlkeys = stat_keys(rlk)
            recip(rl[:rows].unsqueeze(2), O_sb[tl][:rows, :, 128:129], okeys, rlkeys)
            act(sqa[:rows, :].rearrange("p (h d) -> p h d", d=128), O_sb[tl][:rows, :, 0:128], AF.Square,
                okeys, [("sqa", 0)])
            ssh, sk = stat_slot(NH)
            sshk = stat_keys(sk)
            red(ssh[:rows], sqa[:rows, :].rearrange("p (h d) -> p h d", d=128), [("sqa", 0)], sshk)
            tt(ssh[:rows], ssh[:rows], rl[:rows], ALU.mult, sshk + rlkeys, sshk)
            tt(ssh[:rows], ssh[:rows], rl[:rows], ALU.mult, sshk + rlkeys, sshk)
            ss1, sk1 = stat_slot(1)
            ss1k = stat_keys(sk1)
            red(ss1[:rows], ssh[:rows], sshk, ss1k)
            r, rkeys = rstd_from_ss(ss1, ss1k, rows, 1, 1.0 / 1024.0)
            fac, fk = stat_slot(NH)
            fkeys = stat_keys(fk)
            tsc(fac[:rows], rl[:rows], r[:rows, 0:1], None, ALU.mult, None, rlkeys + rkeys, fkeys)
            for h in range(NH):
                stt(a_tm[:rows, h * 128:(h + 1) * 128], O_sb[tl][:rows, h, 0:128], fac[:rows, h:h + 1],
                    ga_bc[:rows, h * 128:(h + 1) * 128], ALU.mult, ALU.mult,
                    okeys + fkeys + [("ga_bc", 0)], [("a_tm", 0)])
            pb = ps_bf(7, 1)
            for h in range(NH):
                tr(pb[:, h * 128:h * 128 + rows], a_tm[:rows, h * 128:(h + 1) * 128], ident[:rows, :rows],
                   [("a_tm", 0)] + KI, [("ps", 7)])
            copy_any(1, mixT[:, 0:8, col0:col0 + rows],
                     pb.rearrange("p (h t) -> p h t", t=128)[:, :, 0:rows], [("ps", 7)], [("mixT", 0, col0 // 128)])

        prev = None
        pend_norm = []
        for (q0, n) in [(0, 512), (512, 512), (None, 0)]:
            heads = range(NH) if q0 is not None else [None]
            for h in heads:
                if h is not None:
                    kv = h // 4
                    pi = ptc[0] % 2
                    ptc[0] += 1
                for sc in range(16):
                    if h is not None:
                        bank = sbank[0] % 3
                        sbank[0] += 1
                        mm(ps(bank)[:, :n], kT[:, kv, sc * 128:(sc + 1) * 128], qT[:, h, q0:q0 + n], True, True,
                           [("kT", sc)] + [("qT", h // 4, q0 // 128 + j) for j in range(4)], [("ps", bank)])
                        act(PT[pi][:, sc, :n], ps(bank)[:, :n], AF.Exp, [("ps", bank)], [("PT", pi, sc)],
                            scale=scale)
                    if prev is not None:
                        ppi, ph = prev
                        for tl in range(4):
                            mm(ps(3 + tl)[:, 0:129], PT[ppi][:, sc, tl * 128:(tl + 1) * 128],
                               Vaug[:, sc, ph // 4, 0:129], sc == 0, sc == 15,
                               [("PT", ppi, sc), ("V", sc), ("V", 99)], [("ps", 3 + tl)])
                if prev is not None:
                    ppi, ph = prev
                    for tl in range(4):
                        copy_any(1, O_sb[tl][:, ph, 0:129], ps(3 + tl)[:, 0:129], [("ps", 3 + tl)],
                                 [("O_sb", tl, ph)])
                    if ph == NH - 1:
                        while pend_norm:
                            pend_norm.pop(0)()
                prev = (pi, h) if h is not None else None
                if h == NH - 1:
                    for tl in range(4):
                        pend_norm.append(lambda tl=tl, q0=q0: attn_norm(tl, 128, q0 + tl * 128))
        obank[0] = 0
        pi = ptc[0] % 2
        ptc[0] += 1
        for h in range(NH):
            kv = h // 4
            bank = sbank[0] % 3
            sbank[0] += 1
            for sc in range(16):
                mm(ps(bank)[:, sc:sc + 1], kT[:, kv, sc * 128:(sc + 1) * 128], qT[:, h, 1024:1025], True, True,
                   [("kT", sc), ("qT", h // 4, 8)], [("ps", bank)])
            act(PT[pi][:, h, 0:16], ps(bank)[:, 0:16], AF.Exp, [("ps", bank)], [("PT", pi, h)], scale=scale)
        while pend_norm:
            pend_norm.pop(0)()
        for h in range(NH):
            kv = h // 4
            bank = 3 + obank[0] % 4
            obank[0] += 1
            for sc in range(16):
                mm(ps(bank)[:1, 0:129], PT[pi][:, h, sc:sc + 1], Vaug[:, sc, kv, 0:129], sc == 0, sc == 15,
                   [("PT", pi, h), ("V", sc), ("V", 99)], [("ps", bank)])
            copy_any(1, O_sb[0][:1, h, 0:129], ps(bank)[:1, 0:129], [("ps", bank)], [("O_sb", 0, h)])
        attn_norm(0, 1, 1024)

        S_.phase = "P4"
        ZT = A.alloc("ZT", R_Q, 32832, BF16, "p (a g t) -> p a g t", a=2, g=8)
        o = R_A
        CS = [A.alloc("CS", o + i * 16384, 16384, BF16, "p (a s k) -> p a s k", a=2, s=16) for i in range(2)]
        o += 32768
        gf_bc = A.alloc("gf_bc", o, 4096, F32); o += 4096
        f_n2 = [A.alloc("f_n", o + i * 2048, 2048, BF16) for i in range(2)]; o += 4096
        AB = A.alloc("AB", o, 4096, BF16, "p (g a d) -> p g a d", a=2, d=128); o += 4096
        ccs = A.alloc("ccs", o, 512, BF16, "p (a c) -> p a c", a=2); o += 512
        wfb = A.alloc("wfb", o, 2048, BF16); o += 2048
        dma("sp", gf_bc, gf_d.partition_broadcast(128), [], [("gf_bc", 0)], "c_gf")
        dma("sp", ccs, ccs_d.rearrange("p (a c) -> p a c", a=2), [], [("ccs", 0)], "c_ccs")
        dma("pool", wfb, wf_d, [], [("wfb", 0)], "c_wf")
        for g in range(8):
            for a in range(2):
                mm(ps(0)[:, a * 128:(a + 1) * 128], ccs[:, a, :], wfb[:, g * 128:(g + 1) * 128], True, True,
                   [("ccs", 0), ("wfb", 0)], [("ps", 0)])
            cpy[0] += 1
            copy_any(cpy[0], AB[:, g, :, :], ps(0)[:, 0:256].rearrange("p (a d) -> p a d", d=128), [("ps", 0)],
                     [("AB", g)])
        zb = [0]
        qgroups = [(i * 256, 256) for i in range(4)] + [(1024, 1)]
        pend_out = []
        pend_tr = []

        def fourier_out(gi, tl, rows, col0):
            b0 = 4 + 2 * tl
            pk = [("ps", b0), ("ps", b0 + 1)]
            pout = psum_t[:, b0 * 512:(b0 + 2) * 512]
            for g in range(8):
                for a in range(2):
                    mm(pout[:rows, g * 128:(g + 1) * 128], ZT[:, a, g, col0:col0 + rows], AB[:, g, a, :],
                       a == 0, a == 1, [("ZT", a, g, gi), ("AB", g)], [("ps", b0 + g // 4)])
            ss, sk = stat_slot(1)
            sskeys = stat_keys(sk)
            act(junkq[:rows, 0:1024], pout[:rows], AF.Square, pk, JK + sskeys, accum=ss[:rows], sat=False)
            r, rkeys = rstd_from_ss(ss, sskeys, rows, 1, 1.0 / 1024.0)
            fn = f_n2[tl]
            stt(fn[:rows], pout[:rows], r[:rows, 0:1], gf_bc[:rows], ALU.mult, ALU.mult,
                pk + rkeys + [("gf_bc", 0)], [("f_n", tl)])

            def tail():
                pb = ps_bf(b0, 1)
                for h in range(8):
                    tr(pb[:, h * 128:h * 128 + rows], fn[:rows, h * 128:(h + 1) * 128], ident[:rows, :rows],
                       [("f_n", tl)] + KI, [("ps", b0)])
                cpy[0] += 1
                copy_any(cpy[0], mixT[:, 8:16, col0:col0 + rows],
                         pb.rearrange("p (h t) -> p h t", t=128)[:, :, 0:rows], [("ps", b0)],
                         [("mixT", 1, col0 // 128)])
            pend_tr.append(tail)

        for gi, (k0, n) in enumerate(qgroups):
            ntile = 2 if n == 256 else 1
            rows = 128 if n == 256 else 1
            ci = gi % 2
            dma("sp", CS[ci], dft_d[gi].rearrange("p (a s k) -> p a s k", a=2, s=16), [], [("CS", ci)], f"CS{ci}")
            for g in range(8):
                bz = 2 * (zb[0] % 2)
                zb[0] += 1
                for sc in range(16):
                    for a in range(2):
                        mm(ps(bz + a)[:, :n], f_tm[:, sc, g * 128:(g + 1) * 128], CS[ci][:, a, sc, :n],
                           sc == 0, sc == 15, [("f_tm", sc, g // 4), ("CS", ci)], [("ps", bz + a)])
                for a in range(2):
                    cpy[0] += 1
                    copy_any(cpy[0], ZT[:, a, g, k0:k0 + n], ps(bz + a)[:, :n], [("ps", bz + a)],
                             [("ZT", a, g, gi)])
                if g == 1:
                    while pend_out:
                        pend_out.pop(0)()
                if g == 4:
                    while pend_tr:
                        pend_tr.pop(0)()
            for tl in range(ntile):
                pend_out.append(lambda gi=gi, tl=tl, rows=rows, k0=k0: fourier_out(gi, tl, rows, k0 + tl * 128))
        while pend_out:
            pend_out.pop(0)()
        while pend_tr:
            pend_tr.pop(0)()

        tap("mixT", mixT, ["mixT"])
        S_.phase = "P5"
        h_sb = A.alloc("h", R_F, 65536, F32, "p (t d) -> p t d", d=2048)
        o = R_A
        u2T = A.alloc("u2T", o, 32832, BF16, "p (k t) -> p k t", t=1026); o += 32832
        g2_bc = A.alloc("g2_bc", o, 8192, F32); o += 8192
        u2_tm = [A.alloc("u2_tm", o + i * 4096, 4096, BF16) for i in range(2)]; o += 8192
        h_halo = A.alloc("h", o, 8192, F32); o += 8192
        dma("sp", g2_bc, g2_d.partition_broadcast(128), [], [("g2_bc", 0)], "c_g2")
        for t in range(8):
            dma("sp", h_sb[:, t, :], x_d[t * 128:(t + 1) * 128, :], [], [("h", t, q) for q in range(4)], f"hx{t}")
        dma("sp", h_halo[0:1, :], x_d[1024:1025, :], [], [("h", 8, q) for q in range(4)], "hx8")

        def htile(t):
            return h_sb[:, t, :] if t < 8 else h_halo

        wacc = [0]
        pend = []

        def do_norm2(t):
            rows = 128 if t < 8 else 1
            hk = [("h", t, q) for q in range(4)]
            b = t % 2
            norm_tile(htile(t), hk, rows, g2_bc, ("g2_bc", 0), u2_tm[b], [("u2_tm", b)])
            pend.append(lambda: transposes16(u2_tm[b], [("u2_tm", b)], rows, u2T, [("u2T", t)], t * 128,
                                             4 + 2 * (t % 2), t))

        for cbk in range(4):
            si = load_slot(wout_d[:, cbk * 8192:(cbk + 1) * 8192], 8192)
            w = slots[si].rearrange("p (k c) -> p k c", c=512)
            for t in range(9):
                rows = 128 if t < 8 else 1
                col0 = t * 128
                bank = wacc[0] % 4
                wacc[0] += 1
                for fc in range(16):
                    mm(ps(bank)[:rows, :], mixT[:, fc, col0:col0 + rows], w[:, fc, :], fc == 0, fc == 15,
                       [("mixT", fc // 8, t), ("slot", si)], [("ps", bank)])
                hv = htile(t)[:rows, cbk * 512:(cbk + 1) * 512]
                tt(hv, ps(bank)[:rows, :], hv, ALU.add, [("ps", bank), ("h", t, cbk)], [("h", t, cbk)])
                if cbk == 3:
                    if t >= 1:
                        do_norm2(t - 1)
                    while len(pend) > 1:
                        pend.pop(0)()
        do_norm2(8)
        while pend:
            pend.pop(0)()

        tap("h1", h_sb, ["h"])
        tap("u2T", u2T, ["u2T"])
        S_.phase = "P6"
        actT = A.alloc("actT", R_MIX, 24576, BF16, "p (c t) -> p c t", t=1024)
        o = R_A + 32832
        tmpG = A.alloc("tmp", o, 4096, F32); o += 4096
        tmpV = A.alloc("tmp", o, 4096, F32); o += 4096
        sg = A.alloc("sg", o, 4096, F32); o += 4096
        gF_bc = A.alloc("gF_bc", o, 8192, F32); o += 8192
        dma("sp", gF_bc, gF_d.partition_broadcast(128), [], [("gF_bc", 0)], "c_gF")
        u2keys = [("u2T", t) for t in range(9)]
        dacc = [0]

        def final_tile(t):
            hk = [("h", t, q) for q in range(4)]
            ss, sk = stat_slot(1)
            sskeys = stat_keys(sk)
            act(junkq, h_sb[:, t, :], AF.Square, hk, JK + sskeys, accum=ss, sat=False)
            r, rkeys = rstd_from_ss(ss, sskeys, 128, 1, 1.0 / D)
            stt(h_sb[:, t, :], h_sb[:, t, :], r[:, 0:1], gF_bc, ALU.mult, ALU.mult,
                hk + rkeys + [("gF_bc", 0)], hk)
            dma("sp", y_d[t * 128:(t + 1) * 128, :], h_sb[:, t, :], hk, [("y", t)], f"y{t % 2}")

        for fb, (cbase, nch) in enumerate(FFB):
            for pbk in range(nch // 2):
                blk = cbase // 2 + pbk
                si = load_slot(wup_d[:, blk * 8192:(blk + 1) * 8192], 8192)
                w = slots[si].rearrange("p (k j c) -> p k j c", j=4, c=128)
                for pi in range(2):
                    cglob = blk * 2 + pi
                    lc = cglob - cbase
                    for gv in range(2):
                        j = 2 * pi + gv
                        b0 = 2 * gv
                        pkm = [("ps", b0), ("ps", b0 + 1)]
                        for k in range(16):
                            mm(ps(b0), w[:, k, j, :], u2T[:, k, 0:512], k == 0, k == 15,
                               [("slot", si)] + u2keys[0:4], [("ps", b0)])
                            mm(ps(b0 + 1), w[:, k, j, :], u2T[:, k, 512:1024], k == 0, k == 15,
                               [("slot", si)] + u2keys[4:8], [("ps", b0 + 1)])
                            mm(ps(4, 1, gv), w[:, k, j, :], u2T[:, k, 1024:1025], k == 0, k == 15,
                               [("slot", si), ("u2T", 8)], [("psH", gv)])
                        chunk = cglob if gv == 0 else NCH + cglob
                        w0 = cw[:, chunk * 3 + 0:chunk * 3 + 1]
                        w1 = cw[:, chunk * 3 + 1:chunk * 3 + 2]
                        w2 = cw[:, chunk * 3 + 2:chunk * 3 + 3]
                        bb = cb[:, chunk:chunk + 1]
                        tmp = tmpG if gv == 0 else tmpV
                        tk = [("tmp", gv)]
                        pfull = psum_t[:, b0 * 512:(b0 + 2) * 512]
                        act(tmp, pfull, AF.Identity, pkm + [("cw", 0), ("cb", 0)], tk, bias=bb, scale=w1)
                        stt(tmp[:, 1:1024], pfull[:, 0:1023], w0, tmp[:, 1:1024], ALU.mult, ALU.add,
                            pkm + tk + [("cw", 0)], tk)
                        stt(tmp[:, 0:1023], pfull[:, 1:1024], w2, tmp[:, 0:1023], ALU.mult, ALU.add,
                            pkm + tk + [("cw", 0)], tk)
                        stt(tmp[:, 1023:1024], ps(4, 1, gv), w2, tmp[:, 1023:1024], ALU.mult, ALU.add,
                            [("psH", gv)] + tk + [("cw", 0)], tk)
                    act(sg, tmpG, AF.Silu, [("tmp", 0)], [("sg", 0)])
                    tt(actT[:, lc, :], tmpV, sg, ALU.mult, [("tmp", 1), ("sg", 0)], [("actT", lc)])
            last = (fb == len(FFB) - 1)
            if not last:
                for q in range(4):
                    off = (cbase * 4 + q * nch) * 512
                    si = load_slot(wdn_d[:, off:off + nch * 512], nch * 512)
                    w = slots[si][:, 0:nch * 512].rearrange("p (j c) -> p j c", c=512)
                    for t in range(8):
                        bank = 5 + dacc[0] % 3
                        dacc[0] += 1
                        for j in range(nch):
                            mm(ps(bank), actT[:, j, t * 128:(t + 1) * 128], w[:, j, :], j == 0, j == nch - 1,
                               [("actT", j), ("slot", si)], [("ps", bank)])
                        hv = h_sb[:, t, q * 512:(q + 1) * 512]
                        tt(hv, ps(bank), hv, ALU.add, [("ps", bank), ("h", t, q)], [("h", t, q)])
            else:
                S_.phase = "P7"
                ws = []
                for half in range(2):
                    off = (cbase * 4 + 2 * half * nch) * 512
                    si = load_slot(wdn_d[:, off:off + 2 * nch * 512], 2 * nch * 512)
                    ws.append((si, slots[si][:, 0:2 * nch * 512].rearrange("p (q j c) -> p q j c", q=2, c=512)))
                for t in range(8):
                    for q in range(4):
                        si, w = ws[q // 2]
                        bank = 5 + dacc[0] % 3
                        dacc[0] += 1
                        for j in range(nch):
                            mm(ps(bank), actT[:, j, t * 128:(t + 1) * 128], w[:, q % 2, j, :], j == 0, j == nch - 1,
                               [("actT", j), ("slot", si)], [("ps", bank)])
                        hv = h_sb[:, t, q * 512:(q + 1) * 512]
                        tt(hv, ps(bank), hv, ALU.add, [("ps", bank), ("h", t, q)], [("h", t, q)])
                    if t >= 1:
                        final_tile(t - 1)
                final_tile(7)
        S_.op("sp", None, [("y", t) for t in range(8)] + [k for k in S_.lastw if k[0] == "dbg"], [])

        for e in ENGS:
            cnt = 0
            for o_ in S_.per[e]:
                if o_.signal and not o_.stream:
                    cnt += 1
                    o_.sigval = cnt
        ngen = {e: (max([o_.sigval for o_ in S_.per[e]] + [0]) // SEM_GEN) + 1 for e in ENGS}
        eng_sems = {e: [es.enter_context(nc.semaphore(f"s_{e}_{g}")) for g in range(ngen[e])] for e in ENGS}
        stream_sems = {s: es.enter_context(nc.semaphore(f"d_{s}")) for s in S_.stream_cnt}
        block = es.enter_context(nc.Block())

        def emit(engname, eng):
            seen = {}
            for o_ in S_.per[engname]:
                for d in o_.deps:
                    if d.stream:
                        k = ("d", d.stream)
                        val = d.sval
                        sem = stream_sems[d.stream]
                    else:
                        g = (d.sigval - 1) // SEM_GEN
                        k = ("e", d.eng, g)
                        val = d.sigval - g * SEM_GEN
                        sem = eng_sems[d.eng][g]
                        if seen.get(("e", d.eng, g + 1), 0) > 0:
                            continue
                    if seen.get(k, 0) >= val:
                        continue
                    seen[k] = val
                    eng.wait_ge(sem, val)
                if o_.fn is None:
                    continue
                ins = o_.fn(eng)
                if o_.stream:
                    ins.then_inc(stream_sems[o_.stream], 16)
                elif o_.signal:
                    g = (o_.sigval - 1) // SEM_GEN
                    ins.then_inc(eng_sems[o_.eng][g], 1)

        @block.tensor
        def _(e):
            emit("pe", e)

        @block.scalar
        def _(e):
            emit("act", e)

        @block.vector
        def _(e):
            emit("dve", e)

        @block.gpsimd
        def _(e):
            emit("pool", e)

        @block.sync
        def _(e):
            emit("sp", e)

    nc._sched = S_
    return nc


def _const_tables(half):
    perm = np.arange(S) if half == 0 else (S - 1 - np.arange(S))
    inv_freq = (np.float32(10000.0) ** (-np.arange(32, dtype=np.float32) / np.float32(32))).astype(np.float32)
    row = (perm // 64).astype(np.float32)
    col = (perm % 64).astype(np.float32)
    ar = row[:, None] * inv_freq[None, :]
    ac = col[:, None] * inv_freq[None, :]
    cr, sr = np.cos(ar).astype(np.float32), np.sin(ar).astype(np.float32)
    cc, sc_ = np.cos(ac).astype(np.float32), np.sin(ac).astype(np.float32)
    rope = np.concatenate([cr, cr, cc, cc, -sr, sr, -sc_, sc_], axis=1).astype(np.float32)
    ps_ = perm.astype(np.int64)
    pk_ = perm[:NQ].astype(np.int64)
    m = (ps_[:, None] * pk_[None, :]) % S
    ang = 2.0 * np.pi * m.astype(np.float64) / S
    ct = (np.cos(ang) / np.sqrt(S))
    st = (np.sin(ang) / np.sqrt(S))
    dft = np.zeros((5, 128, 2, 16, 256), dtype=np.float32)
    for gi in range(5):
        k0 = gi * 256
        n = 256 if gi < 4 else 1
        for a, tab in enumerate((ct, st)):
            blk = tab[:, k0:k0 + n].reshape(16, 128, n)
            dft[gi, :, a, :, :n] = blk.transpose(1, 0, 2)
    dft = dft.reshape(5, 128, 2 * 16 * 256).astype(ml_dtypes.bfloat16)
    return rope, dft


def _channel_dft():
    c = np.arange(128)
    ang = 2.0 * np.pi * ((c[:, None] * c[None, :]) % 128) / 128.0
    cc = np.cos(ang) / np.sqrt(128.0)
    ns = -np.sin(ang) / np.sqrt(128.0)
    return np.concatenate([cc, ns], axis=1).astype(ml_dtypes.bfloat16)


def _tile_rows(w, ncols_blk):
    K, N = w.shape
    nk = K // 128
    nb = N // ncols_blk
    a = w.reshape(nk, 128, nb, ncols_blk).transpose(1, 2, 0, 3)
    return np.ascontiguousarray(a).reshape(128, nb * nk * ncols_blk)


def prep_inputs(x, norm1_g, w_in, q_norm_g, k_norm_g, w_fmix, attn_out_g, fourier_out_g, w_out, norm2_g,
                w_up, conv_w, conv_b, w_down, final_g):
    f32 = np.float32
    x = np.asarray(x, f32)
    w_in0 = np.asarray(w_in, f32)[0]
    w_out0 = np.asarray(w_out, f32)[0]
    w_up0 = np.asarray(w_up, f32)[0]
    w_dn0 = np.asarray(w_down, f32)[0]
    conv_w0 = np.asarray(conv_w, f32)[0]
    conv_b0 = np.asarray(conv_b, f32)[0]
    win_t = _tile_rows(w_in0, 512)
    wout_t = _tile_rows(w_out0, 512)
    g = w_up0[:, :DFF].reshape(16, 128, NCH, 128)
    v = w_up0[:, DFF:].reshape(16, 128, NCH, 128)
    gv = np.stack([g, v], axis=3)
    gv = gv.reshape(16, 128, 22, 2, 2, 128)
    wup_t = np.ascontiguousarray(gv.transpose(1, 2, 0, 3, 4, 5)).reshape(128, 22 * 8192)
    parts = []
    wd = w_dn0.reshape(NCH, 128, 4, 512)
    for (cbase, nch) in FFB:
        for q in range(4):
            parts.append(np.ascontiguousarray(wd[cbase:cbase + nch, :, q, :].transpose(1, 0, 2)).reshape(128, nch * 512))
    wdn_t = np.concatenate(parts, axis=1)
    wf_t = np.ascontiguousarray(np.asarray(w_fmix, f32)[0].transpose(1, 0, 2)).reshape(128, 1024)
    cb_t = np.ascontiguousarray(conv_b0.reshape(88, 128).T)
    ccs = _channel_dft()
    ident = np.eye(128, dtype=f32).astype(ml_dtypes.bfloat16)
    tabs = [_const_tables(0), _const_tables(1)]
    common = {
        "ccs": ccs, "ident": ident,
        "g1": np.asarray(norm1_g, f32)[0], "g2": np.asarray(norm2_g, f32)[0], "gF": np.asarray(final_g, f32),
        "ga": np.asarray(attn_out_g, f32)[0], "gf": np.asarray(fourier_out_g, f32)[0],
        "gq": np.asarray(q_norm_g, f32)[0], "gk": np.asarray(k_norm_g, f32)[0],
        "cb": cb_t, "wf": wf_t, "win": win_t, "wout": wout_t, "wup": wup_t, "wdn": wdn_t,
    }
    in_maps = []
    for c in range(8):
        b, half = c // 2, c % 2
        xl = x[b] if half == 0 else x[b][::-1]
        cwc = conv_w0 if half == 0 else conv_w0[::-1]
        cw_t = np.ascontiguousarray(cwc.reshape(3, 88, 128).transpose(2, 1, 0)).reshape(128, 88 * 3)
        m = dict(common)
        m["x_loc"] = np.ascontiguousarray(xl)
        m["rope"] = tabs[half][0]
        m["dft"] = tabs[half][1]
        m["cw"] = cw_t
        in_maps.append(m)
    return in_maps


_NC_CACHE = {}


def kernel(**inputs):
    in_maps = prep_inputs(**inputs)
    if "nc" not in _NC_CACHE:
        _NC_CACHE["nc"] = build_program()
    nc = _NC_CACHE["nc"]
    res = run_bass_kernel_spmd(nc, in_maps, core_ids=list(range(8)))
    out = np.empty((4, S, D), dtype=np.float32)
    for c in range(8):
        b, half = c // 2, c % 2
        y = np.asarray(res.results[c]["y"], dtype=np.float32)
        if half == 0:
            out[b, :T] = y
        else:
            out[b, T:] = y[::-1]
    return out
```

```python
from contextlib import ExitStack

import numpy as np
import ml_dtypes

import concourse.bass as bass
import concourse.mybir as mybir
from concourse.bass_utils import run_bass_kernel_spmd

F32 = mybir.dt.float32
BF16 = mybir.dt.bfloat16
AF = mybir.ActivationFunctionType
ALU = mybir.AluOpType
AX = mybir.AxisListType

D = 2048
S = 2048
T = 1024
NQ = T + 1
HD = 128
NH = 8
NKV = 2
DFF = 5632
NCH = DFF // 128
EPS = 1e-6
FFB = [(0, 12), (12, 12), (24, 12), (36, 8)]
SEM_GEN = 3000
TAP_UTE = False

ENGS = ["pe", "act", "dve", "pool", "sp"]


class Op:
    __slots__ = ("eng", "fn", "stream", "sval", "signal", "sigval", "gidx", "deps", "phase")


class Sched:
    def __init__(self):
        self.ops = []
        self.per = {e: [] for e in ENGS}
        self.lastw = {}
        self.readers = {}
        self.touch = {}
        self.inherit = {}
        self.stream_cnt = {}
        self.phase = ""

    @staticmethod
    def _k(o):
        return ("d", o.stream) if o.stream else ("e", o.eng)

    def op(self, eng, fn, reads=(), writes=(), stream=None):
        o = Op()
        o.eng = eng
        o.fn = fn
        o.stream = stream
        o.signal = False
        o.sigval = 0
        o.sval = 0
        o.gidx = len(self.ops)
        o.phase = self.phase
        if stream:
            self.stream_cnt[stream] = self.stream_cnt.get(stream, 0) + 1
            o.sval = 16 * self.stream_cnt[stream]
        deps = {}

        def add(d):
            k = self._k(d)
            if k not in deps or deps[k].gidx < d.gidx:
                deps[k] = d

        reads = list(reads)
        writes = list(writes)
        for key in reads + writes:
            if key not in self.lastw and key not in self.readers:
                for d in self.inherit.get(key[0], {}).values():
                    add(d)
        for key in reads:
            w = self.lastw.get(key)
            if w is not None:
                add(w)
        for key in writes:
            w = self.lastw.get(key)
            if w is not None and not (key[0] == "junkq" and w.eng == eng and not w.stream):
                add(w)
            for r in self.readers.get(key, {}).values():
                add(r)
        for key in reads:
            self.readers.setdefault(key, {})[self._k(o)] = o
        for key in writes:
            self.lastw[key] = o
            self.readers[key] = {}
        for key in reads + writes:
            self.touch.setdefault(key[0], {})[self._k(o)] = o
        o.deps = []
        for d in deps.values():
            if (not d.stream) and d.eng == "pe" and eng == "pe" and not stream:
                continue
            o.deps.append(d)
            if not d.stream:
                d.signal = True
        self.ops.append(o)
        self.per[eng].append(o)
        return o

    def alias(self, new_name, old_names):
        m = self.inherit.setdefault(new_name, {})
        for n in old_names:
            for k, d in self.touch.get(n, {}).items():
                if k not in m or m[k].gidx < d.gidx:
                    m[k] = d
            for k, d in self.inherit.get(n, {}).items():
                if k not in m or m[k].gidx < d.gidx:
                    m[k] = d


class Arena:
    def __init__(self, tensor, sched, nbytes):
        self.t = tensor
        self.s = sched
        self.nbytes = nbytes
        self.live = []

    def alloc(self, name, off, nbytes, dtype, shape_str=None, **dims):
        assert off % 4 == 0 and off + nbytes <= self.nbytes, (name, off, nbytes)
        b0, b1 = off, off + nbytes
        old = sorted(set(n for (n, a0, a1) in self.live if a0 < b1 and b0 < a1))
        self.live.append((name, b0, b1))
        if old:
            self.s.alias(name, old)
        w0 = off // 4
        w1 = (off + nbytes + 3) // 4
        ap = self.t[:, w0:w1]
        if dtype != F32:
            ap = ap.bitcast(dtype)
        if shape_str:
            ap = ap.rearrange(shape_str, **dims)
        return ap


def build_program(debug=False):
    nc = bass.Bass("TRN2", target_bir_lowering=False)
    S_ = Sched()

    def dram(name, shape, dt=F32, kind="ExternalInput"):
        return nc.dram_tensor(name, list(shape), dt, kind=kind).ap()

    x_d = dram("x_loc", [S, D])
    rope_d = dram("rope", [S, 256])
    dft_d = dram("dft", [5, 128, 2 * 16 * 256], BF16)
    ccs_d = dram("ccs", [128, 256], BF16)
    ident_d = dram("ident", [128, 128], BF16)
    g1_d = dram("g1", [D])
    g2_d = dram("g2", [D])
    gF_d = dram("gF", [D])
    ga_d = dram("gaT", [128, 8])
    gf_d = dram("gf", [1024])
    gq_d = dram("gq", [128])
    gk_d = dram("gk", [128])
    cw_d = dram("cw", [128, 88 * 3])
    cb_d = dram("cb", [128, 88])
    wf_d = dram("wf", [128, 1024])
    win_d = dram("win", [128, 5 * 8192])
    wout_d = dram("wout", [128, 4 * 8192])
    wup_d = dram("wup", [128, 22 * 8192])
    wdn_d = dram("wdn", [128, 44 * 2048])
    y_d = dram("y", [T, D], F32, kind="ExternalOutput")

    ARENA_BYTES = 212480
    es = ExitStack()
    with es:
        arena_t = es.enter_context(nc.sbuf_tensor("arena", [128, ARENA_BYTES // 4], F32))
        psum_t = es.enter_context(nc.psum_tensor("psum", [128, 4096], F32))
        A = Arena(arena_t, S_, ARENA_BYTES)

        R_SLOT = 0
        R_F = 49152
        R_Q = 81920
        R_MIX = 114944
        R_A = 147840
        R_C = 207744

        slots = [A.alloc("slot", R_SLOT + i * 16384, 16384, BF16) for i in range(3)]
        ident = A.alloc("ident", R_C, 256, BF16)
        cw = A.alloc("cw", R_C + 256, 1056, F32)
        cb = A.alloc("cb", R_C + 1312, 352, F32)
        stat = A.alloc("stat", R_C + 1664, 512, F32)
        epsc = A.alloc("epsc", R_C + 2176, 4, F32)
        junkq = A.alloc("junkq", R_C + 2184, 2048, mybir.dt.float8e4)
        JK = [("junkq", 0)]

        def ps(bank, n=512, off=0):
            return psum_t[:, bank * 512 + off: bank * 512 + off + n]

        def ps_bf(bank, nbanks=1):
            return psum_t[:, bank * 512:(bank + nbanks) * 512].bitcast(BF16)

        def mm(out, lhsT, rhs, start, stop, reads, writes):
            return S_.op("pe", lambda e: e.matmul(out, lhsT, rhs, start=start, stop=stop), reads, writes)

        def tr(out, in_, idn, reads, writes):
            return S_.op("pe", lambda e: e.transpose(out, in_, idn), reads, writes)

        def act(out, in_, func, reads, writes, bias=None, scale=None, accum=None, sat=None):
            kw = {}
            if sat is not None:
                kw["saturate"] = sat
            if bias is not None:
                kw["bias"] = bias
            if scale is not None:
                kw["scale"] = scale
            if accum is not None:
                kw["accum_out"] = accum
            return S_.op("act", lambda e: e.activation(out, in_, func, **kw), reads, writes)

        def dve(fn, reads, writes):
            return S_.op("dve", fn, reads, writes)

        def tt(out, in0, in1, op, reads, writes):
            return dve(lambda e: e.tensor_tensor(out, in0, in1, op), reads, writes)

        def stt(out, in0, scalar, in1, op0, op1, reads, writes):
            return dve(lambda e: e.scalar_tensor_tensor(out, in0, scalar, in1, op0, op1), reads, writes)

        def tsc(out, in0, s1, s2, op0, op1, reads, writes):
            if op1 is None:
                return dve(lambda e: e.tensor_scalar(out, in0, s1, s2, op0), reads, writes)
            return dve(lambda e: e.tensor_scalar(out, in0, s1, s2, op0, op1), reads, writes)

        def red(out, in_, reads, writes):
            return dve(lambda e: e.tensor_reduce(out, in_, AX.X, ALU.add), reads, writes)

        def recip(out, in_, reads, writes):
            return dve(lambda e: e.reciprocal(out, in_), reads, writes)

        def copy_any(which, out, in_, reads, writes):
            if which % 2 == 0:
                return act(out, in_, AF.Copy, reads, writes)
            return dve(lambda e: e.tensor_copy(out, in_), reads, writes)

        def dma(eng, out, in_, reads, writes, stream, **kw):
            return S_.op(eng, lambda e: e.dma_start(out=out, in_=in_, **kw), reads, writes, stream=stream)

        stat_ctr = [0]

        def stat_slot(n=1):
            i = stat_ctr[0]
            if i % 128 + n > 128:
                i += 128 - i % 128
            stat_ctr[0] = i + n
            c = i % 128
            return stat[:, c:c + n], ("stat", c, c + n - 1)

        def stat_keys(k):
            return [("stat", j) for j in range(k[1], k[2] + 1)]

        def rstd_from_ss(ss, sskeys, rows, ncols, inv_n):
            r, rk = stat_slot(ncols)
            rkeys = stat_keys(rk)
            act(r[:rows], ss[:rows], AF.Sqrt, sskeys + [("epsc", 0)], rkeys, bias=epsc[:rows], scale=inv_n)
            recip(r[:rows], r[:rows], rkeys, rkeys)
            return r, rkeys

        slot_ctr = [0]

        def load_slot(src_ap, ncols, after=()):
            i = slot_ctr[0] % 3
            slot_ctr[0] += 1
            dma("pool", slots[i][:, 0:ncols], src_ap, list(after), [("slot", i)], f"slot{i}",
                max_dma_last_dim=8192)
            return i

        def tap(name, ap, names):
            if not debug:
                return
            shape = [int(s) for s in ap.shape]
            dd = nc.dram_tensor("dbg_" + name, shape, ap.dtype, kind="ExternalOutput").ap()
            reads = [k for k in S_.lastw if k[0] in names]
            dma("sp", dd, ap, reads, [("dbg", name)], "dbg_" + name)

        dma("sp", ident, ident_d, [], [("ident", 0)], "c_ident")
        dma("sp", cw, cw_d, [], [("cw", 0)], "c_cw")
        dma("sp", cb, cb_d, [], [("cb", 0)], "c_cb")
        KI = [("ident", 0)]

        f_tm = A.alloc("f_tm", R_F, 32768, BF16, "p (s c) -> p s c", c=1024)
        qT = A.alloc("qT", R_Q, 16416, BF16, "p (h t) -> p h t", t=1026)
        kT = A.alloc("kT", R_Q + 16416, 8192, BF16, "p (h t) -> p h t", t=2048)
        Vaug = A.alloc("V", R_Q + 24608, 8320, BF16, "p (s h c) -> p s h c", h=2, c=130)
        uT = A.alloc("uT", R_A, 32768, BF16, "p (k t) -> p k t", t=1024)
        o = R_A + 32768
        sq = A.alloc("sq", o, 2048, F32); o += 2048
        qn = A.alloc("qn", o, 2048, F32); o += 2048
        t1 = A.alloc("t1", o, 2048, F32); o += 2048
        t2 = A.alloc("t2", o, 2048, F32); o += 2048
        qr = [A.alloc("qr", o + i * 1024, 1024, BF16) for i in range(3)]; o += 3072
        ropet = [A.alloc("ropet", o + i * 1024, 1024, F32) for i in range(2)]; o += 2048
        gq_bc = A.alloc("gq_bc", o, 512, F32); o += 512
        gk_bc = A.alloc("gk_bc", o, 512, F32); o += 512
        xs = [A.alloc("xs", R_MIX + i * 8192, 8192, F32) for i in range(2)] + [A.alloc("xs", R_A + 49152, 8192, F32)]
        g1_bc = A.alloc("g1_bc", R_MIX + 16384, 8192, F32)
        u_tm = [A.alloc("u_tm", R_MIX + 24576 + i * 4096, 4096, BF16) for i in range(2)]

        dve(lambda e: e.memset(epsc, EPS), [], [("epsc", 0)])
        dma("sp", g1_bc, g1_d.partition_broadcast(128), [], [("g1_bc", 0)], "c_g1")
        dma("sp", gq_bc, gq_d.partition_broadcast(128), [], [("gq_bc", 0)], "c_gq")
        dma("sp", gk_bc, gk_d.partition_broadcast(128), [], [("gk_bc", 0)], "c_gk")
        dve(lambda e: e.memset(Vaug[:, :, :, 128:129], 1.0), [], [("V", 99)])

        cpy = [0]
        rope_ctr = [0]
        psacc_ctr = [0]
        pstr_ctr = [0]
        qr_ctr = [0]

        def norm_tile(src, srckeys, rows, gbc, gkey, dst, dstkeys):
            ss, sk = stat_slot(1)
            sskeys = stat_keys(sk)
            act(junkq[:rows], src[:rows], AF.Square, srckeys, JK + sskeys, accum=ss[:rows], sat=False)
            r, rkeys = rstd_from_ss(ss, sskeys, rows, 1, 1.0 / D)
            stt(dst[:rows], src[:rows], r[:rows, 0:1], gbc[:rows], ALU.mult, ALU.mult,
                srckeys + rkeys + [gkey], dstkeys)

        def transposes16(src, srckeys, rows, dstT, dstkeys_fn, col0, bankpair, which):
            pb = ps_bf(bankpair, 2)
            pk = [("ps", bankpair), ("ps", bankpair + 1)]
            for j in range(16):
                tr(pb[:, j * 128: j * 128 + rows], src[:rows, j * 128:(j + 1) * 128], ident[:rows, :rows],
                   srckeys + KI, pk)
            copy_any(which, dstT[:, :, col0:col0 + rows],
                     pb.rearrange("p (k t) -> p k t", t=128)[:, :, 0:rows], pk, dstkeys_fn)

        def qk_post(psrc, pkeys, rows, nheads, gbc, gbkey, rtile, rkey, dstT, dst_h0, col0, dstkeys):
            n = nheads * 128
            act(sq[:rows, :n], psrc[:rows, :n], AF.Square, pkeys, [("sq", 0)])
            ss, sk = stat_slot(nheads)
            sskeys = stat_keys(sk)
            red(ss[:rows], sq[:rows, :n].rearrange("p (h d) -> p h d", d=128), [("sq", 0)], sskeys)
            r, rkeys = rstd_from_ss(ss, sskeys, rows, nheads, 1.0 / HD)
            for h in range(nheads):
                stt(qn[:rows, h * 128:(h + 1) * 128], psrc[:rows, h * 128:(h + 1) * 128],
                    r[:rows, h:h + 1], gbc[:rows], ALU.mult, ALU.mult,
                    pkeys + rkeys + [gbkey], [("qn", 0)])
            qn3 = qn[:rows, :n].rearrange("p (h d) -> p h d", d=128)
            cosb = rtile[:rows, 0:128].unsqueeze(1).to_broadcast([rows, nheads, 128])
            tt(t1[:rows, :n].rearrange("p (h d) -> p h d", d=128), qn3, cosb, ALU.mult,
               [("qn", 0), rkey], [("t1", 0)])
            qn5 = qn[:rows, :n].rearrange("p (h a b c) -> p h a b c", a=2, b=2, c=32)
            t25 = t2[:rows, :n].rearrange("p (h a b c) -> p h a b c", a=2, b=2, c=32)
            sin5 = rtile[:rows, 128:256].rearrange("p (a b c) -> p a b c", a=2, b=2, c=32)
            for blk in range(2):
                sb = sin5[:, :, blk, :].unsqueeze(1).to_broadcast([rows, nheads, 2, 32])
                tt(t25[:, :, :, blk, :], qn5[:, :, :, 1 - blk, :], sb, ALU.mult,
                   [("qn", 0), rkey], [("t2", blk)])
            qi = qr_ctr[0] % 3
            qr_ctr[0] += 1
            qrb = qr[qi]
            tt(qrb[:rows, :n], t1[:rows, :n], t2[:rows, :n], ALU.add,
               [("t1", 0), ("t2", 0), ("t2", 1)], [("qr", qi)])

            def tail():
                bank = pstr_ctr[0] % 3
                pstr_ctr[0] += 1
                pb = ps_bf(bank, 1)
                for h in range(nheads):
                    tr(pb[:, h * 128:h * 128 + rows], qrb[:rows, h * 128:(h + 1) * 128], ident[:rows, :rows],
                       [("qr", qi)] + KI, [("ps", bank)])
                copy_any(0, dstT[:, dst_h0:dst_h0 + nheads, col0:col0 + rows],
                         pb[:, 0:n].rearrange("p (h t) -> p h t", t=128)[:, :, 0:rows], [("ps", bank)], dstkeys)
            return tail

        def load_rope(gt):
            i = rope_ctr[0] % 2
            rope_ctr[0] += 1
            dma("sp", ropet[i], rope_d[gt * 128:(gt + 1) * 128, :], [], [("ropet", i)], f"ropet{i}")
            return ropet[i], ("ropet", i)

        pend = []

        def flush(keep):
            while len(pend) > keep:
                pend.pop(0)()

        for pa in range(2):
            if pa == 1 and TAP_UTE:
                tap("uTE", uT, ["uT"])
            S_.phase = f"P1{'AB'[pa]}"
            for t in range(8):
                gt = pa * 8 + t
                b = gt % 2
                xb = gt % 3
                dma("sp", xs[xb], x_d[gt * 128:(gt + 1) * 128, :], [], [("xs", xb)], f"xs{xb}")
                norm_tile(xs[xb], [("xs", xb)], 128, g1_bc, ("g1_bc", 0), u_tm[b], [("u_tm", b)])
                pend.append(lambda t=t, b=b: transposes16(u_tm[b], [("u_tm", b)], 128, uT, [("uT", t)], t * 128,
                                                          2 * (t % 2), t))
                flush(1)
            flush(0)
            S_.phase = f"P2{'AB'[pa]}"
            blocks = [0, 1, 2, 3, 4] if pa == 0 else [2, 3, 4, 0, 1]
            for blk in blocks:
                after = []
                if pa == 0 and blk == 1:
                    after = [("uT", 3)]
                if pa == 0 and blk == 2:
                    after = [("uT", 6)]
                si = load_slot(win_d[:, blk * 8192:(blk + 1) * 8192], 8192, after)
                w = slots[si].rearrange("p (k c) -> p k c", c=512)
                halo_only = (pa == 1 and blk < 2)
                tiles = [0] if halo_only else list(range(8))
                for t in tiles:
                    gt = pa * 8 + t
                    rows = 1 if halo_only else 128
                    bank = 4 + psacc_ctr[0] % 4
                    psacc_ctr[0] += 1
                    pk = [("ps", bank)]
                    for k in range(16):
                        mm(ps(bank)[:rows, :], uT[:, k, t * 128:t * 128 + rows], w[:, k, :], k == 0, k == 15,
                           [("uT", t), ("slot", si)], pk)
                    if blk < 2:
                        rt, rk = load_rope(gt)
                        col0 = 1024 if halo_only else t * 128
                        pend.append(qk_post(ps(bank), pk, rows, 4, gq_bc, ("gq_bc", 0), rt, rk, qT, 4 * blk, col0,
                                            [("qT", blk, gt)]))
                    elif blk == 2:
                        rt, rk = load_rope(gt)
                        act(Vaug[:, gt, :, 0:128], ps(bank)[:, 256:512].rearrange("p (h d) -> p h d", d=128),
                            AF.Copy, pk, [("V", gt)])
                        pend.append(qk_post(ps(bank), pk, 128, 2, gk_bc, ("gk_bc", 0), rt, rk, kT, 0, gt * 128,
                                            [("kT", gt)]))
                    else:
                        fb = blk - 3
                        act(f_tm[:, gt, fb * 512:(fb + 1) * 512], ps(bank), AF.Copy, pk, [("f_tm", gt, fb)])
                    flush(2)
            flush(0)

        tap("qT", qT, ["qT"])
        tap("kT", kT, ["kT"])
        tap("V", Vaug, ["V"])
        tap("f", f_tm, ["f_tm"])
        S_.phase = "P3"
        mixT = A.alloc("mixT", R_MIX, 32832, BF16, "p (k t) -> p k t", t=1026)
        o = R_A
        PT = [A.alloc("PT", o + i * 16384, 16384, BF16, "p (s q) -> p s q", q=512) for i in range(2)]; o += 32768
        O_sb = [A.alloc("O_sb", o + i * 4160, 4160, F32, "p (h c) -> p h c", c=130) for i in range(4)]; o += 16640
        sqa = A.alloc("sqa", o, 2048, BF16); o += 2048
        a_tm = [A.alloc("a_tm", o + i * 2048, 2048, BF16) for i in range(4)]; o += 8192
        gaT = A.alloc("gaT", o, 32, F32); o += 32
        dma("sp", gaT, ga_d, [], [("gaT", 0)], "c_ga")
        pend_atr = []

        scale = 1.0 / float(np.sqrt(HD))
        sbank = [0]
        obank = [0]
        ptc = [0]

        def attn_norm(tl, rows, col0):
            okeys = [("O_sb", tl, h) for h in range(NH)]
            rl, rlk = stat_slot(NH)
            rlkeys = stat_keys(rlk)
            recip(rl[:rows].unsqueeze(2), O_sb[tl][:rows, :, 128:129], okeys, rlkeys)
            act(sqa[:rows, :].rearrange("p (h d) -> p h d", d=128), O_sb[tl][:rows, :, 0:128], AF.Square,
                okeys, [("sqa", 0)])
            ssh, sk = stat_slot(NH)
            sshk = stat_keys(sk)
            red(ssh[:rows], sqa[:rows, :].rearrange("p (h d) -> p h d", d=128), [("sqa", 0)], sshk)
            tt(ssh[:rows], ssh[:rows], rl[:rows], ALU.mult, sshk + rlkeys, sshk)
            tt(ssh[:rows], ssh[:rows], rl[:rows], ALU.mult, sshk + rlkeys, sshk)
            ss1, sk1 = stat_slot(1)
            ss1k = stat_keys(sk1)
            red(ss1[:rows], ssh[:rows], sshk, ss1k)
            r, rkeys = rstd_from_ss(ss1, ss1k, rows, 1, 1.0 / 1024.0)
            fac, fk = stat_slot(NH)
            fkeys = stat_keys(fk)
            tsc(fac[:rows], rl[:rows], r[:rows, 0:1], None, ALU.mult, None, rlkeys + rkeys, fkeys)
            atm = a_tm[tl]
            tt(atm[:rows, :].rearrange("p (h d) -> p h d", d=128), O_sb[tl][:rows, :, 0:128],
               fac[:rows].unsqueeze(2).to_broadcast([rows, NH, 128]), ALU.mult,
               okeys + fkeys, [("a_tm", tl)])

            def tail():
                pb = ps_bf(7, 1)
                for h in range(NH):
                    tr(pb[:, h * 128:h * 128 + rows], atm[:rows, h * 128:(h + 1) * 128], ident[:rows, :rows],
                       [("a_tm", tl)] + KI, [("ps", 7)])
                tt(mixT[:, 0:8, col0:col0 + rows], pb.rearrange("p (h t) -> p h t", t=128)[:, :, 0:rows],
                   gaT[:, :].unsqueeze(2).to_broadcast([128, NH, rows]), ALU.mult,
                   [("ps", 7), ("gaT", 0)], [("mixT", 0, col0 // 128)])
            pend_atr.append(tail)

        prev = None
        pend_norm = []
        for (q0, n) in [(0, 512), (512, 512), (None, 0)]:
            heads = range(NH) if q0 is not None else [None]
            for h in heads:
                if h is not None:
                    kv = h // 4
                    pi = ptc[0] % 2
                    ptc[0] += 1
                for sc in range(16):
                    if h is not None:
                        bank = sbank[0] % 3
                        sbank[0] += 1
                        mm(ps(bank)[:, :n], kT[:, kv, sc * 128:(sc + 1) * 128], qT[:, h, q0:q0 + n], True, True,
                           [("kT", sc)] + [("qT", h // 4, q0 // 128 + j) for j in range(4)], [("ps", bank)])
                        act(PT[pi][:, sc, :n], ps(bank)[:, :n], AF.Exp, [("ps", bank)], [("PT", pi, sc)],
                            scale=scale)
                    if prev is not None:
                        ppi, ph = prev
                        for tl in range(4):
                            mm(ps(3 + tl)[:, 0:129], PT[ppi][:, sc, tl * 128:(tl + 1) * 128],
                               Vaug[:, sc, ph // 4, 0:129], sc == 0, sc == 15,
                               [("PT", ppi, sc), ("V", sc), ("V", 99)], [("ps", 3 + tl)])
                if prev is not None:
                    ppi, ph = prev
                    for tl in range(4):
                        copy_any(1, O_sb[tl][:, ph, 0:129], ps(3 + tl)[:, 0:129], [("ps", 3 + tl)],
                                 [("O_sb", tl, ph)])
                    if ph == NH - 1:
                        while pend_norm:
                            pend_norm.pop(0)()
                    if ph == 1:
                        while pend_atr:
                            pend_atr.pop(0)()
                prev = (pi, h) if h is not None else None
                if h == NH - 1:
                    for tl in range(4):
                        pend_norm.append(lambda tl=tl, q0=q0: attn_norm(tl, 128, q0 + tl * 128))
        obank[0] = 0
        pi = ptc[0] % 2
        ptc[0] += 1
        for h in range(NH):
            kv = h // 4
            bank = sbank[0] % 3
            sbank[0] += 1
            for sc in range(16):
                mm(ps(bank)[:, sc:sc + 1], kT[:, kv, sc * 128:(sc + 1) * 128], qT[:, h, 1024:1025], True, True,
                   [("kT", sc), ("qT", h // 4, 8)], [("ps", bank)])
            act(PT[pi][:, h, 0:16], ps(bank)[:, 0:16], AF.Exp, [("ps", bank)], [("PT", pi, h)], scale=scale)
        while pend_norm:
            pend_norm.pop(0)()
        while pend_atr:
            pend_atr.pop(0)()
        for h in range(NH):
            kv = h // 4
            bank = 3 + obank[0] % 4
            obank[0] += 1
            for sc in range(16):
                mm(ps(bank)[:1, 0:129], PT[pi][:, h, sc:sc + 1], Vaug[:, sc, kv, 0:129], sc == 0, sc == 15,
                   [("PT", pi, h), ("V", sc), ("V", 99)], [("ps", bank)])
            copy_any(1, O_sb[0][:1, h, 0:129], ps(bank)[:1, 0:129], [("ps", bank)], [("O_sb", 0, h)])
        attn_norm(0, 1, 1024)
        while pend_atr:
            pend_atr.pop(0)()

        S_.phase = "P4"
        ZT = A.alloc("ZT", R_Q, 32832, BF16, "p (a g t) -> p a g t", a=2, g=8)
        o = R_A
        CS = [A.alloc("CS", o + i * 16384, 16384, BF16, "p (a s k) -> p a s k", a=2, s=16) for i in range(2)]
        o += 32768
        gf_bc = A.alloc("gf_bc", o, 4096, F32); o += 4096
        f_n2 = [A.alloc("f_n", o + i * 2048, 2048, BF16) for i in range(2)]; o += 4096
        AB = A.alloc("AB", o, 4096, BF16, "p (g a d) -> p g a d", a=2, d=128); o += 4096
        ccs = A.alloc("ccs", o, 512, BF16, "p (a c) -> p a c", a=2); o += 512
        wfb = A.alloc("wfb", o, 2048, BF16); o += 2048
        dma("sp", gf_bc, gf_d.partition_broadcast(128), [], [("gf_bc", 0)], "c_gf")
        dma("sp", ccs, ccs_d.rearrange("p (a c) -> p a c", a=2), [], [("ccs", 0)], "c_ccs")
        dma("pool", wfb, wf_d, [], [("wfb", 0)], "c_wf")
        for g in range(8):
            for a in range(2):
                mm(ps(0)[:, a * 128:(a + 1) * 128], ccs[:, a, :], wfb[:, g * 128:(g + 1) * 128], True, True,
                   [("ccs", 0), ("wfb", 0)], [("ps", 0)])
            cpy[0] += 1
            copy_any(cpy[0], AB[:, g, :, :], ps(0)[:, 0:256].rearrange("p (a d) -> p a d", d=128), [("ps", 0)],
                     [("AB", g)])
        zb = [0]
        qgroups = [(i * 256, 256) for i in range(4)] + [(1024, 1)]
        pend_out = []
        pend_tr = []

        def fourier_out(gi, tl, rows, col0):
            b0 = 4 + 2 * tl
            pk = [("ps", b0), ("ps", b0 + 1)]
            pout = psum_t[:, b0 * 512:(b0 + 2) * 512]
            for g in range(8):
                for a in range(2):
                    mm(pout[:rows, g * 128:(g + 1) * 128], ZT[:, a, g, col0:col0 + rows], AB[:, g, a, :],
                       a == 0, a == 1, [("ZT", a, g, gi), ("AB", g)], [("ps", b0 + g // 4)])
            ss, sk = stat_slot(1)
            sskeys = stat_keys(sk)
            act(junkq[:rows, 0:1024], pout[:rows], AF.Square, pk, JK + sskeys, accum=ss[:rows], sat=False)
            r, rkeys = rstd_from_ss(ss, sskeys, rows, 1, 1.0 / 1024.0)
            fn = f_n2[tl]
            stt(fn[:rows], pout[:rows], r[:rows, 0:1], gf_bc[:rows], ALU.mult, ALU.mult,
                pk + rkeys + [("gf_bc", 0)], [("f_n", tl)])

            def tail():
                pb = ps_bf(b0, 1)
                for h in range(8):
                    tr(pb[:, h * 128:h * 128 + rows], fn[:rows, h * 128:(h + 1) * 128], ident[:rows, :rows],
                       [("f_n", tl)] + KI, [("ps", b0)])
                cpy[0] += 1
                copy_any(cpy[0], mixT[:, 8:16, col0:col0 + rows],
                         pb.rearrange("p (h t) -> p h t", t=128)[:, :, 0:rows], [("ps", b0)],
                         [("mixT", 1, col0 // 128)])
            pend_tr.append(tail)

        for gi, (k0, n) in enumerate(qgroups):
            ntile = 2 if n == 256 else 1
            rows = 128 if n == 256 else 1
            ci = gi % 2
            dma("sp", CS[ci], dft_d[gi].rearrange("p (a s k) -> p a s k", a=2, s=16), [], [("CS", ci)], f"CS{ci}")
            for g in range(8):
                bz = 2 * (zb[0] % 2)
                zb[0] += 1
                for sc in range(16):
                    for a in range(2):
                        mm(ps(bz + a)[:, :n], f_tm[:, sc, g * 128:(g + 1) * 128], CS[ci][:, a, sc, :n],
                           sc == 0, sc == 15, [("f_tm", sc, g // 4), ("CS", ci)], [("ps", bz + a)])
                for a in range(2):
                    cpy[0] += 1
                    copy_any(cpy[0], ZT[:, a, g, k0:k0 + n], ps(bz + a)[:, :n], [("ps", bz + a)],
                             [("ZT", a, g, gi)])
                if g == 1:
                    while pend_out:
                        pend_out.pop(0)()
                if g == 4:
                    while pend_tr:
                        pend_tr.pop(0)()
            for tl in range(ntile):
                pend_out.append(lambda gi=gi, tl=tl, rows=rows, k0=k0: fourier_out(gi, tl, rows, k0 + tl * 128))
        while pend_out:
            pend_out.pop(0)()
        while pend_tr:
            pend_tr.pop(0)()

        tap("mixT", mixT, ["mixT"])
        S_.phase = "P5"
        h_sb = A.alloc("h", R_F, 65536, F32, "p (t d) -> p t d", d=2048)
        o = R_A
        u2T = A.alloc("u2T", o, 32832, BF16, "p (k t) -> p k t", t=1026); o += 32832
        g2_bc = A.alloc("g2_bc", o, 8192, F32); o += 8192
        u2_tm = [A.alloc("u2_tm", o + i * 4096, 4096, BF16) for i in range(2)]; o += 8192
        h_halo = A.alloc("h", o, 8192, F32); o += 8192
        dma("sp", g2_bc, g2_d.partition_broadcast(128), [], [("g2_bc", 0)], "c_g2")
        for t in range(8):
            dma("sp", h_sb[:, t, :], x_d[t * 128:(t + 1) * 128, :], [], [("h", t, q) for q in range(4)], f"hx{t}")
        dma("sp", h_halo[0:1, :], x_d[1024:1025, :], [], [("h", 8, q) for q in range(4)], "hx8")

        def htile(t):
            return h_sb[:, t, :] if t < 8 else h_halo

        wacc = [0]
        pend = []

        def do_norm2(t):
            rows = 128 if t < 8 else 1
            hk = [("h", t, q) for q in range(4)]
            b = t % 2
            norm_tile(htile(t), hk, rows, g2_bc, ("g2_bc", 0), u2_tm[b], [("u2_tm", b)])
            pend.append(lambda: transposes16(u2_tm[b], [("u2_tm", b)], rows, u2T, [("u2T", t)], t * 128,
                                             4 + 2 * (t % 2), t))

        for cbk in range(4):
            si = load_slot(wout_d[:, cbk * 8192:(cbk + 1) * 8192], 8192)
            w = slots[si].rearrange("p (k c) -> p k c", c=512)
            for t in range(9):
                rows = 128 if t < 8 else 1
                col0 = t * 128
                bank = wacc[0] % 4
                wacc[0] += 1
                for fc in range(16):
                    mm(ps(bank)[:rows, :], mixT[:, fc, col0:col0 + rows], w[:, fc, :], fc == 0, fc == 15,
                       [("mixT", fc // 8, t), ("slot", si)], [("ps", bank)])
                hv = htile(t)[:rows, cbk * 512:(cbk + 1) * 512]
                tt(hv, ps(bank)[:rows, :], hv, ALU.add, [("ps", bank), ("h", t, cbk)], [("h", t, cbk)])
                if cbk == 3:
                    if t >= 1:
                        do_norm2(t - 1)
                    while len(pend) > 1:
                        pend.pop(0)()
        do_norm2(8)
        while pend:
            pend.pop(0)()

        tap("h1", h_sb, ["h"])
        tap("u2T", u2T, ["u2T"])
        S_.phase = "P6"
        actT = A.alloc("actT", R_MIX, 24576, BF16, "p (c t) -> p c t", t=1024)
        o = R_A + 32832
        tmpG = A.alloc("tmp", o, 4096, F32); o += 4096
        tmpV = A.alloc("tmp", o, 4096, F32); o += 4096
        sg = A.alloc("sg", o, 4096, F32); o += 4096
        gF_bc = A.alloc("gF_bc", o, 8192, F32); o += 8192
        dma("sp", gF_bc, gF_d.partition_broadcast(128), [], [("gF_bc", 0)], "c_gF")
        u2keys = [("u2T", t) for t in range(9)]
        dacc = [0]

        def final_tile(t):
            hk = [("h", t, q) for q in range(4)]
            ss, sk = stat_slot(1)
            sskeys = stat_keys(sk)
            act(junkq, h_sb[:, t, :], AF.Square, hk, JK + sskeys, accum=ss, sat=False)
            r, rkeys = rstd_from_ss(ss, sskeys, 128, 1, 1.0 / D)
            stt(h_sb[:, t, :], h_sb[:, t, :], r[:, 0:1], gF_bc, ALU.mult, ALU.mult,
                hk + rkeys + [("gF_bc", 0)], hk)
            dma("sp", y_d[t * 128:(t + 1) * 128, :], h_sb[:, t, :], hk, [("y", t)], f"y{t % 2}")

        for fb, (cbase, nch) in enumerate(FFB):
            for pbk in range(nch // 2):
                blk = cbase // 2 + pbk
                si = load_slot(wup_d[:, blk * 8192:(blk + 1) * 8192], 8192)
                w = slots[si].rearrange("p (k j c) -> p k j c", j=4, c=128)
                for pi in range(2):
                    cglob = blk * 2 + pi
                    lc = cglob - cbase
                    for gv in range(2):
                        j = 2 * pi + gv
                        b0 = 2 * gv
                        pkm = [("ps", b0), ("ps", b0 + 1)]
                        for k in range(16):
                            mm(ps(b0), w[:, k, j, :], u2T[:, k, 0:512], k == 0, k == 15,
                               [("slot", si)] + u2keys[0:4], [("ps", b0)])
                            mm(ps(b0 + 1), w[:, k, j, :], u2T[:, k, 512:1024], k == 0, k == 15,
                               [("slot", si)] + u2keys[4:8], [("ps", b0 + 1)])
                            mm(ps(4 + gv, 1), w[:, k, j, :], u2T[:, k, 1024:1025], k == 0, k == 15,
                               [("slot", si), ("u2T", 8)], [("ps", 4 + gv)])
                        chunk = cglob if gv == 0 else NCH + cglob
                        w0 = cw[:, chunk * 3 + 0:chunk * 3 + 1]
                        w1 = cw[:, chunk * 3 + 1:chunk * 3 + 2]
                        w2 = cw[:, chunk * 3 + 2:chunk * 3 + 3]
                        bb = cb[:, chunk:chunk + 1]
                        tmp = tmpG if gv == 0 else tmpV
                        tk = [("tmp", gv)]
                        pfull = psum_t[:, b0 * 512:(b0 + 2) * 512]
                        act(tmp, pfull, AF.Identity, pkm + [("cw", 0), ("cb", 0)], tk, bias=bb, scale=w1)
                        stt(tmp[:, 1:1024], pfull[:, 0:1023], w0, tmp[:, 1:1024], ALU.mult, ALU.add,
                            pkm + tk + [("cw", 0)], tk)
                        stt(tmp[:, 0:1023], pfull[:, 1:1024], w2, tmp[:, 0:1023], ALU.mult, ALU.add,
                            pkm + tk + [("cw", 0)], tk)
                        stt(tmp[:, 1023:1024], ps(4 + gv, 1), w2, tmp[:, 1023:1024], ALU.mult, ALU.add,
                            [("ps", 4 + gv)] + tk + [("cw", 0)], tk)
                    act(sg, tmpG, AF.Silu, [("tmp", 0)], [("sg", 0)])
                    tt(actT[:, lc, :], tmpV, sg, ALU.mult, [("tmp", 1), ("sg", 0)], [("actT", lc)])
            last = (fb == len(FFB) - 1)
            if not last:
                for q in range(4):
                    off = (cbase * 4 + q * nch) * 512
                    si = load_slot(wdn_d[:, off:off + nch * 512], nch * 512)
                    w = slots[si][:, 0:nch * 512].rearrange("p (j c) -> p j c", c=512)
                    for t in range(8):
                        bank = 6 + dacc[0] % 2
                        dacc[0] += 1
                        for j in range(nch):
                            mm(ps(bank), actT[:, j, t * 128:(t + 1) * 128], w[:, j, :], j == 0, j == nch - 1,
                               [("actT", j), ("slot", si)], [("ps", bank)])
                        hv = h_sb[:, t, q * 512:(q + 1) * 512]
                        tt(hv, ps(bank), hv, ALU.add, [("ps", bank), ("h", t, q)], [("h", t, q)])
            else:
                S_.phase = "P7"
                ws = []
                for half in range(2):
                    off = (cbase * 4 + 2 * half * nch) * 512
                    si = load_slot(wdn_d[:, off:off + 2 * nch * 512], 2 * nch * 512)
                    ws.append((si, slots[si][:, 0:2 * nch * 512].rearrange("p (q j c) -> p q j c", q=2, c=512)))
                for t in range(8):
                    for q in range(4):
                        si, w = ws[q // 2]
                        bank = 6 + dacc[0] % 2
                        dacc[0] += 1
                        for j in range(nch):
                            mm(ps(bank), actT[:, j, t * 128:(t + 1) * 128], w[:, q % 2, j, :], j == 0, j == nch - 1,
                               [("actT", j), ("slot", si)], [("ps", bank)])
                        hv = h_sb[:, t, q * 512:(q + 1) * 512]
                        tt(hv, ps(bank), hv, ALU.add, [("ps", bank), ("h", t, q)], [("h", t, q)])
                    if t >= 1:
                        final_tile(t - 1)
                final_tile(7)
        S_.op("sp", None, [("y", t) for t in range(8)] + [k for k in S_.lastw if k[0] == "dbg"], [])

        for e in ENGS:
            cnt = 0
            for o_ in S_.per[e]:
                if o_.signal and not o_.stream:
                    cnt += 1
                    o_.sigval = cnt
        ngen = {e: (max([o_.sigval for o_ in S_.per[e]] + [0]) // SEM_GEN) + 1 for e in ENGS}
        eng_sems = {e: [es.enter_context(nc.semaphore(f"s_{e}_{g}")) for g in range(ngen[e])] for e in ENGS}
        stream_sems = {s: es.enter_context(nc.semaphore(f"d_{s}")) for s in S_.stream_cnt}
        block = es.enter_context(nc.Block())

        def emit(engname, eng):
            seen = {}
            for o_ in S_.per[engname]:
                for d in o_.deps:
                    if d.stream:
                        k = ("d", d.stream)
                        val = d.sval
                        sem = stream_sems[d.stream]
                    else:
                        g = (d.sigval - 1) // SEM_GEN
                        k = ("e", d.eng, g)
                        val = d.sigval - g * SEM_GEN
                        sem = eng_sems[d.eng][g]
                        if seen.get(("e", d.eng, g + 1), 0) > 0:
                            continue
                    if seen.get(k, 0) >= val:
                        continue
                    seen[k] = val
                    eng.wait_ge(sem, val)
                if o_.fn is None:
                    continue
                ins = o_.fn(eng)
                if o_.stream:
                    ins.then_inc(stream_sems[o_.stream], 16)
                elif o_.signal:
                    g = (o_.sigval - 1) // SEM_GEN
                    ins.then_inc(eng_sems[o_.eng][g], 1)

        @block.tensor
        def _(e):
            emit("pe", e)

        @block.scalar
        def _(e):
            emit("act", e)

        @block.vector
        def _(e):
            emit("dve", e)

        @block.gpsimd
        def _(e):
            emit("pool", e)

        @block.sync
        def _(e):
            emit("sp", e)

    nc._sched = S_
    return nc


def _const_tables(half):
    perm = np.arange(S) if half == 0 else (S - 1 - np.arange(S))
    inv_freq = (np.float32(10000.0) ** (-np.arange(32, dtype=np.float32) / np.float32(32))).astype(np.float32)
    row = (perm // 64).astype(np.float32)
    col = (perm % 64).astype(np.float32)
    ar = row[:, None] * inv_freq[None, :]
    ac = col[:, None] * inv_freq[None, :]
    cr, sr = np.cos(ar).astype(np.float32), np.sin(ar).astype(np.float32)
    cc, sc_ = np.cos(ac).astype(np.float32), np.sin(ac).astype(np.float32)
    rope = np.concatenate([cr, cr, cc, cc, -sr, sr, -sc_, sc_], axis=1).astype(np.float32)
    ps_ = perm.astype(np.int64)
    pk_ = perm[:NQ].astype(np.int64)
    m = (ps_[:, None] * pk_[None, :]) % S
    ang = 2.0 * np.pi * m.astype(np.float64) / S
    ct = (np.cos(ang) / np.sqrt(S))
    st = (np.sin(ang) / np.sqrt(S))
    dft = np.zeros((5, 128, 2, 16, 256), dtype=np.float32)
    for gi in range(5):
        k0 = gi * 256
        n = 256 if gi < 4 else 1
        for a, tab in enumerate((ct, st)):
            blk = tab[:, k0:k0 + n].reshape(16, 128, n)
            dft[gi, :, a, :, :n] = blk.transpose(1, 0, 2)
    dft = dft.reshape(5, 128, 2 * 16 * 256).astype(ml_dtypes.bfloat16)
    return rope, dft


def _channel_dft():
    c = np.arange(128)
    ang = 2.0 * np.pi * ((c[:, None] * c[None, :]) % 128) / 128.0
    cc = np.cos(ang) / np.sqrt(128.0)
    ns = -np.sin(ang) / np.sqrt(128.0)
    return np.concatenate([cc, ns], axis=1).astype(ml_dtypes.bfloat16)


def _tile_rows(w, ncols_blk):
    K, N = w.shape
    nk = K // 128
    nb = N // ncols_blk
    a = w.reshape(nk, 128, nb, ncols_blk).transpose(1, 2, 0, 3)
    return np.ascontiguousarray(a).reshape(128, nb * nk * ncols_blk)


def prep_inputs(x, norm1_g, w_in, q_norm_g, k_norm_g, w_fmix, attn_out_g, fourier_out_g, w_out, norm2_g,
                w_up, conv_w, conv_b, w_down, final_g):
    f32 = np.float32
    x = np.asarray(x, f32)
    w_in0 = np.asarray(w_in, f32)[0]
    w_out0 = np.asarray(w_out, f32)[0]
    w_up0 = np.asarray(w_up, f32)[0]
    w_dn0 = np.asarray(w_down, f32)[0]
    conv_w0 = np.asarray(conv_w, f32)[0]
    conv_b0 = np.asarray(conv_b, f32)[0]
    win_t = _tile_rows(w_in0, 512)
    wout_t = _tile_rows(w_out0, 512)
    g = w_up0[:, :DFF].reshape(16, 128, NCH, 128)
    v = w_up0[:, DFF:].reshape(16, 128, NCH, 128)
    gv = np.stack([g, v], axis=3)
    gv = gv.reshape(16, 128, 22, 2, 2, 128)
    wup_t = np.ascontiguousarray(gv.transpose(1, 2, 0, 3, 4, 5)).reshape(128, 22 * 8192)
    parts = []
    wd = w_dn0.reshape(NCH, 128, 4, 512)
    for (cbase, nch) in FFB:
        for q in range(4):
            parts.append(np.ascontiguousarray(wd[cbase:cbase + nch, :, q, :].transpose(1, 0, 2)).reshape(128, nch * 512))
    wdn_t = np.concatenate(parts, axis=1)
    wf_t = np.ascontiguousarray(np.asarray(w_fmix, f32)[0].transpose(1, 0, 2)).reshape(128, 1024)
    cb_t = np.ascontiguousarray(conv_b0.reshape(88, 128).T)
    ccs = _channel_dft()
    ident = np.eye(128, dtype=f32).astype(ml_dtypes.bfloat16)
    tabs = [_const_tables(0), _const_tables(1)]
    common = {
        "ccs": ccs, "ident": ident,
        "g1": np.asarray(norm1_g, f32)[0], "g2": np.asarray(norm2_g, f32)[0], "gF": np.asarray(final_g, f32),
        "gaT": np.ascontiguousarray(np.asarray(attn_out_g, f32)[0].reshape(8, 128).T), "gf": np.asarray(fourier_out_g, f32)[0],
        "gq": np.asarray(q_norm_g, f32)[0], "gk": np.asarray(k_norm_g, f32)[0],
        "cb": cb_t, "wf": wf_t, "win": win_t, "wout": wout_t, "wup": wup_t, "wdn": wdn_t,
    }
    in_maps = []
    for c in range(8):
        b, half = c // 2, c % 2
        xl = x[b] if half == 0 else x[b][::-1]
        cwc = conv_w0 if half == 0 else conv_w0[::-1]
        cw_t = np.ascontiguousarray(cwc.reshape(3, 88, 128).transpose(2, 1, 0)).reshape(128, 88 * 3)
        m = dict(common)
        m["x_loc"] = np.ascontiguousarray(xl)
        m["rope"] = tabs[half][0]
        m["dft"] = tabs[half][1]
        m["cw"] = cw_t
        in_maps.append(m)
    return in_maps


_NC_CACHE = {}


def kernel(**inputs):
    in_maps = prep_inputs(**inputs)
    if "nc" not in _NC_CACHE:
        _NC_CACHE["nc"] = build_program()
    nc = _NC_CACHE["nc"]
    res = run_bass_kernel_spmd(nc, in_maps, core_ids=list(range(8)))
    out = np.empty((4, S, D), dtype=np.float32)
    for c in range(8):
        b, half = c // 2, c % 2
        y = np.asarray(res.results[c]["y"], dtype=np.float32)
        if half == 0:
            out[b, :T] = y
        else:
            out[b, T:] = y[::-1]
    return out
```

```python
from contextlib import ExitStack

import numpy as np
import ml_dtypes

import concourse.bass as bass
import concourse.mybir as mybir
from concourse.bass_utils import run_bass_kernel_spmd

F32 = mybir.dt.float32
BF16 = mybir.dt.bfloat16
AF = mybir.ActivationFunctionType
ALU = mybir.AluOpType
AX = mybir.AxisListType

D = 2048
S = 2048
T = 1024
NQ = T + 1
HD = 128
NH = 8
NKV = 2
DFF = 5632
NCH = DFF // 128
EPS = 1e-6
FFB = [(0, 12), (12, 12), (24, 12), (36, 8)]
SEM_GEN = 3000
TAP_UTE = False

ENGS = ["pe", "act", "dve", "pool", "sp"]


class Op:
    __slots__ = ("eng", "fn", "stream", "sval", "signal", "sigval", "gidx", "deps", "phase")


class Sched:
    def __init__(self):
        self.ops = []
        self.per = {e: [] for e in ENGS}
        self.lastw = {}
        self.readers = {}
        self.touch = {}
        self.inherit = {}
        self.stream_cnt = {}
        self.phase = ""

    @staticmethod
    def _k(o):
        return ("d", o.stream) if o.stream else ("e", o.eng)

    def op(self, eng, fn, reads=(), writes=(), stream=None):
        o = Op()
        o.eng = eng
        o.fn = fn
        o.stream = stream
        o.signal = False
        o.sigval = 0
        o.sval = 0
        o.gidx = len(self.ops)
        o.phase = self.phase
        if stream:
            self.stream_cnt[stream] = self.stream_cnt.get(stream, 0) + 1
            o.sval = 16 * self.stream_cnt[stream]
        deps = {}

        def add(d):
            k = self._k(d)
            if k not in deps or deps[k].gidx < d.gidx:
                deps[k] = d

        reads = list(reads)
        writes = list(writes)
        for key in reads + writes:
            if key not in self.lastw and key not in self.readers:
                for d in self.inherit.get(key[0], {}).values():
                    add(d)
        for key in reads:
            w = self.lastw.get(key)
            if w is not None:
                add(w)
        for key in writes:
            w = self.lastw.get(key)
            if w is not None and not (key[0] == "junkq" and w.eng == eng and not w.stream):
                add(w)
            for r in self.readers.get(key, {}).values():
                add(r)
        for key in reads:
            self.readers.setdefault(key, {})[self._k(o)] = o
        for key in writes:
            self.lastw[key] = o
            self.readers[key] = {}
        for key in reads + writes:
            self.touch.setdefault(key[0], {})[self._k(o)] = o
        o.deps = []
        for d in deps.values():
            if (not d.stream) and d.eng == "pe" and eng == "pe" and not stream:
                continue
            o.deps.append(d)
            if not d.stream:
                d.signal = True
        self.ops.append(o)
        self.per[eng].append(o)
        return o

    def alias(self, new_name, old_names):
        m = self.inherit.setdefault(new_name, {})
        for n in old_names:
            for k, d in self.touch.get(n, {}).items():
                if k not in m or m[k].gidx < d.gidx:
                    m[k] = d
            for k, d in self.inherit.get(n, {}).items():
                if k not in m or m[k].gidx < d.gidx:
                    m[k] = d


class Arena:
    def __init__(self, tensor, sched, nbytes):
        self.t = tensor
        self.s = sched
        self.nbytes = nbytes
        self.live = []

    def alloc(self, name, off, nbytes, dtype, shape_str=None, **dims):
        assert off % 4 == 0 and off + nbytes <= self.nbytes, (name, off, nbytes)
        b0, b1 = off, off + nbytes
        old = sorted(set(n for (n, a0, a1) in self.live if a0 < b1 and b0 < a1))
        self.live.append((name, b0, b1))
        if old:
            self.s.alias(name, old)
        w0 = off // 4
        w1 = (off + nbytes + 3) // 4
        ap = self.t[:, w0:w1]
        if dtype != F32:
            ap = ap.bitcast(dtype)
        if shape_str:
            ap = ap.rearrange(shape_str, **dims)
        return ap


def build_program(debug=False):
    nc = bass.Bass("TRN2", target_bir_lowering=False)
    S_ = Sched()

    def dram(name, shape, dt=F32, kind="ExternalInput"):
        return nc.dram_tensor(name, list(shape), dt, kind=kind).ap()

    x_d = dram("x_loc", [S, D])
    rope_d = dram("rope", [S, 256])
    dft_d = dram("dft", [5, 128, 2 * 16 * 256], BF16)
    ccs_d = dram("ccs", [128, 256], BF16)
    ident_d = dram("ident", [128, 128], BF16)
    g1_d = dram("g1", [D])
    g2_d = dram("g2", [D])
    gF_d = dram("gF", [D])
    ga_d = dram("gaT", [128, 8])
    gf_d = dram("gf", [1024])
    gq_d = dram("gq", [128])
    gk_d = dram("gk", [128])
    cw_d = dram("cw", [128, 88 * 3])
    cb_d = dram("cb", [128, 88])
    wf_d = dram("wf", [128, 1024])
    win_d = dram("win", [128, 5 * 8192])
    wout_d = dram("wout", [128, 4 * 8192])
    wup_d = dram("wup", [128, 22 * 8192])
    wdn_d = dram("wdn", [128, 44 * 2048])
    y_d = dram("y", [T, D], F32, kind="ExternalOutput")

    ARENA_BYTES = 212480
    es = ExitStack()
    with es:
        arena_t = es.enter_context(nc.sbuf_tensor("arena", [128, ARENA_BYTES // 4], F32))
        psum_t = es.enter_context(nc.psum_tensor("psum", [128, 4096], F32))
        A = Arena(arena_t, S_, ARENA_BYTES)

        R_SLOT = 0
        R_F = 49152
        R_Q = 81920
        R_MIX = 114944
        R_A = 147840
        R_C = 207744

        slots = [A.alloc("slot", R_SLOT + i * 16384, 16384, BF16) for i in range(3)]
        ident = A.alloc("ident", R_C, 256, BF16)
        cw = A.alloc("cw", R_C + 256, 1056, F32)
        cb = A.alloc("cb", R_C + 1312, 352, F32)
        stat = A.alloc("stat", R_C + 1664, 512, F32)
        epsc = A.alloc("epsc", R_C + 2176, 4, F32)
        junkq = A.alloc("junkq", R_C + 2184, 2048, mybir.dt.float8e4)
        JK = [("junkq", 0)]

        def ps(bank, n=512, off=0):
            return psum_t[:, bank * 512 + off: bank * 512 + off + n]

        def ps_bf(bank, nbanks=1):
            return psum_t[:, bank * 512:(bank + nbanks) * 512].bitcast(BF16)

        def mm(out, lhsT, rhs, start, stop, reads, writes):
            return S_.op("pe", lambda e: e.matmul(out, lhsT, rhs, start=start, stop=stop), reads, writes)

        def tr(out, in_, idn, reads, writes):
            return S_.op("pe", lambda e: e.transpose(out, in_, idn), reads, writes)

        def act(out, in_, func, reads, writes, bias=None, scale=None, accum=None, sat=None):
            kw = {}
            if sat is not None:
                kw["saturate"] = sat
            if bias is not None:
                kw["bias"] = bias
            if scale is not None:
                kw["scale"] = scale
            if accum is not None:
                kw["accum_out"] = accum
            return S_.op("act", lambda e: e.activation(out, in_, func, **kw), reads, writes)

        def dve(fn, reads, writes):
            return S_.op("dve", fn, reads, writes)

        def tt(out, in0, in1, op, reads, writes):
            return dve(lambda e: e.tensor_tensor(out, in0, in1, op), reads, writes)

        def stt(out, in0, scalar, in1, op0, op1, reads, writes):
            return dve(lambda e: e.scalar_tensor_tensor(out, in0, scalar, in1, op0, op1), reads, writes)

        def tsc(out, in0, s1, s2, op0, op1, reads, writes):
            if op1 is None:
                return dve(lambda e: e.tensor_scalar(out, in0, s1, s2, op0), reads, writes)
            return dve(lambda e: e.tensor_scalar(out, in0, s1, s2, op0, op1), reads, writes)

        def red(out, in_, reads, writes):
            return dve(lambda e: e.tensor_reduce(out, in_, AX.X, ALU.add), reads, writes)

        def recip(out, in_, reads, writes):
            return dve(lambda e: e.reciprocal(out, in_), reads, writes)

        def copy_any(which, out, in_, reads, writes):
            if which % 2 == 0:
                return act(out, in_, AF.Copy, reads, writes)
            return dve(lambda e: e.tensor_copy(out, in_), reads, writes)

        def dma(eng, out, in_, reads, writes, stream, **kw):
            return S_.op(eng, lambda e: e.dma_start(out=out, in_=in_, **kw), reads, writes, stream=stream)

        stat_ctr = [0]

        def stat_slot(n=1):
            i = stat_ctr[0]
            if i % 128 + n > 128:
                i += 128 - i % 128
            stat_ctr[0] = i + n
            c = i % 128
            return stat[:, c:c + n], ("stat", c, c + n - 1)

        def stat_keys(k):
            return [("stat", j) for j in range(k[1], k[2] + 1)]

        def rstd_from_ss(ss, sskeys, rows, ncols, inv_n):
            r, rk = stat_slot(ncols)
            rkeys = stat_keys(rk)
            act(r[:rows], ss[:rows], AF.Sqrt, sskeys + [("epsc", 0)], rkeys, bias=epsc[:rows], scale=inv_n)
            recip(r[:rows], r[:rows], rkeys, rkeys)
            return r, rkeys

        slot_ctr = [0]

        def load_slot(src_ap, ncols, after=()):
            i = slot_ctr[0] % 3
            slot_ctr[0] += 1
            dma("pool", slots[i][:, 0:ncols], src_ap, list(after), [("slot", i)], f"slot{i}",
                max_dma_last_dim=8192)
            return i

        def tap(name, ap, names):
            if not debug:
                return
            shape = [int(s) for s in ap.shape]
            dd = nc.dram_tensor("dbg_" + name, shape, ap.dtype, kind="ExternalOutput").ap()
            reads = [k for k in S_.lastw if k[0] in names]
            dma("sp", dd, ap, reads, [("dbg", name)], "dbg_" + name)

        dma("sp", ident, ident_d, [], [("ident", 0)], "c_ident")
        dma("sp", cw, cw_d, [], [("cw", 0)], "c_cw")
        dma("sp", cb, cb_d, [], [("cb", 0)], "c_cb")
        KI = [("ident", 0)]

        f_tm = A.alloc("f_tm", R_F, 32768, BF16, "p (s c) -> p s c", c=1024)
        qT = A.alloc("qT", R_Q, 16416, BF16, "p (h t) -> p h t", t=1026)
        kT = A.alloc("kT", R_Q + 16416, 8192, BF16, "p (h t) -> p h t", t=2048)
        Vaug = A.alloc("V", R_Q + 24608, 8320, BF16, "p (s h c) -> p s h c", h=2, c=130)
        uT = A.alloc("uT", R_A, 32768, BF16, "p (k t) -> p k t", t=1024)
        o = R_A + 32768
        sq = A.alloc("sq", o, 2048, F32); o += 2048
        qn = A.alloc("qn", o, 2048, F32); o += 2048
        t1 = A.alloc("t1", o, 2048, F32); o += 2048
        t2 = A.alloc("t2", o, 2048, F32); o += 2048
        qr = [A.alloc("qr", o + i * 1024, 1024, BF16) for i in range(3)]; o += 3072
        ropet = [A.alloc("ropet", o + i * 1024, 1024, F32) for i in range(2)]; o += 2048
        gq_bc = A.alloc("gq_bc", o, 512, F32); o += 512
        gk_bc = A.alloc("gk_bc", o, 512, F32); o += 512
        xs = [A.alloc("xs", R_MIX + i * 8192, 8192, F32) for i in range(2)] + [A.alloc("xs", R_A + 49152, 8192, F32)]
        g1_bc = A.alloc("g1_bc", R_MIX + 16384, 8192, F32)
        u_tm = [A.alloc("u_tm", R_MIX + 24576 + i * 4096, 4096, BF16) for i in range(2)]

        dve(lambda e: e.memset(epsc, EPS), [], [("epsc", 0)])
        dma("sp", g1_bc, g1_d.partition_broadcast(128), [], [("g1_bc", 0)], "c_g1")
        dma("sp", gq_bc, gq_d.partition_broadcast(128), [], [("gq_bc", 0)], "c_gq")
        dma("sp", gk_bc, gk_d.partition_broadcast(128), [], [("gk_bc", 0)], "c_gk")
        dve(lambda e: e.memset(Vaug[:, :, :, 128:129], 1.0), [], [("V", 99)])

        cpy = [0]
        rope_ctr = [0]
        psacc_ctr = [0]
        pstr_ctr = [0]
        qr_ctr = [0]

        def norm_tile(src, srckeys, rows, gbc, gkey, dst, dstkeys):
            ss, sk = stat_slot(1)
            sskeys = stat_keys(sk)
            act(junkq[:rows], src[:rows], AF.Square, srckeys, JK + sskeys, accum=ss[:rows], sat=False)
            r, rkeys = rstd_from_ss(ss, sskeys, rows, 1, 1.0 / D)
            stt(dst[:rows], src[:rows], r[:rows, 0:1], gbc[:rows], ALU.mult, ALU.mult,
                srckeys + rkeys + [gkey], dstkeys)

        def transposes16(src, srckeys, rows, dstT, dstkeys_fn, col0, bankpair, which):
            pb = ps_bf(bankpair, 2)
            pk = [("ps", bankpair), ("ps", bankpair + 1)]
            for j in range(16):
                tr(pb[:, j * 128: j * 128 + rows], src[:rows, j * 128:(j + 1) * 128], ident[:rows, :rows],
                   srckeys + KI, pk)
            copy_any(which, dstT[:, :, col0:col0 + rows],
                     pb.rearrange("p (k t) -> p k t", t=128)[:, :, 0:rows], pk, dstkeys_fn)

        def qk_post(psrc, pkeys, rows, nheads, gbc, gbkey, rtile, rkey, dstT, dst_h0, col0, dstkeys):
            n = nheads * 128
            act(sq[:rows, :n], psrc[:rows, :n], AF.Square, pkeys, [("sq", 0)])
            ss, sk = stat_slot(nheads)
            sskeys = stat_keys(sk)
            red(ss[:rows], sq[:rows, :n].rearrange("p (h d) -> p h d", d=128), [("sq", 0)], sskeys)
            r, rkeys = rstd_from_ss(ss, sskeys, rows, nheads, 1.0 / HD)
            for h in range(nheads):
                stt(qn[:rows, h * 128:(h + 1) * 128], psrc[:rows, h * 128:(h + 1) * 128],
                    r[:rows, h:h + 1], gbc[:rows], ALU.mult, ALU.mult,
                    pkeys + rkeys + [gbkey], [("qn", 0)])
            qn3 = qn[:rows, :n].rearrange("p (h d) -> p h d", d=128)
            cosb = rtile[:rows, 0:128].unsqueeze(1).to_broadcast([rows, nheads, 128])
            tt(t1[:rows, :n].rearrange("p (h d) -> p h d", d=128), qn3, cosb, ALU.mult,
               [("qn", 0), rkey], [("t1", 0)])
            qn5 = qn[:rows, :n].rearrange("p (h a b c) -> p h a b c", a=2, b=2, c=32)
            t25 = t2[:rows, :n].rearrange("p (h a b c) -> p h a b c", a=2, b=2, c=32)
            sin5 = rtile[:rows, 128:256].rearrange("p (a b c) -> p a b c", a=2, b=2, c=32)
            for blk in range(2):
                sb = sin5[:, :, blk, :].unsqueeze(1).to_broadcast([rows, nheads, 2, 32])
                tt(t25[:, :, :, blk, :], qn5[:, :, :, 1 - blk, :], sb, ALU.mult,
                   [("qn", 0), rkey], [("t2", blk)])
            qi = qr_ctr[0] % 3
            qr_ctr[0] += 1
            qrb = qr[qi]
            tt(qrb[:rows, :n], t1[:rows, :n], t2[:rows, :n], ALU.add,
               [("t1", 0), ("t2", 0), ("t2", 1)], [("qr", qi)])

            def tail():
                bank = pstr_ctr[0] % 3
                pstr_ctr[0] += 1
                pb = ps_bf(bank, 1)
                for h in range(nheads):
                    tr(pb[:, h * 128:h * 128 + rows], qrb[:rows, h * 128:(h + 1) * 128], ident[:rows, :rows],
                       [("qr", qi)] + KI, [("ps", bank)])
                copy_any(0, dstT[:, dst_h0:dst_h0 + nheads, col0:col0 + rows],
                         pb[:, 0:n].rearrange("p (h t) -> p h t", t=128)[:, :, 0:rows], [("ps", bank)], dstkeys)
            return tail

        def load_rope(gt):
            i = rope_ctr[0] % 2
            rope_ctr[0] += 1
            dma("sp", ropet[i], rope_d[gt * 128:(gt + 1) * 128, :], [], [("ropet", i)], f"ropet{i}")
            return ropet[i], ("ropet", i)

        pend = []

        def flush(keep):
            while len(pend) > keep:
                pend.pop(0)()

        for pa in range(2):
            if pa == 1 and TAP_UTE:
                tap("uTE", uT, ["uT"])
            S_.phase = f"P1{'AB'[pa]}"
            for t in range(8):
                gt = pa * 8 + t
                b = gt % 2
                xb = gt % 3
                dma("sp", xs[xb], x_d[gt * 128:(gt + 1) * 128, :], [], [("xs", xb)], f"xs{xb}")
                norm_tile(xs[xb], [("xs", xb)], 128, g1_bc, ("g1_bc", 0), u_tm[b], [("u_tm", b)])
                pend.append(lambda t=t, b=b: transposes16(u_tm[b], [("u_tm", b)], 128, uT, [("uT", t)], t * 128,
                                                          2 * (t % 2), t))
                flush(1)
            flush(0)
            S_.phase = f"P2{'AB'[pa]}"
            blocks = [0, 1, 2, 3, 4] if pa == 0 else [2, 3, 4, 0, 1]
            for blk in blocks:
                after = []
                if pa == 0 and blk == 1:
                    after = [("uT", 3)]
                if pa == 0 and blk == 2:
                    after = [("uT", 6)]
                si = load_slot(win_d[:, blk * 8192:(blk + 1) * 8192], 8192, after)
                w = slots[si].rearrange("p (k c) -> p k c", c=512)
                halo_only = (pa == 1 and blk < 2)
                tiles = [0] if halo_only else list(range(8))
                for t in tiles:
                    gt = pa * 8 + t
                    rows = 1 if halo_only else 128
                    bank = 4 + psacc_ctr[0] % 4
                    psacc_ctr[0] += 1
                    pk = [("ps", bank)]
                    for k in range(16):
                        mm(ps(bank)[:rows, :], uT[:, k, t * 128:t * 128 + rows], w[:, k, :], k == 0, k == 15,
                           [("uT", t), ("slot", si)], pk)
                    if blk < 2:
                        rt, rk = load_rope(gt)
                        col0 = 1024 if halo_only else t * 128
                        pend.append(qk_post(ps(bank), pk, rows, 4, gq_bc, ("gq_bc", 0), rt, rk, qT, 4 * blk, col0,
                                            [("qT", blk, gt)]))
                    elif blk == 2:
                        rt, rk = load_rope(gt)
                        act(Vaug[:, gt, :, 0:128], ps(bank)[:, 256:512].rearrange("p (h d) -> p h d", d=128),
                            AF.Copy, pk, [("V", gt)])
                        pend.append(qk_post(ps(bank), pk, 128, 2, gk_bc, ("gk_bc", 0), rt, rk, kT, 0, gt * 128,
                                            [("kT", gt)]))
                    else:
                        fb = blk - 3
                        act(f_tm[:, gt, fb * 512:(fb + 1) * 512], ps(bank), AF.Copy, pk, [("f_tm", gt, fb)])
                    flush(2)
            flush(0)

        tap("qT", qT, ["qT"])
        tap("kT", kT, ["kT"])
        tap("V", Vaug, ["V"])
        tap("f", f_tm, ["f_tm"])
        S_.phase = "P3"
        mixT = A.alloc("mixT", R_MIX, 32832, BF16, "p (k t) -> p k t", t=1026)
        o = R_A
        PT = [A.alloc("PT", o + i * 16384, 16384, BF16, "p (s q) -> p s q", q=512) for i in range(2)]; o += 32768
        O_sb = [A.alloc("O_sb", o + i * 4160, 4160, F32, "p (h c) -> p h c", c=130) for i in range(4)]; o += 16640
        sqa = A.alloc("sqa", o, 2048, BF16); o += 2048
        a_tm = [A.alloc("a_tm", o + i * 2048, 2048, BF16) for i in range(4)]; o += 8192
        gaT = A.alloc("gaT", o, 32, F32); o += 32
        dma("sp", gaT, ga_d, [], [("gaT", 0)], "c_ga")
        pend_atr = []

        scale = 1.0 / float(np.sqrt(HD))
        sbank = [0]
        obank = [0]
        ptc = [0]

        def attn_norm(tl, rows, col0):
            okeys = [("O_sb", tl, h) for h in range(NH)]
            rl, rlk = stat_slot(NH)
            rlkeys = stat_keys(rlk)
            recip(rl[:rows].unsqueeze(2), O_sb[tl][:rows, :, 128:129], okeys, rlkeys)
            act(sqa[:rows, :].rearrange("p (h d) -> p h d", d=128), O_sb[tl][:rows, :, 0:128], AF.Square,
                okeys, [("sqa", 0)])
            ssh, sk = stat_slot(NH)
            sshk = stat_keys(sk)
            red(ssh[:rows], sqa[:rows, :].rearrange("p (h d) -> p h d", d=128), [("sqa", 0)], sshk)
            tt(ssh[:rows], ssh[:rows], rl[:rows], ALU.mult, sshk + rlkeys, sshk)
            tt(ssh[:rows], ssh[:rows], rl[:rows], ALU.mult, sshk + rlkeys, sshk)
            ss1, sk1 = stat_slot(1)
            ss1k = stat_keys(sk1)
            red(ss1[:rows], ssh[:rows], sshk, ss1k)
            atm = a_tm[tl]

            def part_b():
                r, rkeys = rstd_from_ss(ss1, ss1k, rows, 1, 1.0 / 1024.0)
                fac, fk = stat_slot(NH)
                fkeys = stat_keys(fk)
                tsc(fac[:rows], rl[:rows], r[:rows, 0:1], None, ALU.mult, None, rlkeys + rkeys, fkeys)
                tt(atm[:rows, :].rearrange("p (h d) -> p h d", d=128), O_sb[tl][:rows, :, 0:128],
                   fac[:rows].unsqueeze(2).to_broadcast([rows, NH, 128]), ALU.mult,
                   okeys + fkeys, [("a_tm", tl)])

            def tail():
                pb = ps_bf(7, 1)
                for h in range(NH):
                    tr(pb[:, h * 128:h * 128 + rows], atm[:rows, h * 128:(h + 1) * 128], ident[:rows, :rows],
                       [("a_tm", tl)] + KI, [("ps", 7)])
                tt(mixT[:, 0:8, col0:col0 + rows], pb.rearrange("p (h t) -> p h t", t=128)[:, :, 0:rows],
                   gaT[:, :].unsqueeze(2).to_broadcast([128, NH, rows]), ALU.mult,
                   [("ps", 7), ("gaT", 0)], [("mixT", 0, col0 // 128)])
            return part_b, tail

        prev = None
        pend_norm = []
        pend_b = []
        todo = {}
        for (q0, n) in [(0, 512), (512, 512), (None, 0)]:
            heads = range(NH) if q0 is not None else [None]
            for h in heads:
                if h is not None:
                    kv = h // 4
                    pi = ptc[0] % 2
                    ptc[0] += 1
                for sc in range(16):
                    if h is not None:
                        bank = sbank[0] % 3
                        sbank[0] += 1
                        mm(ps(bank)[:, :n], kT[:, kv, sc * 128:(sc + 1) * 128], qT[:, h, q0:q0 + n], True, True,
                           [("kT", sc)] + [("qT", h // 4, q0 // 128 + j) for j in range(4)], [("ps", bank)])
                        act(PT[pi][:, sc, :n], ps(bank)[:, :n], AF.Exp, [("ps", bank)], [("PT", pi, sc)],
                            scale=scale)
                    if prev is not None:
                        ppi, ph = prev
                        for tl in range(4):
                            mm(ps(3 + tl)[:, 0:129], PT[ppi][:, sc, tl * 128:(tl + 1) * 128],
                               Vaug[:, sc, ph // 4, 0:129], sc == 0, sc == 15,
                               [("PT", ppi, sc), ("V", sc), ("V", 99)], [("ps", 3 + tl)])
                if prev is not None:
                    ppi, ph = prev
                    if ph == 0:
                        while pend_b:
                            pend_b.pop(0)()
                    for tl in range(4):
                        copy_any(1, O_sb[tl][:, ph, 0:129], ps(3 + tl)[:, 0:129], [("ps", 3 + tl)],
                                 [("O_sb", tl, ph)])
                    if ph == NH - 1:
                        todo.clear()
                        while pend_norm:
                            tl_, pb_, tail_ = pend_norm.pop(0)()
                            pend_b.append(pb_)
                            todo.setdefault(tl_ + 1, []).append(tail_)
                    elif ph in todo:
                        for fn_ in todo.pop(ph):
                            fn_()
                prev = (pi, h) if h is not None else None
                if h == NH - 1:
                    for tl in range(4):
                        pend_norm.append(lambda tl=tl, q0=q0: (tl,) + attn_norm(tl, 128, q0 + tl * 128))
        obank[0] = 0
        pi = ptc[0] % 2
        ptc[0] += 1
        for h in range(NH):
            kv = h // 4
            bank = sbank[0] % 3
            sbank[0] += 1
            for sc in range(16):
                mm(ps(bank)[:, sc:sc + 1], kT[:, kv, sc * 128:(sc + 1) * 128], qT[:, h, 1024:1025], True, True,
                   [("kT", sc), ("qT", h // 4, 8)], [("ps", bank)])
            act(PT[pi][:, h, 0:16], ps(bank)[:, 0:16], AF.Exp, [("ps", bank)], [("PT", pi, h)], scale=scale)
        while pend_b:
            pend_b.pop(0)()
        for key_ in sorted(todo):
            for fn_ in todo[key_]:
                fn_()
        todo.clear()
        for h in range(NH):
            kv = h // 4
            bank = 3 + obank[0] % 4
            obank[0] += 1
            for sc in range(16):
                mm(ps(bank)[:1, 0:129], PT[pi][:, h, sc:sc + 1], Vaug[:, sc, kv, 0:129], sc == 0, sc == 15,
                   [("PT", pi, h), ("V", sc), ("V", 99)], [("ps", bank)])
            copy_any(1, O_sb[0][:1, h, 0:129], ps(bank)[:1, 0:129], [("ps", bank)], [("O_sb", 0, h)])
        pb_, tail_ = attn_norm(0, 1, 1024)
        pb_()
        tail_()

        S_.phase = "P4"
        ZT = A.alloc("ZT", R_Q, 32832, BF16, "p (a g t) -> p a g t", a=2, g=8)
        o = R_A
        CS = [A.alloc("CS", o + i * 16384, 16384, BF16, "p (a s k) -> p a s k", a=2, s=16) for i in range(2)]
        o += 32768
        gf_bc = A.alloc("gf_bc", o, 4096, F32); o += 4096
        f_n2 = [A.alloc("f_n", o + i * 2048, 2048, BF16) for i in range(2)]; o += 4096
        AB = A.alloc("AB", o, 4096, BF16, "p (g a d) -> p g a d", a=2, d=128); o += 4096
        ccs = A.alloc("ccs", o, 512, BF16, "p (a c) -> p a c", a=2); o += 512
        wfb = A.alloc("wfb", o, 2048, BF16); o += 2048
        dma("sp", gf_bc, gf_d.partition_broadcast(128), [], [("gf_bc", 0)], "c_gf")
        dma("sp", ccs, ccs_d.rearrange("p (a c) -> p a c", a=2), [], [("ccs", 0)], "c_ccs")
        dma("pool", wfb, wf_d, [], [("wfb", 0)], "c_wf")
        for g in range(8):
            for a in range(2):
                mm(ps(0)[:, a * 128:(a + 1) * 128], ccs[:, a, :], wfb[:, g * 128:(g + 1) * 128], True, True,
                   [("ccs", 0), ("wfb", 0)], [("ps", 0)])
            cpy[0] += 1
            copy_any(cpy[0], AB[:, g, :, :], ps(0)[:, 0:256].rearrange("p (a d) -> p a d", d=128), [("ps", 0)],
                     [("AB", g)])
        zb = [0]
        qgroups = [(i * 256, 256) for i in range(4)] + [(1024, 1)]
        pend_out = []
        pend_tr = []

        def fourier_out(gi, tl, rows, col0):
            b0 = 4 + 2 * tl
            pk = [("ps", b0), ("ps", b0 + 1)]
            pout = psum_t[:, b0 * 512:(b0 + 2) * 512]
            for g in range(8):
                for a in range(2):
                    mm(pout[:rows, g * 128:(g + 1) * 128], ZT[:, a, g, col0:col0 + rows], AB[:, g, a, :],
                       a == 0, a == 1, [("ZT", a, g, gi), ("AB", g)], [("ps", b0 + g // 4)])
            ss, sk = stat_slot(1)
            sskeys = stat_keys(sk)
            act(junkq[:rows, 0:1024], pout[:rows], AF.Square, pk, JK + sskeys, accum=ss[:rows], sat=False)
            r, rkeys = rstd_from_ss(ss, sskeys, rows, 1, 1.0 / 1024.0)
            fn = f_n2[tl]
            stt(fn[:rows], pout[:rows], r[:rows, 0:1], gf_bc[:rows], ALU.mult, ALU.mult,
                pk + rkeys + [("gf_bc", 0)], [("f_n", tl)])

            def tail():
                pb = ps_bf(b0, 1)
                for h in range(8):
                    tr(pb[:, h * 128:h * 128 + rows], fn[:rows, h * 128:(h + 1) * 128], ident[:rows, :rows],
                       [("f_n", tl)] + KI, [("ps", b0)])
                cpy[0] += 1
                copy_any(cpy[0], mixT[:, 8:16, col0:col0 + rows],
                         pb.rearrange("p (h t) -> p h t", t=128)[:, :, 0:rows], [("ps", b0)],
                         [("mixT", 1, col0 // 128)])
            pend_tr.append(tail)

        for gi, (k0, n) in enumerate(qgroups):
            ntile = 2 if n == 256 else 1
            rows = 128 if n == 256 else 1
            ci = gi % 2
            dma("sp", CS[ci], dft_d[gi].rearrange("p (a s k) -> p a s k", a=2, s=16), [], [("CS", ci)], f"CS{ci}")
            for g in range(8):
                bz = 2 * (zb[0] % 2)
                zb[0] += 1
                for sc in range(16):
                    for a in range(2):
                        mm(ps(bz + a)[:, :n], f_tm[:, sc, g * 128:(g + 1) * 128], CS[ci][:, a, sc, :n],
                           sc == 0, sc == 15, [("f_tm", sc, g // 4), ("CS", ci)], [("ps", bz + a)])
                for a in range(2):
                    cpy[0] += 1
                    copy_any(cpy[0], ZT[:, a, g, k0:k0 + n], ps(bz + a)[:, :n], [("ps", bz + a)],
                             [("ZT", a, g, gi)])
                if g == 1:
                    while pend_out:
                        pend_out.pop(0)()
                if g == 4:
                    while pend_tr:
                        pend_tr.pop(0)()
            for tl in range(ntile):
                pend_out.append(lambda gi=gi, tl=tl, rows=rows, k0=k0: fourier_out(gi, tl, rows, k0 + tl * 128))
        while pend_out:
            pend_out.pop(0)()
        while pend_tr:
            pend_tr.pop(0)()

        tap("mixT", mixT, ["mixT"])
        S_.phase = "P5"
        h_sb = A.alloc("h", R_F, 65536, F32, "p (t d) -> p t d", d=2048)
        o = R_A
        u2T = A.alloc("u2T", o, 32832, BF16, "p (k t) -> p k t", t=1026); o += 32832
        g2_bc = A.alloc("g2_bc", o, 8192, F32); o += 8192
        u2_tm = [A.alloc("u2_tm", o + i * 4096, 4096, BF16) for i in range(2)]; o += 8192
        h_halo = A.alloc("h", o, 8192, F32); o += 8192
        dma("sp", g2_bc, g2_d.partition_broadcast(128), [], [("g2_bc", 0)], "c_g2")
        for t in range(8):
            dma("sp", h_sb[:, t, :], x_d[t * 128:(t + 1) * 128, :], [], [("h", t, q) for q in range(4)], f"hx{t}")
        dma("sp", h_halo[0:1, :], x_d[1024:1025, :], [], [("h", 8, q) for q in range(4)], "hx8")

        def htile(t):
            return h_sb[:, t, :] if t < 8 else h_halo

        wacc = [0]
        pend = []

        def do_norm2(t):
            rows = 128 if t < 8 else 1
            hk = [("h", t, q) for q in range(4)]
            b = t % 2
            norm_tile(htile(t), hk, rows, g2_bc, ("g2_bc", 0), u2_tm[b], [("u2_tm", b)])
            pend.append(lambda: transposes16(u2_tm[b], [("u2_tm", b)], rows, u2T, [("u2T", t)], t * 128,
                                             4 + 2 * (t % 2), t))

        for cbk in range(4):
            si = load_slot(wout_d[:, cbk * 8192:(cbk + 1) * 8192], 8192)
            w = slots[si].rearrange("p (k c) -> p k c", c=512)
            for t in range(9):
                rows = 128 if t < 8 else 1
                col0 = t * 128
                bank = wacc[0] % 4
                wacc[0] += 1
                for fc in range(16):
                    mm(ps(bank)[:rows, :], mixT[:, fc, col0:col0 + rows], w[:, fc, :], fc == 0, fc == 15,
                       [("mixT", fc // 8, t), ("slot", si)], [("ps", bank)])
                hv = htile(t)[:rows, cbk * 512:(cbk + 1) * 512]
                tt(hv, ps(bank)[:rows, :], hv, ALU.add, [("ps", bank), ("h", t, cbk)], [("h", t, cbk)])
                if cbk == 3:
                    if t >= 1:
                        do_norm2(t - 1)
                    while len(pend) > 1:
                        pend.pop(0)()
        do_norm2(8)
        while pend:
            pend.pop(0)()

        tap("h1", h_sb, ["h"])
        tap("u2T", u2T, ["u2T"])
        S_.phase = "P6"
        actT = A.alloc("actT", R_MIX, 32768, BF16, "p (c t) -> p c t", t=1024)
        o = R_A + 32832
        tmpG = A.alloc("tmp", o, 4096, F32); o += 4096
        tmpV = A.alloc("tmp", o, 4096, F32); o += 4096
        sg = A.alloc("sg", o, 4096, F32); o += 4096
        gF_bc = A.alloc("gF_bc", o, 8192, F32); o += 8192
        dma("sp", gF_bc, gF_d.partition_broadcast(128), [], [("gF_bc", 0)], "c_gF")
        u2keys = [("u2T", t) for t in range(9)]
        dacc = [0]

        def final_tile(t):
            hk = [("h", t, q) for q in range(4)]
            ss, sk = stat_slot(1)
            sskeys = stat_keys(sk)
            act(junkq, h_sb[:, t, :], AF.Square, hk, JK + sskeys, accum=ss, sat=False)
            r, rkeys = rstd_from_ss(ss, sskeys, 128, 1, 1.0 / D)
            stt(h_sb[:, t, :], h_sb[:, t, :], r[:, 0:1], gF_bc, ALU.mult, ALU.mult,
                hk + rkeys + [("gF_bc", 0)], hk)
            dma("sp", y_d[t * 128:(t + 1) * 128, :], h_sb[:, t, :], hk, [("y", t)], f"y{t % 2}")

        def aslot(cglob):
            return cglob % 16

        def up_pairblock(blk):
            si = load_slot(wup_d[:, blk * 8192:(blk + 1) * 8192], 8192)
            w = slots[si].rearrange("p (k j c) -> p k j c", j=4, c=128)
            for pi in range(2):
                cglob = blk * 2 + pi
                for gv in range(2):
                    j = 2 * pi + gv
                    b0 = 2 * gv
                    pkm = [("ps", b0), ("ps", b0 + 1)]
                    for k in range(16):
                        mm(ps(b0), w[:, k, j, :], u2T[:, k, 0:512], k == 0, k == 15,
                           [("slot", si)] + u2keys[0:4], [("ps", b0)])
                        mm(ps(b0 + 1), w[:, k, j, :], u2T[:, k, 512:1024], k == 0, k == 15,
                           [("slot", si)] + u2keys[4:8], [("ps", b0 + 1)])
                        mm(ps(4 + gv, 1), w[:, k, j, :], u2T[:, k, 1024:1025], k == 0, k == 15,
                           [("slot", si), ("u2T", 8)], [("ps", 4 + gv)])
                    chunk = cglob if gv == 0 else NCH + cglob
                    w0 = cw[:, chunk * 3 + 0:chunk * 3 + 1]
                    w1 = cw[:, chunk * 3 + 1:chunk * 3 + 2]
                    w2 = cw[:, chunk * 3 + 2:chunk * 3 + 3]
                    bb = cb[:, chunk:chunk + 1]
                    tmp = tmpG if gv == 0 else tmpV
                    tk = [("tmp", gv)]
                    pfull = psum_t[:, b0 * 512:(b0 + 2) * 512]
                    act(tmp, pfull, AF.Identity, pkm + [("cw", 0), ("cb", 0)], tk, bias=bb, scale=w1)
                    stt(tmp[:, 1:1024], pfull[:, 0:1023], w0, tmp[:, 1:1024], ALU.mult, ALU.add,
                        pkm + tk + [("cw", 0)], tk)
                    stt(tmp[:, 0:1023], pfull[:, 1:1024], w2, tmp[:, 0:1023], ALU.mult, ALU.add,
                        pkm + tk + [("cw", 0)], tk)
                    stt(tmp[:, 1023:1024], ps(4 + gv, 1), w2, tmp[:, 1023:1024], ALU.mult, ALU.add,
                        [("ps", 4 + gv)] + tk + [("cw", 0)], tk)
                act(sg, tmpG, AF.Silu, [("tmp", 0)], [("sg", 0)])
                tt(actT[:, aslot(cglob), :], tmpV, sg, ALU.mult, [("tmp", 1), ("sg", 0)],
                   [("actT", aslot(cglob))])

        def down_block(fb):
            cbase, nch = FFB[fb]
            last = (fb == len(FFB) - 1)
            if not last:
                for q in range(4):
                    off = (cbase * 4 + q * nch) * 512
                    si = load_slot(wdn_d[:, off:off + nch * 512], nch * 512)
                    w = slots[si][:, 0:nch * 512].rearrange("p (j c) -> p j c", c=512)
                    for t in range(8):
                        bank = 6 + dacc[0] % 2
                        dacc[0] += 1
                        for j in range(nch):
                            sl = aslot(cbase + j)
                            mm(ps(bank), actT[:, sl, t * 128:(t + 1) * 128], w[:, j, :], j == 0, j == nch - 1,
                               [("actT", sl), ("slot", si)], [("ps", bank)])
                        hv = h_sb[:, t, q * 512:(q + 1) * 512]
                        tt(hv, ps(bank), hv, ALU.add, [("ps", bank), ("h", t, q)], [("h", t, q)])
            else:
                S_.phase = "P7"
                ws = []
                for half in range(2):
                    off = (cbase * 4 + 2 * half * nch) * 512
                    si = load_slot(wdn_d[:, off:off + 2 * nch * 512], 2 * nch * 512)
                    ws.append((si, slots[si][:, 0:2 * nch * 512].rearrange("p (q j c) -> p q j c", q=2, c=512)))
                for t in range(8):
                    for q in range(4):
                        si, w = ws[q // 2]
                        bank = 6 + dacc[0] % 2
                        dacc[0] += 1
                        for j in range(nch):
                            sl = aslot(cbase + j)
                            mm(ps(bank), actT[:, sl, t * 128:(t + 1) * 128], w[:, q % 2, j, :], j == 0,
                               j == nch - 1, [("actT", sl), ("slot", si)], [("ps", bank)])
                        hv = h_sb[:, t, q * 512:(q + 1) * 512]
                        tt(hv, ps(bank), hv, ALU.add, [("ps", bank), ("h", t, q)], [("h", t, q)])
                    if t >= 1:
                        final_tile(t - 1)
                final_tile(7)

        hoisted = set()
        for fb, (cbase, nch) in enumerate(FFB):
            for pbk in range(nch // 2):
                blk = cbase // 2 + pbk
                if blk not in hoisted:
                    up_pairblock(blk)
            if fb + 1 < len(FFB):
                nb = FFB[fb + 1][0] // 2
                up_pairblock(nb)
                hoisted.add(nb)
            down_block(fb)
        S_.op("sp", None, [("y", t) for t in range(8)] + [k for k in S_.lastw if k[0] == "dbg"], [])

        for e in ENGS:
            cnt = 0
            for o_ in S_.per[e]:
                if o_.signal and not o_.stream:
                    cnt += 1
                    o_.sigval = cnt
        ngen = {e: (max([o_.sigval for o_ in S_.per[e]] + [0]) // SEM_GEN) + 1 for e in ENGS}
        eng_sems = {e: [es.enter_context(nc.semaphore(f"s_{e}_{g}")) for g in range(ngen[e])] for e in ENGS}
        stream_sems = {s: es.enter_context(nc.semaphore(f"d_{s}")) for s in S_.stream_cnt}
        block = es.enter_context(nc.Block())

        def emit(engname, eng):
            seen = {}
            for o_ in S_.per[engname]:
                for d in o_.deps:
                    if d.stream:
                        k = ("d", d.stream)
                        val = d.sval
                        sem = stream_sems[d.stream]
                    else:
                        g = (d.sigval - 1) // SEM_GEN
                        k = ("e", d.eng, g)
                        val = d.sigval - g * SEM_GEN
                        sem = eng_sems[d.eng][g]
                        if seen.get(("e", d.eng, g + 1), 0) > 0:
                            continue
                    if seen.get(k, 0) >= val:
                        continue
                    seen[k] = val
                    eng.wait_ge(sem, val)
                if o_.fn is None:
                    continue
                ins = o_.fn(eng)
                if o_.stream:
                    ins.then_inc(stream_sems[o_.stream], 16)
                elif o_.signal:
                    g = (o_.sigval - 1) // SEM_GEN
                    ins.then_inc(eng_sems[o_.eng][g], 1)

        @block.tensor
        def _(e):
            emit("pe", e)

        @block.scalar
        def _(e):
            emit("act", e)

        @block.vector
        def _(e):
            emit("dve", e)

        @block.gpsimd
        def _(e):
            emit("pool", e)

        @block.sync
        def _(e):
            emit("sp", e)

    nc._sched = S_
    return nc


def _const_tables(half):
    perm = np.arange(S) if half == 0 else (S - 1 - np.arange(S))
    inv_freq = (np.float32(10000.0) ** (-np.arange(32, dtype=np.float32) / np.float32(32))).astype(np.float32)
    row = (perm // 64).astype(np.float32)
    col = (perm % 64).astype(np.float32)
    ar = row[:, None] * inv_freq[None, :]
    ac = col[:, None] * inv_freq[None, :]
    cr, sr = np.cos(ar).astype(np.float32), np.sin(ar).astype(np.float32)
    cc, sc_ = np.cos(ac).astype(np.float32), np.sin(ac).astype(np.float32)
    rope = np.concatenate([cr, cr, cc, cc, -sr, sr, -sc_, sc_], axis=1).astype(np.float32)
    ps_ = perm.astype(np.int64)
    pk_ = perm[:NQ].astype(np.int64)
    m = (ps_[:, None] * pk_[None, :]) % S
    ang = 2.0 * np.pi * m.astype(np.float64) / S
    ct = (np.cos(ang) / np.sqrt(S))
    st = (np.sin(ang) / np.sqrt(S))
    dft = np.zeros((5, 128, 2, 16, 256), dtype=np.float32)
    for gi in range(5):
        k0 = gi * 256
        n = 256 if gi < 4 else 1
        for a, tab in enumerate((ct, st)):
            blk = tab[:, k0:k0 + n].reshape(16, 128, n)
            dft[gi, :, a, :, :n] = blk.transpose(1, 0, 2)
    dft = dft.reshape(5, 128, 2 * 16 * 256).astype(ml_dtypes.bfloat16)
    return rope, dft


def _channel_dft():
    c = np.arange(128)
    ang = 2.0 * np.pi * ((c[:, None] * c[None, :]) % 128) / 128.0
    cc = np.cos(ang) / np.sqrt(128.0)
    ns = -np.sin(ang) / np.sqrt(128.0)
    return np.concatenate([cc, ns], axis=1).astype(ml_dtypes.bfloat16)


def _tile_rows(w, ncols_blk):
    K, N = w.shape
    nk = K // 128
    nb = N // ncols_blk
    a = w.reshape(nk, 128, nb, ncols_blk).transpose(1, 2, 0, 3)
    return np.ascontiguousarray(a).reshape(128, nb * nk * ncols_blk)


def prep_inputs(x, norm1_g, w_in, q_norm_g, k_norm_g, w_fmix, attn_out_g, fourier_out_g, w_out, norm2_g,
                w_up, conv_w, conv_b, w_down, final_g):
    f32 = np.float32
    x = np.asarray(x, f32)
    w_in0 = np.asarray(w_in, f32)[0]
    w_out0 = np.asarray(w_out, f32)[0]
    w_up0 = np.asarray(w_up, f32)[0]
    w_dn0 = np.asarray(w_down, f32)[0]
    conv_w0 = np.asarray(conv_w, f32)[0]
    conv_b0 = np.asarray(conv_b, f32)[0]
    win_t = _tile_rows(w_in0, 512)
    wout_t = _tile_rows(w_out0, 512)
    g = w_up0[:, :DFF].reshape(16, 128, NCH, 128)
    v = w_up0[:, DFF:].reshape(16, 128, NCH, 128)
    gv = np.stack([g, v], axis=3)
    gv = gv.reshape(16, 128, 22, 2, 2, 128)
    wup_t = np.ascontiguousarray(gv.transpose(1, 2, 0, 3, 4, 5)).reshape(128, 22 * 8192)
    parts = []
    wd = w_dn0.reshape(NCH, 128, 4, 512)
    for (cbase, nch) in FFB:
        for q in range(4):
            parts.append(np.ascontiguousarray(wd[cbase:cbase + nch, :, q, :].transpose(1, 0, 2)).reshape(128, nch * 512))
    wdn_t = np.concatenate(parts, axis=1)
    wf_t = np.ascontiguousarray(np.asarray(w_fmix, f32)[0].transpose(1, 0, 2)).reshape(128, 1024)
    cb_t = np.ascontiguousarray(conv_b0.reshape(88, 128).T)
    ccs = _channel_dft()
    ident = np.eye(128, dtype=f32).astype(ml_dtypes.bfloat16)
    tabs = [_const_tables(0), _const_tables(1)]
    common = {
        "ccs": ccs, "ident": ident,
        "g1": np.asarray(norm1_g, f32)[0], "g2": np.asarray(norm2_g, f32)[0], "gF": np.asarray(final_g, f32),
        "gaT": np.ascontiguousarray(np.asarray(attn_out_g, f32)[0].reshape(8, 128).T), "gf": np.asarray(fourier_out_g, f32)[0],
        "gq": np.asarray(q_norm_g, f32)[0], "gk": np.asarray(k_norm_g, f32)[0],
        "cb": cb_t, "wf": wf_t, "win": win_t, "wout": wout_t, "wup": wup_t, "wdn": wdn_t,
    }
    in_maps = []
    for c in range(8):
        b, half = c // 2, c % 2
        xl = x[b] if half == 0 else x[b][::-1]
        cwc = conv_w0 if half == 0 else conv_w0[::-1]
        cw_t = np.ascontiguousarray(cwc.reshape(3, 88, 128).transpose(2, 1, 0)).reshape(128, 88 * 3)
        m = dict(common)
        m["x_loc"] = np.ascontiguousarray(xl)
        m["rope"] = tabs[half][0]
        m["dft"] = tabs[half][1]
        m["cw"] = cw_t
        in_maps.append(m)
    return in_maps


_NC_CACHE = {}


def kernel(**inputs):
    in_maps = prep_inputs(**inputs)
    if "nc" not in _NC_CACHE:
        _NC_CACHE["nc"] = build_program()
    nc = _NC_CACHE["nc"]
    res = run_bass_kernel_spmd(nc, in_maps, core_ids=list(range(8)))
    out = np.empty((4, S, D), dtype=np.float32)
    for c in range(8):
        b, half = c // 2, c % 2
        y = np.asarray(res.results[c]["y"], dtype=np.float32)
        if half == 0:
            out[b, :T] = y
        else:
            out[b, T:] = y[::-1]
    return out
```

```python
from contextlib import ExitStack

import numpy as np
import ml_dtypes

import concourse.bass as bass
import concourse.mybir as mybir
from concourse.bass_utils import run_bass_kernel_spmd

F32 = mybir.dt.float32
BF16 = mybir.dt.bfloat16
AF = mybir.ActivationFunctionType
ALU = mybir.AluOpType
AX = mybir.AxisListType

D = 2048
S = 2048
T = 1024
NQ = T + 1
HD = 128
NH = 8
NKV = 2
DFF = 5632
NCH = DFF // 128
EPS = 1e-6
FFB = [(0, 12), (12, 12), (24, 12), (36, 8)]
SEM_GEN = 3000
TAP_UTE = False

ENGS = ["pe", "act", "dve", "pool", "sp"]


class Op:
    __slots__ = ("eng", "fn", "stream", "sval", "signal", "sigval", "gidx", "deps", "phase")


class Sched:
    def __init__(self):
        self.ops = []
        self.per = {e: [] for e in ENGS}
        self.lastw = {}
        self.readers = {}
        self.touch = {}
        self.inherit = {}
        self.stream_cnt = {}
        self.phase = ""

    @staticmethod
    def _k(o):
        return ("d", o.stream) if o.stream else ("e", o.eng)

    def op(self, eng, fn, reads=(), writes=(), stream=None):
        o = Op()
        o.eng = eng
        o.fn = fn
        o.stream = stream
        o.signal = False
        o.sigval = 0
        o.sval = 0
        o.gidx = len(self.ops)
        o.phase = self.phase
        if stream:
            self.stream_cnt[stream] = self.stream_cnt.get(stream, 0) + 1
            o.sval = 16 * self.stream_cnt[stream]
        deps = {}

        def add(d):
            k = self._k(d)
            if k not in deps or deps[k].gidx < d.gidx:
                deps[k] = d

        reads = list(reads)
        writes = list(writes)
        for key in reads + writes:
            if key not in self.lastw and key not in self.readers:
                for d in self.inherit.get(key[0], {}).values():
                    add(d)
        for key in reads:
            w = self.lastw.get(key)
            if w is not None:
                add(w)
        for key in writes:
            w = self.lastw.get(key)
            if w is not None and not (key[0] == "junkq" and w.eng == eng and not w.stream):
                add(w)
            for r in self.readers.get(key, {}).values():
                add(r)
        for key in reads:
            self.readers.setdefault(key, {})[self._k(o)] = o
        for key in writes:
            self.lastw[key] = o
            self.readers[key] = {}
        for key in reads + writes:
            self.touch.setdefault(key[0], {})[self._k(o)] = o
        o.deps = []
        for d in deps.values():
            if (not d.stream) and d.eng == "pe" and eng == "pe" and not stream:
                continue
            o.deps.append(d)
            if not d.stream:
                d.signal = True
        self.ops.append(o)
        self.per[eng].append(o)
        return o

    def alias(self, new_name, old_names):
        m = self.inherit.setdefault(new_name, {})
        for n in old_names:
            for k, d in self.touch.get(n, {}).items():
                if k not in m or m[k].gidx < d.gidx:
                    m[k] = d
            for k, d in self.inherit.get(n, {}).items():
                if k not in m or m[k].gidx < d.gidx:
                    m[k] = d


class Arena:
    def __init__(self, tensor, sched, nbytes):
        self.t = tensor
        self.s = sched
        self.nbytes = nbytes
        self.live = []

    def alloc(self, name, off, nbytes, dtype, shape_str=None, **dims):
        assert off % 4 == 0 and off + nbytes <= self.nbytes, (name, off, nbytes)
        b0, b1 = off, off + nbytes
        old = sorted(set(n for (n, a0, a1) in self.live if a0 < b1 and b0 < a1))
        self.live.append((name, b0, b1))
        if old:
            self.s.alias(name, old)
        w0 = off // 4
        w1 = (off + nbytes + 3) // 4
        ap = self.t[:, w0:w1]
        if dtype != F32:
            ap = ap.bitcast(dtype)
        if shape_str:
            ap = ap.rearrange(shape_str, **dims)
        return ap


def build_program(debug=False):
    nc = bass.Bass("TRN2", target_bir_lowering=False)
    S_ = Sched()

    def dram(name, shape, dt=F32, kind="ExternalInput"):
        return nc.dram_tensor(name, list(shape), dt, kind=kind).ap()

    x_d = dram("x_loc", [S, D])
    rope_d = dram("rope", [S, 256])
    dft_d = dram("dft", [5, 128, 2 * 16 * 256], BF16)
    ccs_d = dram("ccs", [128, 256], BF16)
    ident_d = dram("ident", [128, 128], BF16)
    g1_d = dram("g1", [D])
    g2_d = dram("g2", [D])
    gF_d = dram("gF", [D])
    ga_d = dram("gaT", [128, 8])
    gf_d = dram("gf", [1024])
    gq_d = dram("gq", [128])
    gk_d = dram("gk", [128])
    cw_d = dram("cw", [128, 88 * 3])
    cb_d = dram("cb", [128, 88])
    wf_d = dram("wf", [128, 1024])
    win_d = dram("win", [128, 5 * 8192])
    wout_d = dram("wout", [128, 4 * 8192])
    wup_d = dram("wup", [128, 22 * 8192])
    wdn_d = dram("wdn", [128, 44 * 2048])
    y_d = dram("y", [T, D], F32, kind="ExternalOutput")

    ARENA_BYTES = 212480
    es = ExitStack()
    with es:
        arena_t = es.enter_context(nc.sbuf_tensor("arena", [128, ARENA_BYTES // 4], F32))
        psum_t = es.enter_context(nc.psum_tensor("psum", [128, 4096], F32))
        A = Arena(arena_t, S_, ARENA_BYTES)

        R_SLOT = 0
        R_F = 49152
        R_Q = 81920
        R_MIX = 114944
        R_A = 147840
        R_C = 207744

        slots = [A.alloc("slot", R_SLOT + i * 16384, 16384, BF16) for i in range(3)]
        ident = A.alloc("ident", R_C, 256, BF16)
        cw = A.alloc("cw", R_C + 256, 1056, F32)
        cb = A.alloc("cb", R_C + 1312, 352, F32)
        stat = A.alloc("stat", R_C + 1664, 512, F32)
        epsc = A.alloc("epsc", R_C + 2176, 4, F32)
        junkq = A.alloc("junkq", R_C + 2184, 2048, mybir.dt.float8e4)
        JK = [("junkq", 0)]

        def ps(bank, n=512, off=0):
            return psum_t[:, bank * 512 + off: bank * 512 + off + n]

        def ps_bf(bank, nbanks=1):
            return psum_t[:, bank * 512:(bank + nbanks) * 512].bitcast(BF16)

        def mm(out, lhsT, rhs, start, stop, reads, writes):
            return S_.op("pe", lambda e: e.matmul(out, lhsT, rhs, start=start, stop=stop), reads, writes)

        def tr(out, in_, idn, reads, writes):
            return S_.op("pe", lambda e: e.transpose(out, in_, idn), reads, writes)

        def act(out, in_, func, reads, writes, bias=None, scale=None, accum=None, sat=None):
            kw = {}
            if sat is not None:
                kw["saturate"] = sat
            if bias is not None:
                kw["bias"] = bias
            if scale is not None:
                kw["scale"] = scale
            if accum is not None:
                kw["accum_out"] = accum
            return S_.op("act", lambda e: e.activation(out, in_, func, **kw), reads, writes)

        def dve(fn, reads, writes):
            return S_.op("dve", fn, reads, writes)

        def tt(out, in0, in1, op, reads, writes):
            return dve(lambda e: e.tensor_tensor(out, in0, in1, op), reads, writes)

        def stt(out, in0, scalar, in1, op0, op1, reads, writes):
            return dve(lambda e: e.scalar_tensor_tensor(out, in0, scalar, in1, op0, op1), reads, writes)

        def tsc(out, in0, s1, s2, op0, op1, reads, writes):
            if op1 is None:
                return dve(lambda e: e.tensor_scalar(out, in0, s1, s2, op0), reads, writes)
            return dve(lambda e: e.tensor_scalar(out, in0, s1, s2, op0, op1), reads, writes)

        def red(out, in_, reads, writes):
            return dve(lambda e: e.tensor_reduce(out, in_, AX.X, ALU.add), reads, writes)

        def recip(out, in_, reads, writes):
            return dve(lambda e: e.reciprocal(out, in_), reads, writes)

        def copy_any(which, out, in_, reads, writes):
            if which % 2 == 0:
                return act(out, in_, AF.Copy, reads, writes)
            return dve(lambda e: e.tensor_copy(out, in_), reads, writes)

        def dma(eng, out, in_, reads, writes, stream, **kw):
            return S_.op(eng, lambda e: e.dma_start(out=out, in_=in_, **kw), reads, writes, stream=stream)

        stat_ctr = [0]

        def stat_slot(n=1):
            i = stat_ctr[0]
            if i % 128 + n > 128:
                i += 128 - i % 128
            stat_ctr[0] = i + n
            c = i % 128
            return stat[:, c:c + n], ("stat", c, c + n - 1)

        def stat_keys(k):
            return [("stat", j) for j in range(k[1], k[2] + 1)]

        def rstd_from_ss(ss, sskeys, rows, ncols, inv_n):
            r, rk = stat_slot(ncols)
            rkeys = stat_keys(rk)
            act(r[:rows], ss[:rows], AF.Sqrt, sskeys + [("epsc", 0)], rkeys, bias=epsc[:rows], scale=inv_n)
            recip(r[:rows], r[:rows], rkeys, rkeys)
            return r, rkeys

        slot_ctr = [0]

        def load_slot(src_ap, ncols, after=()):
            i = slot_ctr[0] % 3
            slot_ctr[0] += 1
            dma("pool", slots[i][:, 0:ncols], src_ap, list(after), [("slot", i)], f"slot{i}",
                max_dma_last_dim=8192)
            return i

        def tap(name, ap, names):
            if not debug:
                return
            shape = [int(s) for s in ap.shape]
            dd = nc.dram_tensor("dbg_" + name, shape, ap.dtype, kind="ExternalOutput").ap()
            reads = [k for k in S_.lastw if k[0] in names]
            dma("sp", dd, ap, reads, [("dbg", name)], "dbg_" + name)

        KI = [("ident", 0)]

        f_tm = A.alloc("f_tm", R_F, 32768, BF16, "p (s c) -> p s c", c=1024)
        qT = A.alloc("qT", R_Q, 16416, BF16, "p (h t) -> p h t", t=1026)
        kT = A.alloc("kT", R_Q + 16416, 8192, BF16, "p (h t) -> p h t", t=2048)
        Vaug = A.alloc("V", R_Q + 24608, 8320, BF16, "p (s h c) -> p s h c", h=2, c=130)
        uT = A.alloc("uT", R_A, 32768, BF16, "p (k t) -> p k t", t=1024)
        o = R_A + 32768
        sq = A.alloc("sq", o, 2048, F32); o += 2048
        qn = A.alloc("qn", o, 2048, F32); o += 2048
        t1 = A.alloc("t1", o, 2048, F32); o += 2048
        t2 = A.alloc("t2", o, 2048, F32); o += 2048
        qr = [A.alloc("qr", o + i * 1024, 1024, BF16) for i in range(3)]; o += 3072
        ropet = [A.alloc("ropet", o + i * 1024, 1024, F32) for i in range(2)]; o += 2048
        gq_bc = A.alloc("gq_bc", o, 512, F32); o += 512
        gk_bc = A.alloc("gk_bc", o, 512, F32); o += 512
        xs = [A.alloc("xs", R_MIX + i * 8192, 8192, F32) for i in range(2)] + [A.alloc("xs", R_A + 49152, 8192, F32)]
        g1_bc = A.alloc("g1_bc", R_MIX + 16384, 8192, F32)
        u_tm = [A.alloc("u_tm", R_MIX + 24576 + i * 4096, 4096, BF16) for i in range(2)]

        dve(lambda e: e.memset(epsc, EPS), [], [("epsc", 0)])
        dma("sp", xs[0], x_d[0:128, :], [], [("xs", 0)], "xs0")
        dma("sp", g1_bc, g1_d.partition_broadcast(128), [], [("g1_bc", 0)], "c_g1")
        dma("sp", ident, ident_d, [], [("ident", 0)], "c_ident")
        dma("sp", cw, cw_d, [], [("cw", 0)], "c_cw")
        dma("sp", cb, cb_d, [], [("cb", 0)], "c_cb")
        dma("sp", gq_bc, gq_d.partition_broadcast(128), [], [("gq_bc", 0)], "c_gq")
        dma("sp", gk_bc, gk_d.partition_broadcast(128), [], [("gk_bc", 0)], "c_gk")
        dve(lambda e: e.memset(Vaug[:, :, :, 128:129], 1.0), [], [("V", 99)])

        cpy = [0]
        rope_ctr = [0]
        psacc_ctr = [0]
        pstr_ctr = [0]
        qr_ctr = [0]

        def norm_tile(src, srckeys, rows, gbc, gkey, dst, dstkeys):
            ss, sk = stat_slot(1)
            sskeys = stat_keys(sk)
            act(junkq[:rows], src[:rows], AF.Square, srckeys, JK + sskeys, accum=ss[:rows], sat=False)
            r, rkeys = rstd_from_ss(ss, sskeys, rows, 1, 1.0 / D)
            stt(dst[:rows], src[:rows], r[:rows, 0:1], gbc[:rows], ALU.mult, ALU.mult,
                srckeys + rkeys + [gkey], dstkeys)

        def transposes16(src, srckeys, rows, dstT, dstkeys_fn, col0, bankpair, which):
            pb = ps_bf(bankpair, 2)
            pk = [("ps", bankpair), ("ps", bankpair + 1)]
            for j in range(16):
                tr(pb[:, j * 128: j * 128 + rows], src[:rows, j * 128:(j + 1) * 128], ident[:rows, :rows],
                   srckeys + KI, pk)
            copy_any(which, dstT[:, :, col0:col0 + rows],
                     pb.rearrange("p (k t) -> p k t", t=128)[:, :, 0:rows], pk, dstkeys_fn)

        def qk_post(psrc, pkeys, rows, nheads, gbc, gbkey, rtile, rkey, dstT, dst_h0, col0, dstkeys):
            n = nheads * 128
            act(sq[:rows, :n], psrc[:rows, :n], AF.Square, pkeys, [("sq", 0)])
            ss, sk = stat_slot(nheads)
            sskeys = stat_keys(sk)
            red(ss[:rows], sq[:rows, :n].rearrange("p (h d) -> p h d", d=128), [("sq", 0)], sskeys)
            r, rkeys = rstd_from_ss(ss, sskeys, rows, nheads, 1.0 / HD)
            for h in range(nheads):
                stt(qn[:rows, h * 128:(h + 1) * 128], psrc[:rows, h * 128:(h + 1) * 128],
                    r[:rows, h:h + 1], gbc[:rows], ALU.mult, ALU.mult,
                    pkeys + rkeys + [gbkey], [("qn", 0)])
            qn3 = qn[:rows, :n].rearrange("p (h d) -> p h d", d=128)
            cosb = rtile[:rows, 0:128].unsqueeze(1).to_broadcast([rows, nheads, 128])
            tt(t1[:rows, :n].rearrange("p (h d) -> p h d", d=128), qn3, cosb, ALU.mult,
               [("qn", 0), rkey], [("t1", 0)])
            qn5 = qn[:rows, :n].rearrange("p (h a b c) -> p h a b c", a=2, b=2, c=32)
            t25 = t2[:rows, :n].rearrange("p (h a b c) -> p h a b c", a=2, b=2, c=32)
            sin5 = rtile[:rows, 128:256].rearrange("p (a b c) -> p a b c", a=2, b=2, c=32)
            for blk in range(2):
                sb = sin5[:, :, blk, :].unsqueeze(1).to_broadcast([rows, nheads, 2, 32])
                tt(t25[:, :, :, blk, :], qn5[:, :, :, 1 - blk, :], sb, ALU.mult,
                   [("qn", 0), rkey], [("t2", blk)])
            qi = qr_ctr[0] % 3
            qr_ctr[0] += 1
            qrb = qr[qi]
            tt(qrb[:rows, :n], t1[:rows, :n], t2[:rows, :n], ALU.add,
               [("t1", 0), ("t2", 0), ("t2", 1)], [("qr", qi)])

            def tail():
                bank = pstr_ctr[0] % 3
                pstr_ctr[0] += 1
                pb = ps_bf(bank, 1)
                for h in range(nheads):
                    tr(pb[:, h * 128:h * 128 + rows], qrb[:rows, h * 128:(h + 1) * 128], ident[:rows, :rows],
                       [("qr", qi)] + KI, [("ps", bank)])
                copy_any(0, dstT[:, dst_h0:dst_h0 + nheads, col0:col0 + rows],
                         pb[:, 0:n].rearrange("p (h t) -> p h t", t=128)[:, :, 0:rows], [("ps", bank)], dstkeys)
            return tail

        def load_rope(gt):
            i = rope_ctr[0] % 2
            rope_ctr[0] += 1
            dma("sp", ropet[i], rope_d[gt * 128:(gt + 1) * 128, :], [], [("ropet", i)], f"ropet{i}")
            return ropet[i], ("ropet", i)

        pend = []

        def flush(keep):
            while len(pend) > keep:
                pend.pop(0)()

        for pa in range(2):
            if pa == 1 and TAP_UTE:
                tap("uTE", uT, ["uT"])
            S_.phase = f"P1{'AB'[pa]}"
            for t in range(8):
                gt = pa * 8 + t
                b = gt % 2
                xb = gt % 3
                if gt > 0:
                    dma("sp", xs[xb], x_d[gt * 128:(gt + 1) * 128, :], [], [("xs", xb)], f"xs{xb}")
                norm_tile(xs[xb], [("xs", xb)], 128, g1_bc, ("g1_bc", 0), u_tm[b], [("u_tm", b)])
                pend.append(lambda t=t, b=b: transposes16(u_tm[b], [("u_tm", b)], 128, uT, [("uT", t)], t * 128,
                                                          2 * (t % 2), t))
                flush(1)
            flush(0)
            S_.phase = f"P2{'AB'[pa]}"
            blocks = [0, 1, 2, 3, 4] if pa == 0 else [2, 3, 4, 0, 1]
            for blk in blocks:
                after = []
                if pa == 0 and blk == 1:
                    after = [("uT", 3)]
                if pa == 0 and blk == 2:
                    after = [("uT", 6)]
                si = load_slot(win_d[:, blk * 8192:(blk + 1) * 8192], 8192, after)
                w = slots[si].rearrange("p (k c) -> p k c", c=512)
                halo_only = (pa == 1 and blk < 2)
                tiles = [0] if halo_only else list(range(8))
                for t in tiles:
                    gt = pa * 8 + t
                    rows = 1 if halo_only else 128
                    bank = 4 + psacc_ctr[0] % 4
                    psacc_ctr[0] += 1
                    pk = [("ps", bank)]
                    for k in range(16):
                        mm(ps(bank)[:rows, :], uT[:, k, t * 128:t * 128 + rows], w[:, k, :], k == 0, k == 15,
                           [("uT", t), ("slot", si)], pk)
                    if blk < 2:
                        rt, rk = load_rope(gt)
                        col0 = 1024 if halo_only else t * 128
                        pend.append(qk_post(ps(bank), pk, rows, 4, gq_bc, ("gq_bc", 0), rt, rk, qT, 4 * blk, col0,
                                            [("qT", blk, gt)]))
                    elif blk == 2:
                        rt, rk = load_rope(gt)
                        act(Vaug[:, gt, :, 0:128], ps(bank)[:, 256:512].rearrange("p (h d) -> p h d", d=128),
                            AF.Copy, pk, [("V", gt)])
                        pend.append(qk_post(ps(bank), pk, 128, 2, gk_bc, ("gk_bc", 0), rt, rk, kT, 0, gt * 128,
                                            [("kT", gt)]))
                    else:
                        fb = blk - 3
                        act(f_tm[:, gt, fb * 512:(fb + 1) * 512], ps(bank), AF.Copy, pk, [("f_tm", gt, fb)])
                    flush(2)
            flush(0)

        tap("qT", qT, ["qT"])
        tap("kT", kT, ["kT"])
        tap("V", Vaug, ["V"])
        tap("f", f_tm, ["f_tm"])
        S_.phase = "P3"
        mixT = A.alloc("mixT", R_MIX, 32832, BF16, "p (k t) -> p k t", t=1026)
        o = R_A
        PT = [A.alloc("PT", o + i * 16384, 16384, BF16, "p (s q) -> p s q", q=512) for i in range(2)]; o += 32768
        O_sb = [A.alloc("O_sb", o + i * 4160, 4160, F32, "p (h c) -> p h c", c=130) for i in range(4)]; o += 16640
        sqa = A.alloc("sqa", o, 2048, BF16); o += 2048
        a_tm = [A.alloc("a_tm", o + i * 2048, 2048, BF16) for i in range(4)]; o += 8192
        gaT = A.alloc("gaT", o, 32, F32); o += 32
        dma("sp", gaT, ga_d, [], [("gaT", 0)], "c_ga")
        pend_atr = []

        scale = 1.0 / float(np.sqrt(HD))
        sbank = [0]
        obank = [0]
        ptc = [0]

        def attn_norm(tl, rows, col0):
            okeys = [("O_sb", tl, h) for h in range(NH)]
            rl, rlk = stat_slot(NH)
            rlkeys = stat_keys(rlk)
            recip(rl[:rows].unsqueeze(2), O_sb[tl][:rows, :, 128:129], okeys, rlkeys)
            act(sqa[:rows, :].rearrange("p (h d) -> p h d", d=128), O_sb[tl][:rows, :, 0:128], AF.Square,
                okeys, [("sqa", 0)])
            ssh, sk = stat_slot(NH)
            sshk = stat_keys(sk)
            red(ssh[:rows], sqa[:rows, :].rearrange("p (h d) -> p h d", d=128), [("sqa", 0)], sshk)
            tt(ssh[:rows], ssh[:rows], rl[:rows], ALU.mult, sshk + rlkeys, sshk)
            tt(ssh[:rows], ssh[:rows], rl[:rows], ALU.mult, sshk + rlkeys, sshk)
            ss1, sk1 = stat_slot(1)
            ss1k = stat_keys(sk1)
            red(ss1[:rows], ssh[:rows], sshk, ss1k)
            atm = a_tm[tl]

            def part_b():
                r, rkeys = rstd_from_ss(ss1, ss1k, rows, 1, 1.0 / 1024.0)
                fac, fk = stat_slot(NH)
                fkeys = stat_keys(fk)
                tsc(fac[:rows], rl[:rows], r[:rows, 0:1], None, ALU.mult, None, rlkeys + rkeys, fkeys)
                tt(atm[:rows, :].rearrange("p (h d) -> p h d", d=128), O_sb[tl][:rows, :, 0:128],
                   fac[:rows].unsqueeze(2).to_broadcast([rows, NH, 128]), ALU.mult,
                   okeys + fkeys, [("a_tm", tl)])

            def tail():
                pb = ps_bf(7, 1)
                for h in range(NH):
                    tr(pb[:, h * 128:h * 128 + rows], atm[:rows, h * 128:(h + 1) * 128], ident[:rows, :rows],
                       [("a_tm", tl)] + KI, [("ps", 7)])
                tt(mixT[:, 0:8, col0:col0 + rows], pb.rearrange("p (h t) -> p h t", t=128)[:, :, 0:rows],
                   gaT[:, :].unsqueeze(2).to_broadcast([128, NH, rows]), ALU.mult,
                   [("ps", 7), ("gaT", 0)], [("mixT", 0, col0 // 128)])
            return part_b, tail

        prev = None
        pend_norm = []
        pend_b = []
        todo = {}
        for (q0, n) in [(0, 512), (512, 512), (None, 0)]:
            heads = range(NH) if q0 is not None else [None]
            for h in heads:
                if h is not None:
                    kv = h // 4
                    pi = ptc[0] % 2
                    ptc[0] += 1
                for sc in range(16):
                    if h is not None:
                        bank = sbank[0] % 3
                        sbank[0] += 1
                        mm(ps(bank)[:, :n], kT[:, kv, sc * 128:(sc + 1) * 128], qT[:, h, q0:q0 + n], True, True,
                           [("kT", sc)] + [("qT", h // 4, q0 // 128 + j) for j in range(4)], [("ps", bank)])
                        act(PT[pi][:, sc, :n], ps(bank)[:, :n], AF.Exp, [("ps", bank)], [("PT", pi, sc)],
                            scale=scale)
                    if prev is not None:
                        ppi, ph = prev
                        for tl in range(4):
                            mm(ps(3 + tl)[:, 0:129], PT[ppi][:, sc, tl * 128:(tl + 1) * 128],
                               Vaug[:, sc, ph // 4, 0:129], sc == 0, sc == 15,
                               [("PT", ppi, sc), ("V", sc), ("V", 99)], [("ps", 3 + tl)])
                if prev is not None:
                    ppi, ph = prev
                    if ph == 0:
                        while pend_b:
                            pend_b.pop(0)()
                    for tl in range(4):
                        copy_any(1, O_sb[tl][:, ph, 0:129], ps(3 + tl)[:, 0:129], [("ps", 3 + tl)],
                                 [("O_sb", tl, ph)])
                    if ph == NH - 1:
                        todo.clear()
                        while pend_norm:
                            tl_, pb_, tail_ = pend_norm.pop(0)()
                            pend_b.append(pb_)
                            todo.setdefault(tl_ + 1, []).append(tail_)
                    elif ph in todo:
                        for fn_ in todo.pop(ph):
                            fn_()
                prev = (pi, h) if h is not None else None
                if h == NH - 1:
                    for tl in range(4):
                        pend_norm.append(lambda tl=tl, q0=q0: (tl,) + attn_norm(tl, 128, q0 + tl * 128))
        obank[0] = 0
        pi = ptc[0] % 2
        ptc[0] += 1
        for h in range(NH):
            kv = h // 4
            bank = sbank[0] % 3
            sbank[0] += 1
            for sc in range(16):
                mm(ps(bank)[:, sc:sc + 1], kT[:, kv, sc * 128:(sc + 1) * 128], qT[:, h, 1024:1025], True, True,
                   [("kT", sc), ("qT", h // 4, 8)], [("ps", bank)])
            act(PT[pi][:, h, 0:16], ps(bank)[:, 0:16], AF.Exp, [("ps", bank)], [("PT", pi, h)], scale=scale)
        while pend_b:
            pend_b.pop(0)()
        late_tails = []
        for key_ in sorted(todo):
            late_tails.extend(todo[key_])
        todo.clear()
        late_tails.pop(0)()
        for h in range(NH):
            kv = h // 4
            bank = 3 + obank[0] % 4
            obank[0] += 1
            for sc in range(16):
                mm(ps(bank)[:1, 0:129], PT[pi][:, h, sc:sc + 1], Vaug[:, sc, kv, 0:129], sc == 0, sc == 15,
                   [("PT", pi, h), ("V", sc), ("V", 99)], [("ps", bank)])
            copy_any(1, O_sb[0][:1, h, 0:129], ps(bank)[:1, 0:129], [("ps", bank)], [("O_sb", 0, h)])
        pb_, tail_ = attn_norm(0, 1, 1024)
        pb_()
        late_tails.append(tail_)

        S_.phase = "P4"
        ZT = A.alloc("ZT", R_Q, 32832, BF16, "p (a g t) -> p a g t", a=2, g=8)
        o = R_A
        CS = [A.alloc("CS", o + i * 16384, 16384, BF16, "p (a s k) -> p a s k", a=2, s=16) for i in range(2)]
        o += 32768
        gf_bc = A.alloc("gf_bc", o, 4096, F32); o += 4096
        f_n2 = [A.alloc("f_n", o + i * 2048, 2048, BF16) for i in range(2)]; o += 4096
        AB = A.alloc("AB", o, 4096, BF16, "p (g a d) -> p g a d", a=2, d=128); o += 4096
        ccs = A.alloc("ccs", o, 512, BF16, "p (a c) -> p a c", a=2); o += 512
        wfb = A.alloc("wfb", o, 2048, BF16); o += 2048
        dma("sp", gf_bc, gf_d.partition_broadcast(128), [], [("gf_bc", 0)], "c_gf")
        dma("sp", ccs, ccs_d.rearrange("p (a c) -> p a c", a=2), [], [("ccs", 0)], "c_ccs")
        dma("pool", wfb, wf_d, [], [("wfb", 0)], "c_wf")
        for g in range(8):
            for a in range(2):
                mm(ps(0)[:, a * 128:(a + 1) * 128], ccs[:, a, :], wfb[:, g * 128:(g + 1) * 128], True, True,
                   [("ccs", 0), ("wfb", 0)], [("ps", 0)])
            cpy[0] += 1
            copy_any(cpy[0], AB[:, g, :, :], ps(0)[:, 0:256].rearrange("p (a d) -> p a d", d=128), [("ps", 0)],
                     [("AB", g)])
        zb = [0]
        qgroups = [(i * 256, 256) for i in range(4)] + [(1024, 1)]
        pend_out = []
        pend_tr = []

        def fourier_out(gi, tl, rows, col0):
            b0 = 4 + 2 * tl
            pk = [("ps", b0), ("ps", b0 + 1)]
            pout = psum_t[:, b0 * 512:(b0 + 2) * 512]
            for g in range(8):
                for a in range(2):
                    mm(pout[:rows, g * 128:(g + 1) * 128], ZT[:, a, g, col0:col0 + rows], AB[:, g, a, :],
                       a == 0, a == 1, [("ZT", a, g, gi), ("AB", g)], [("ps", b0 + g // 4)])
            ss, sk = stat_slot(1)
            sskeys = stat_keys(sk)
            act(junkq[:rows, 0:1024], pout[:rows], AF.Square, pk, JK + sskeys, accum=ss[:rows], sat=False)
            r, rkeys = rstd_from_ss(ss, sskeys, rows, 1, 1.0 / 1024.0)
            fn = f_n2[tl]
            stt(fn[:rows], pout[:rows], r[:rows, 0:1], gf_bc[:rows], ALU.mult, ALU.mult,
                pk + rkeys + [("gf_bc", 0)], [("f_n", tl)])

            def tail():
                pb = ps_bf(b0, 1)
                for h in range(8):
                    tr(pb[:, h * 128:h * 128 + rows], fn[:rows, h * 128:(h + 1) * 128], ident[:rows, :rows],
                       [("f_n", tl)] + KI, [("ps", b0)])
                cpy[0] += 1
                copy_any(cpy[0], mixT[:, 8:16, col0:col0 + rows],
                         pb.rearrange("p (h t) -> p h t", t=128)[:, :, 0:rows], [("ps", b0)],
                         [("mixT", 1, col0 // 128)])
            pend_tr.append(tail)

        for gi, (k0, n) in enumerate(qgroups):
            ntile = 2 if n == 256 else 1
            rows = 128 if n == 256 else 1
            ci = gi % 2
            dma("sp", CS[ci], dft_d[gi].rearrange("p (a s k) -> p a s k", a=2, s=16), [], [("CS", ci)], f"CS{ci}")
            for g in range(8):
                bz = 2 * (zb[0] % 2)
                zb[0] += 1
                for sc in range(16):
                    for a in range(2):
                        mm(ps(bz + a)[:, :n], f_tm[:, sc, g * 128:(g + 1) * 128], CS[ci][:, a, sc, :n],
                           sc == 0, sc == 15, [("f_tm", sc, g // 4), ("CS", ci)], [("ps", bz + a)])
                for a in range(2):
                    cpy[0] += 1
                    copy_any(cpy[0], ZT[:, a, g, k0:k0 + n], ps(bz + a)[:, :n], [("ps", bz + a)],
                             [("ZT", a, g, gi)])
                if g == 1:
                    while late_tails:
                        late_tails.pop(0)()
                    while pend_out:
                        pend_out.pop(0)()
                if g == 4:
                    while pend_tr:
                        pend_tr.pop(0)()
            for tl in range(ntile):
                pend_out.append(lambda gi=gi, tl=tl, rows=rows, k0=k0: fourier_out(gi, tl, rows, k0 + tl * 128))
        while pend_out:
            pend_out.pop(0)()
        while pend_tr:
            pend_tr.pop(0)()

        tap("mixT", mixT, ["mixT"])
        S_.phase = "P5"
        h_sb = A.alloc("h", R_F, 65536, F32, "p (t d) -> p t d", d=2048)
        o = R_A
        u2T = A.alloc("u2T", o, 32832, BF16, "p (k t) -> p k t", t=1026); o += 32832
        g2_bc = A.alloc("g2_bc", o, 8192, F32); o += 8192
        u2_tm = [A.alloc("u2_tm", o + i * 4096, 4096, BF16) for i in range(2)]; o += 8192
        h_halo = A.alloc("h", o, 8192, F32); o += 8192
        dma("sp", g2_bc, g2_d.partition_broadcast(128), [], [("g2_bc", 0)], "c_g2")
        for t in range(8):
            dma("sp", h_sb[:, t, :], x_d[t * 128:(t + 1) * 128, :], [], [("h", t, q) for q in range(4)], f"hx{t}")
        dma("sp", h_halo[0:1, :], x_d[1024:1025, :], [], [("h", 8, q) for q in range(4)], "hx8")

        def htile(t):
            return h_sb[:, t, :] if t < 8 else h_halo

        wacc = [0]
        pend = []

        def do_norm2(t):
            rows = 128 if t < 8 else 1
            hk = [("h", t, q) for q in range(4)]
            b = t % 2
            norm_tile(htile(t), hk, rows, g2_bc, ("g2_bc", 0), u2_tm[b], [("u2_tm", b)])
            pend.append(lambda: transposes16(u2_tm[b], [("u2_tm", b)], rows, u2T, [("u2T", t)], t * 128,
                                             4 + 2 * (t % 2), t))

        for cbk in range(4):
            si = load_slot(wout_d[:, cbk * 8192:(cbk + 1) * 8192], 8192)
            w = slots[si].rearrange("p (k c) -> p k c", c=512)
            for t in range(9):
                rows = 128 if t < 8 else 1
                col0 = t * 128
                bank = wacc[0] % 4
                wacc[0] += 1
                for fc in range(16):
                    mm(ps(bank)[:rows, :], mixT[:, fc, col0:col0 + rows], w[:, fc, :], fc == 0, fc == 15,
                       [("mixT", fc // 8, t), ("slot", si)], [("ps", bank)])
                hv = htile(t)[:rows, cbk * 512:(cbk + 1) * 512]
                tt(hv, ps(bank)[:rows, :], hv, ALU.add, [("ps", bank), ("h", t, cbk)], [("h", t, cbk)])
                if cbk == 3:
                    if t >= 1:
                        do_norm2(t - 1)
                    while len(pend) > 1:
                        pend.pop(0)()
        do_norm2(8)
        while pend:
            pend.pop(0)()

        tap("h1", h_sb, ["h"])
        tap("u2T", u2T, ["u2T"])
        S_.phase = "P6"
        actT = A.alloc("actT", R_MIX, 32768, BF16, "p (c t) -> p c t", t=1024)
        o = R_A + 32832
        tmpG = A.alloc("tmp", o, 4096, F32); o += 4096
        tmpV = A.alloc("tmp", o, 4096, F32); o += 4096
        sg = A.alloc("sg", o, 4096, F32); o += 4096
        gF_bc = A.alloc("gF_bc", o, 8192, F32); o += 8192
        dma("sp", gF_bc, gF_d.partition_broadcast(128), [], [("gF_bc", 0)], "c_gF")
        u2keys = [("u2T", t) for t in range(9)]
        dacc = [0]

        def final_tile(t):
            hk = [("h", t, q) for q in range(4)]
            ss, sk = stat_slot(1)
            sskeys = stat_keys(sk)
            act(junkq, h_sb[:, t, :], AF.Square, hk, JK + sskeys, accum=ss, sat=False)
            r, rkeys = rstd_from_ss(ss, sskeys, 128, 1, 1.0 / D)
            stt(h_sb[:, t, :], h_sb[:, t, :], r[:, 0:1], gF_bc, ALU.mult, ALU.mult,
                hk + rkeys + [("gF_bc", 0)], hk)
            dma("sp", y_d[t * 128:(t + 1) * 128, :], h_sb[:, t, :], hk, [("y", t)], f"y{t % 2}")

        def aslot(cglob):
            return cglob % 16

        def up_pairblock(blk):
            si = load_slot(wup_d[:, blk * 8192:(blk + 1) * 8192], 8192)
            w = slots[si].rearrange("p (k j c) -> p k j c", j=4, c=128)
            for pi in range(2):
                cglob = blk * 2 + pi
                for gv in range(2):
                    j = 2 * pi + gv
                    b0 = 2 * gv
                    pkm = [("ps", b0), ("ps", b0 + 1)]
                    for k in range(16):
                        mm(ps(b0), w[:, k, j, :], u2T[:, k, 0:512], k == 0, k == 15,
                           [("slot", si)] + u2keys[0:4], [("ps", b0)])
                        mm(ps(b0 + 1), w[:, k, j, :], u2T[:, k, 512:1024], k == 0, k == 15,
                           [("slot", si)] + u2keys[4:8], [("ps", b0 + 1)])
                        mm(ps(4 + gv, 1), w[:, k, j, :], u2T[:, k, 1024:1025], k == 0, k == 15,
                           [("slot", si), ("u2T", 8)], [("ps", 4 + gv)])
                    chunk = cglob if gv == 0 else NCH + cglob
                    w0 = cw[:, chunk * 3 + 0:chunk * 3 + 1]
                    w1 = cw[:, chunk * 3 + 1:chunk * 3 + 2]
                    w2 = cw[:, chunk * 3 + 2:chunk * 3 + 3]
                    bb = cb[:, chunk:chunk + 1]
                    tmp = tmpG if gv == 0 else tmpV
                    tk = [("tmp", gv)]
                    pfull = psum_t[:, b0 * 512:(b0 + 2) * 512]
                    act(tmp, pfull, AF.Identity, pkm + [("cw", 0), ("cb", 0)], tk, bias=bb, scale=w1)
                    stt(tmp[:, 1:1024], pfull[:, 0:1023], w0, tmp[:, 1:1024], ALU.mult, ALU.add,
                        pkm + tk + [("cw", 0)], tk)
                    stt(tmp[:, 0:1023], pfull[:, 1:1024], w2, tmp[:, 0:1023], ALU.mult, ALU.add,
                        pkm + tk + [("cw", 0)], tk)
                    stt(tmp[:, 1023:1024], ps(4 + gv, 1), w2, tmp[:, 1023:1024], ALU.mult, ALU.add,
                        [("ps", 4 + gv)] + tk + [("cw", 0)], tk)
                act(sg, tmpG, AF.Silu, [("tmp", 0)], [("sg", 0)])
                tt(actT[:, aslot(cglob), :], tmpV, sg, ALU.mult, [("tmp", 1), ("sg", 0)],
                   [("actT", aslot(cglob))])

        def down_block(fb):
            cbase, nch = FFB[fb]
            last = (fb == len(FFB) - 1)
            if not last:
                for q in range(4):
                    off = (cbase * 4 + q * nch) * 512
                    si = load_slot(wdn_d[:, off:off + nch * 512], nch * 512)
                    w = slots[si][:, 0:nch * 512].rearrange("p (j c) -> p j c", c=512)
                    for t in range(8):
                        bank = 6 + dacc[0] % 2
                        dacc[0] += 1
                        for j in range(nch):
                            sl = aslot(cbase + j)
                            mm(ps(bank), actT[:, sl, t * 128:(t + 1) * 128], w[:, j, :], j == 0, j == nch - 1,
                               [("actT", sl), ("slot", si)], [("ps", bank)])
                        hv = h_sb[:, t, q * 512:(q + 1) * 512]
                        tt(hv, ps(bank), hv, ALU.add, [("ps", bank), ("h", t, q)], [("h", t, q)])
            else:
                S_.phase = "P7"
                ws = []
                for half in range(2):
                    off = (cbase * 4 + 2 * half * nch) * 512
                    si = load_slot(wdn_d[:, off:off + 2 * nch * 512], 2 * nch * 512)
                    ws.append((si, slots[si][:, 0:2 * nch * 512].rearrange("p (q j c) -> p q j c", q=2, c=512)))
                for t in range(8):
                    for q in range(4):
                        si, w = ws[q // 2]
                        bank = 6 + dacc[0] % 2
                        dacc[0] += 1
                        for j in range(nch):
                            sl = aslot(cbase + j)
                            mm(ps(bank), actT[:, sl, t * 128:(t + 1) * 128], w[:, q % 2, j, :], j == 0,
                               j == nch - 1, [("actT", sl), ("slot", si)], [("ps", bank)])
                        hv = h_sb[:, t, q * 512:(q + 1) * 512]
                        tt(hv, ps(bank), hv, ALU.add, [("ps", bank), ("h", t, q)], [("h", t, q)])
                    if t >= 1:
                        final_tile(t - 1)
                final_tile(7)

        hoisted = set()
        for fb, (cbase, nch) in enumerate(FFB):
            for pbk in range(nch // 2):
                blk = cbase // 2 + pbk
                if blk not in hoisted:
                    up_pairblock(blk)
            if fb + 1 < len(FFB):
                nb = FFB[fb + 1][0] // 2
                up_pairblock(nb)
                hoisted.add(nb)
            down_block(fb)
        S_.op("sp", None, [("y", t) for t in range(8)] + [k for k in S_.lastw if k[0] == "dbg"], [])

        for e in ENGS:
            cnt = 0
            for o_ in S_.per[e]:
                if o_.signal and not o_.stream:
                    cnt += 1
                    o_.sigval = cnt
        ngen = {e: (max([o_.sigval for o_ in S_.per[e]] + [0]) // SEM_GEN) + 1 for e in ENGS}
        eng_sems = {e: [es.enter_context(nc.semaphore(f"s_{e}_{g}")) for g in range(ngen[e])] for e in ENGS}
        stream_sems = {s: es.enter_context(nc.semaphore(f"d_{s}")) for s in S_.stream_cnt}
        block = es.enter_context(nc.Block())

        def emit(engname, eng):
            seen = {}
            for o_ in S_.per[engname]:
                for d in o_.deps:
                    if d.stream:
                        k = ("d", d.stream)
                        val = d.sval
                        sem = stream_sems[d.stream]
                    else:
                        g = (d.sigval - 1) // SEM_GEN
                        k = ("e", d.eng, g)
                        val = d.sigval - g * SEM_GEN
                        sem = eng_sems[d.eng][g]
                        if seen.get(("e", d.eng, g + 1), 0) > 0:
                            continue
                    if seen.get(k, 0) >= val:
                        continue
                    seen[k] = val
                    eng.wait_ge(sem, val)
                if o_.fn is None:
                    continue
                ins = o_.fn(eng)
                if o_.stream:
                    ins.then_inc(stream_sems[o_.stream], 16)
                elif o_.signal:
                    g = (o_.sigval - 1) // SEM_GEN
                    ins.then_inc(eng_sems[o_.eng][g], 1)

        @block.tensor
        def _(e):
            emit("pe", e)

        @block.scalar
        def _(e):
            emit("act", e)

        @block.vector
        def _(e):
            emit("dve", e)

        @block.gpsimd
        def _(e):
            emit("pool", e)

        @block.sync
        def _(e):
            emit("sp", e)

    nc._sched = S_
    return nc


def _const_tables(half):
    perm = np.arange(S) if half == 0 else (S - 1 - np.arange(S))
    inv_freq = (np.float32(10000.0) ** (-np.arange(32, dtype=np.float32) / np.float32(32))).astype(np.float32)
    row = (perm // 64).astype(np.float32)
    col = (perm % 64).astype(np.float32)
    ar = row[:, None] * inv_freq[None, :]
    ac = col[:, None] * inv_freq[None, :]
    cr, sr = np.cos(ar).astype(np.float32), np.sin(ar).astype(np.float32)
    cc, sc_ = np.cos(ac).astype(np.float32), np.sin(ac).astype(np.float32)
    rope = np.concatenate([cr, cr, cc, cc, -sr, sr, -sc_, sc_], axis=1).astype(np.float32)
    ps_ = perm.astype(np.int64)
    pk_ = perm[:NQ].astype(np.int64)
    m = (ps_[:, None] * pk_[None, :]) % S
    ang = 2.0 * np.pi * m.astype(np.float64) / S
    ct = (np.cos(ang) / np.sqrt(S))
    st = (np.sin(ang) / np.sqrt(S))
    dft = np.zeros((5, 128, 2, 16, 256), dtype=np.float32)
    for gi in range(5):
        k0 = gi * 256
        n = 256 if gi < 4 else 1
        for a, tab in enumerate((ct, st)):
            blk = tab[:, k0:k0 + n].reshape(16, 128, n)
            dft[gi, :, a, :, :n] = blk.transpose(1, 0, 2)
    dft = dft.reshape(5, 128, 2 * 16 * 256).astype(ml_dtypes.bfloat16)
    return rope, dft


def _channel_dft():
    c = np.arange(128)
    ang = 2.0 * np.pi * ((c[:, None] * c[None, :]) % 128) / 128.0
    cc = np.cos(ang) / np.sqrt(128.0)
    ns = -np.sin(ang) / np.sqrt(128.0)
    return np.concatenate([cc, ns], axis=1).astype(ml_dtypes.bfloat16)


def _tile_rows(w, ncols_blk):
    K, N = w.shape
    nk = K // 128
    nb = N // ncols_blk
    a = w.reshape(nk, 128, nb, ncols_blk).transpose(1, 2, 0, 3)
    return np.ascontiguousarray(a).reshape(128, nb * nk * ncols_blk)


def prep_inputs(x, norm1_g, w_in, q_norm_g, k_norm_g, w_fmix, attn_out_g, fourier_out_g, w_out, norm2_g,
                w_up, conv_w, conv_b, w_down, final_g):
    f32 = np.float32
    x = np.asarray(x, f32)
    w_in0 = np.asarray(w_in, f32)[0]
    w_out0 = np.asarray(w_out, f32)[0]
    w_up0 = np.asarray(w_up, f32)[0]
    w_dn0 = np.asarray(w_down, f32)[0]
    conv_w0 = np.asarray(conv_w, f32)[0]
    conv_b0 = np.asarray(conv_b, f32)[0]
    win_t = _tile_rows(w_in0, 512)
    wout_t = _tile_rows(w_out0, 512)
    g = w_up0[:, :DFF].reshape(16, 128, NCH, 128)
    v = w_up0[:, DFF:].reshape(16, 128, NCH, 128)
    gv = np.stack([g, v], axis=3)
    gv = gv.reshape(16, 128, 22, 2, 2, 128)
    wup_t = np.ascontiguousarray(gv.transpose(1, 2, 0, 3, 4, 5)).reshape(128, 22 * 8192)
    parts = []
    wd = w_dn0.reshape(NCH, 128, 4, 512)
    for (cbase, nch) in FFB:
        for q in range(4):
            parts.append(np.ascontiguousarray(wd[cbase:cbase + nch, :, q, :].transpose(1, 0, 2)).reshape(128, nch * 512))
    wdn_t = np.concatenate(parts, axis=1)
    wf_t = np.ascontiguousarray(np.asarray(w_fmix, f32)[0].transpose(1, 0, 2)).reshape(128, 1024)
    cb_t = np.ascontiguousarray(conv_b0.reshape(88, 128).T)
    ccs = _channel_dft()
    ident = np.eye(128, dtype=f32).astype(ml_dtypes.bfloat16)
    tabs = [_const_tables(0), _const_tables(1)]
    common = {
        "ccs": ccs, "ident": ident,
        "g1": np.asarray(norm1_g, f32)[0], "g2": np.asarray(norm2_g, f32)[0], "gF": np.asarray(final_g, f32),
        "gaT": np.ascontiguousarray(np.asarray(attn_out_g, f32)[0].reshape(8, 128).T), "gf": np.asarray(fourier_out_g, f32)[0],
        "gq": np.asarray(q_norm_g, f32)[0], "gk": np.asarray(k_norm_g, f32)[0],
        "cb": cb_t, "wf": wf_t, "win": win_t, "wout": wout_t, "wup": wup_t, "wdn": wdn_t,
    }
    in_maps = []
    for c in range(8):
        b, half = c // 2, c % 2
        xl = x[b] if half == 0 else x[b][::-1]
        cwc = conv_w0 if half == 0 else conv_w0[::-1]
        cw_t = np.ascontiguousarray(cwc.reshape(3, 88, 128).transpose(2, 1, 0)).reshape(128, 88 * 3)
        m = dict(common)
        m["x_loc"] = np.ascontiguousarray(xl)
        m["rope"] = tabs[half][0]
        m["dft"] = tabs[half][1]
        m["cw"] = cw_t
        in_maps.append(m)
    return in_maps


_NC_CACHE = {}


def kernel(**inputs):
    in_maps = prep_inputs(**inputs)
    if "nc" not in _NC_CACHE:
        _NC_CACHE["nc"] = build_program()
    nc = _NC_CACHE["nc"]
    res = run_bass_kernel_spmd(nc, in_maps, core_ids=list(range(8)))
    out = np.empty((4, S, D), dtype=np.float32)
    for c in range(8):
        b, half = c // 2, c % 2
        y = np.asarray(res.results[c]["y"], dtype=np.float32)
        if half == 0:
            out[b, :T] = y
        else:
            out[b, T:] = y[::-1]
    return out
```
